# Optimizing a Trainium2 kernel written in Bass

```python
import jax, jax.numpy as jnp
from jax import lax
import numpy as np

D_MODEL = 1024
BATCH = 8
SEQ = 4096
DEPTH = 1

HEAD_DIM = 64
FOX_HEADS = 8
NSA_HEADS = 8
NSA_GROUPS = 2
NSA_HPG = NSA_HEADS // NSA_GROUPS
BRANCH_WIDTH = 512
N_BRANCH = 2
N_NSA_BRANCH = 3
ROPE_DIM = HEAD_DIM // 4
ROPE_THETA = 500000.0
Q_BLOCK = 128
CMP_LEN = 32
CMP_STRIDE = 16
CMP_HIDDEN = 2 * HEAD_DIM
SLC_LEN = 64
SLC_TOP = 16
WINDOW = 512
FORGET_BIAS_INIT = 4.0
FORCED_SCORE = 1e9
NEG_INF = -1e30
PEER_HEADS = 8
N_KEYS = 128
N_EXPERTS = N_KEYS * N_KEYS
PEER_QDIM = 256
PEER_TOPK = 16
PEER_CHUNK = 128
RMS_EPS = 1e-6

KV_WIDTH = NSA_GROUPS * HEAD_DIM
IN_SPLITS = (BRANCH_WIDTH, BRANCH_WIDTH, BRANCH_WIDTH, FOX_HEADS, BRANCH_WIDTH,
             KV_WIDTH, KV_WIDTH, KV_WIDTH, KV_WIDTH, KV_WIDTH, KV_WIDTH,
             NSA_HEADS * N_NSA_BRANCH, N_BRANCH * D_MODEL)
D_IN = sum(IN_SPLITS)

kernel_name = "fox_nsa_gated_hybrid_peer"


def rmsnorm(x, g):
    xf = x.astype(jnp.float32)
    y = xf * lax.rsqrt(jnp.mean(xf * xf, axis=-1, keepdims=True) + RMS_EPS)
    return (y * g.astype(jnp.float32)).astype(x.dtype)


def split_columns(t, sizes):
    offs = np.cumsum((0,) + tuple(sizes))
    return [t[..., int(a):int(b)] for a, b in zip(offs[:-1], offs[1:])]


def rope_tables(pos):
    inv = jnp.power(ROPE_THETA, -jnp.arange(0, ROPE_DIM, 2, dtype=jnp.float32) / ROPE_DIM)
    ang = pos[:, None] * inv[None, :]
    return jnp.cos(ang), jnp.sin(ang)


def apply_partial_rope(x, cos, sin):
    half = ROPE_DIM // 2
    xr = x[..., :ROPE_DIM].astype(jnp.float32)
    x1, x2 = xr[..., :half], xr[..., half:]
    c, s = cos[:, None, :], sin[:, None, :]
    rot = jnp.concatenate([x1 * c - x2 * s, x1 * s + x2 * c], axis=-1)
    return jnp.concatenate([rot.astype(x.dtype), x[..., ROPE_DIM:]], axis=-1)


def fox_attention(q, k, v, log_f):
    B_, S_, H, Dh = q.shape
    scale = Dh ** -0.5
    c = jnp.swapaxes(jnp.cumsum(log_f, axis=1), 1, 2)
    k_pos = jnp.arange(S_)

    def block(i):
        q0 = i * Q_BLOCK
        q_pos = q0 + jnp.arange(Q_BLOCK)
        qb = lax.dynamic_slice_in_dim(q, q0, Q_BLOCK, axis=1)
        cb = lax.dynamic_slice_in_dim(c, q0, Q_BLOCK, axis=2)
        logits = jnp.einsum('bqhd,bkhd->bhqk', qb, k).astype(jnp.float32) * scale
        logits = logits + (cb[..., :, None] - c[..., None, :])
        mask = k_pos[None, :] <= q_pos[:, None]
        p = jax.nn.softmax(jnp.where(mask, logits, NEG_INF), axis=-1)
        return jnp.einsum('bhqk,bkhd->bqhd', p.astype(v.dtype), v).reshape(B_, Q_BLOCK, H * Dh)

    out = lax.map(block, jnp.arange(S_ // Q_BLOCK))
    return jnp.moveaxis(out, 0, 1).reshape(B_, S_, H * Dh)


def compress_kv(k, v, cmp_pos, w1, w2):
    B_, S_ = k.shape[0], k.shape[1]
    n_cmp = (S_ - CMP_LEN) // CMP_STRIDE + 1
    starts = np.arange(n_cmp) * CMP_STRIDE
    idx = starts[:, None] + np.arange(CMP_LEN)[None, :]

    def phi(t, j):
        blocks = t[:, idx] + cmp_pos[j][None, None, :, None, :]
        flat = jnp.swapaxes(blocks, 2, 3).reshape(B_, n_cmp, NSA_GROUPS, CMP_LEN * HEAD_DIM)
        return jax.nn.gelu(flat @ w1[j], approximate=False) @ w2[j]

    return phi(k, 0), phi(v, 1), jnp.asarray(starts + CMP_LEN - 1, dtype=jnp.int32)


def nsa_attention(q, k_cmp, v_cmp, cmp_end, k_slc, v_slc, k_win, v_win, gates):
    B_, S_, H, Dh = q.shape
    G = NSA_GROUPS
    scale = Dh ** -0.5
    n_cmp = k_cmp.shape[1]
    n_slc = S_ // SLC_LEN
    top = min(SLC_TOP, n_slc)
    cs = np.arange(n_cmp)[:, None] * CMP_STRIDE
    ss = np.arange(n_slc)[None, :] * SLC_LEN
    ov = np.clip(np.minimum(cs + CMP_LEN, ss + SLC_LEN) - np.maximum(cs, ss), 0, None) / CMP_LEN
    overlap = jnp.asarray(ov, dtype=jnp.float32)
    kb = k_slc.reshape(B_, n_slc, SLC_LEN, G, Dh).transpose(0, 3, 1, 2, 4)
    vb = v_slc.reshape(B_, n_slc, SLC_LEN, G, Dh).transpose(0, 3, 1, 2, 4)
    pad = ((0, 0), (WINDOW, 0), (0, 0), (0, 0))
    kwp, vwp = jnp.pad(k_win, pad), jnp.pad(v_win, pad)
    bi = jnp.arange(B_)[:, None, None, None]
    gi = jnp.arange(G)[None, :, None, None]
    blk_ids = jnp.arange(n_slc)

    def block(i):
        q0 = i * Q_BLOCK
        q_pos = q0 + jnp.arange(Q_BLOCK)
        qb = lax.dynamic_slice_in_dim(q, q0, Q_BLOCK, axis=1).reshape(B_, Q_BLOCK, G, NSA_HPG, Dh)
        gb = lax.dynamic_slice_in_dim(gates, q0, Q_BLOCK, axis=1).reshape(B_, Q_BLOCK, G, NSA_HPG, N_NSA_BRANCH)
        lc = jnp.einsum('bqghd,bngd->bghqn', qb, k_cmp).astype(jnp.float32) * scale
        mc = cmp_end[None, :] <= q_pos[:, None]
        pc = jax.nn.softmax(jnp.where(mc, lc, NEG_INF), axis=-1) * mc
        o_cmp = jnp.einsum('bghqn,bngd->bqghd', pc.astype(v_cmp.dtype), v_cmp)
        imp = jnp.einsum('bghqn,nj->bgqj', pc, overlap)
        q_blk = q_pos // SLC_LEN
        forced = (blk_ids[None, :] == 0) | (blk_ids[None, :] == q_blk[:, None]) | (blk_ids[None, :] == q_blk[:, None] - 1)
        causal = blk_ids[None, :] <= q_blk[:, None]
        score = jnp.where(forced, FORCED_SCORE, jnp.where(causal, imp, -1.0))
        _, sel = lax.top_k(score, top)
        k_sel = kb[bi, gi, sel]
        v_sel = vb[bi, gi, sel]
        ls = jnp.einsum('bqghd,bgqkld->bghqkl', qb, k_sel).astype(jnp.float32) * scale
        tok = sel[..., None] * SLC_LEN + jnp.arange(SLC_LEN)
        ms = (tok <= q_pos[None, None, :, None, None])[:, :, None]
        ls = jnp.where(ms, ls, NEG_INF).reshape(B_, G, NSA_HPG, Q_BLOCK, top * SLC_LEN)
        ps = jax.nn.softmax(ls, axis=-1).reshape(B_, G, NSA_HPG, Q_BLOCK, top, SLC_LEN)
        o_slc = jnp.einsum('bghqkl,bgqkld->bqghd', ps.astype(v_sel.dtype), v_sel)
        kwb = lax.dynamic_slice_in_dim(kwp, q0, WINDOW + Q_BLOCK, axis=1)
        vwb = lax.dynamic_slice_in_dim(vwp, q0, WINDOW + Q_BLOCK, axis=1)
        k_pos = q0 - WINDOW + jnp.arange(WINDOW + Q_BLOCK)
        lw = jnp.einsum('bqghd,bkgd->bghqk', qb, kwb).astype(jnp.float32) * scale
        rel = q_pos[:, None] - k_pos[None, :]
        mw = (rel >= 0) & (rel < WINDOW) & (k_pos[None, :] >= 0)
        pw = jax.nn.softmax(jnp.where(mw, lw, NEG_INF), axis=-1)
        o_win = jnp.einsum('bghqk,bkgd->bqghd', pw.astype(vwb.dtype), vwb)
        o = gb[..., 0:1] * o_cmp + gb[..., 1:2] * o_slc + gb[..., 2:3] * o_win
        return o.astype(q.dtype).reshape(B_, Q_BLOCK, H * Dh)

    out = lax.map(block, jnp.arange(S_ // Q_BLOCK))
    return jnp.moveaxis(out, 0, 1).reshape(B_, S_, H * Dh)


def hybrid_mixer(h, w_in, fox_f_bias, cmp_pos, cmp_w1, cmp_w2, w_branch, w_out):
    B_, S_, _ = h.shape
    proj = h @ w_in
    (fq, fk, fv, f_logit, nq, kc, vc, ksl, vsl, kwn, vwn,
     nsa_gate_logit, merge_logit) = split_columns(proj, IN_SPLITS)
    heads = lambda t, n: t.reshape(B_, S_, n, HEAD_DIM)
    log_f = jax.nn.log_sigmoid(f_logit.astype(jnp.float32) + fox_f_bias.astype(jnp.float32))
    y_fox = fox_attention(heads(fq, FOX_HEADS), heads(fk, FOX_HEADS), heads(fv, FOX_HEADS), log_f)
    cos, sin = rope_tables(jnp.arange(S_, dtype=jnp.float32))
    q_nsa = apply_partial_rope(heads(nq, NSA_HEADS), cos, sin)
    k_slc = apply_partial_rope(heads(ksl, NSA_GROUPS), cos, sin)
    k_win = apply_partial_rope(heads(kwn, NSA_GROUPS), cos, sin)
    k_cmp, v_cmp, cmp_end = compress_kv(heads(kc, NSA_GROUPS), heads(vc, NSA_GROUPS), cmp_pos, cmp_w1, cmp_w2)
    k_cmp = apply_partial_rope(k_cmp, *rope_tables(cmp_end.astype(jnp.float32)))
    gates = jax.nn.sigmoid(nsa_gate_logit.astype(jnp.float32)).reshape(B_, S_, NSA_HEADS, N_NSA_BRANCH)
    y_nsa = nsa_attention(q_nsa, k_cmp, v_cmp, cmp_end, k_slc, heads(vsl, NSA_GROUPS),
                          k_win, heads(vwn, NSA_GROUPS), gates)
    g = jax.nn.sigmoid(merge_logit.astype(jnp.float32)).reshape(B_, S_, N_BRANCH, D_MODEL).astype(h.dtype)
    ys = jnp.stack([y_fox, y_nsa], axis=2)
    up = jnp.einsum('bsnc,ncd->bsnd', ys, w_branch)
    merged = jnp.sum(g * up, axis=2)
    return merged @ w_out


def peer_ffn(h, wq, subkeys, u, v):
    B_, S_, D = h.shape
    T = B_ * S_
    hf = h.reshape(T, D)
    q = (hf @ wq).reshape(T, PEER_HEADS, 2, PEER_QDIM // 2)
    s = jnp.einsum('thpd,hpnd->thpn', q, subkeys).astype(jnp.float32)
    s1, i1 = lax.top_k(s[:, :, 0], PEER_TOPK)
    s2, i2 = lax.top_k(s[:, :, 1], PEER_TOPK)
    cand = (s1[..., :, None] + s2[..., None, :]).reshape(T, PEER_HEADS, PEER_TOPK * PEER_TOPK)
    cand_idx = (i1[..., :, None] * N_KEYS + i2[..., None, :]).reshape(T, PEER_HEADS, PEER_TOPK * PEER_TOPK)
    top_s, pos = lax.top_k(cand, PEER_TOPK)
    idx = jnp.take_along_axis(cand_idx, pos, axis=-1)
    w = jax.nn.softmax(top_s, axis=-1)

    def chunk(args):
        x_c, idx_c, w_c = args
        act = jax.nn.gelu(jnp.einsum('cd,chkd->chk', x_c, u[idx_c]).astype(jnp.float32), approximate=False)
        coef = (w_c * act).astype(v.dtype)
        return jnp.einsum('chk,chkd->cd', coef, v[idx_c])

    nc = T // PEER_CHUNK
    out = lax.map(chunk, (hf.reshape(nc, PEER_CHUNK, D),
                          idx.reshape(nc, PEER_CHUNK, PEER_HEADS, PEER_TOPK),
                          w.reshape(nc, PEER_CHUNK, PEER_HEADS, PEER_TOPK)))
    return out.reshape(B_, S_, D)


def setup_inputs(seed: int = 0) -> dict:
    key = jax.random.key(seed)
    ks = jax.random.split(key, 16)
    f32 = jnp.float32
    nrm = lambda k, shape, scale: scale * jax.random.normal(k, shape, f32)
    L = DEPTH
    return {
        "x": jax.random.normal(ks[0], (BATCH, SEQ, D_MODEL), f32),
        "norm_mix": 1.0 + nrm(ks[1], (L, D_MODEL), 0.02),
        "w_in": nrm(ks[2], (L, D_MODEL, D_IN), D_MODEL ** -0.5),
        "fox_f_bias": FORGET_BIAS_INIT + nrm(ks[3], (L, FOX_HEADS), 0.1),
        "nsa_cmp_pos": nrm(ks[4], (L, 2, CMP_LEN, HEAD_DIM), 0.02),
        "nsa_cmp_w1": nrm(ks[5], (L, 2, CMP_LEN * HEAD_DIM, CMP_HIDDEN), (CMP_LEN * HEAD_DIM) ** -0.5),
        "nsa_cmp_w2": nrm(ks[6], (L, 2, CMP_HIDDEN, HEAD_DIM), CMP_HIDDEN ** -0.5),
        "w_branch": nrm(ks[7], (L, N_BRANCH, BRANCH_WIDTH, D_MODEL), BRANCH_WIDTH ** -0.5),
        "w_out": nrm(ks[8], (L, D_MODEL, D_MODEL), D_MODEL ** -0.5),
        "norm_ffn": 1.0 + nrm(ks[9], (L, D_MODEL), 0.02),
        "peer_wq": nrm(ks[10], (L, D_MODEL, PEER_HEADS * PEER_QDIM), D_MODEL ** -0.5),
        "peer_subkeys": nrm(ks[11], (L, PEER_HEADS, 2, N_KEYS, PEER_QDIM // 2), (PEER_QDIM // 2) ** -0.5),
        "peer_u": nrm(ks[12], (L, N_EXPERTS, D_MODEL), D_MODEL ** -0.5),
        "peer_v": nrm(ks[13], (L, N_EXPERTS, D_MODEL), (PEER_HEADS * PEER_TOPK) ** -0.5),
        "norm_final": 1.0 + nrm(ks[14], (D_MODEL,), 0.02),
    }


def reference(x, norm_mix, w_in, fox_f_bias, nsa_cmp_pos, nsa_cmp_w1, nsa_cmp_w2, w_branch, w_out,
              norm_ffn, peer_wq, peer_subkeys, peer_u, peer_v, norm_final):
    h = x
    for l in range(DEPTH):
        h = h + hybrid_mixer(rmsnorm(h, norm_mix[l]), w_in[l], fox_f_bias[l], nsa_cmp_pos[l],
                             nsa_cmp_w1[l], nsa_cmp_w2[l], w_branch[l], w_out[l])
        h = h + peer_ffn(rmsnorm(h, norm_ffn[l]), peer_wq[l], peer_subkeys[l], peer_u[l], peer_v[l])
    return rmsnorm(h, norm_final)
```

```python
import numpy as np
from contextlib import ExitStack
import concourse.bass as bass
import concourse.mybir as mybir
from concourse.bass_utils import run_bass_kernel_spmd

F32 = mybir.dt.float32
BF16 = mybir.dt.bfloat16
I32 = mybir.dt.int32
U32 = mybir.dt.uint32
U8 = mybir.dt.uint8
AF = mybir.ActivationFunctionType
ALU = mybir.AluOpType
DTSIZE = {F32: 4, BF16: 2, I32: 4, U32: 4, U8: 1}

S = 4096
D = 1024
NT = 32
DIN_A = 2848
C_FQ, C_FK, C_FV, C_FL, C_NQ = 0, 512, 1024, 1536, 1544
C_KC, C_VC, C_KSL, C_VSL, C_KWN, C_VWN, C_NG, C_MG = 2056, 2184, 2312, 2440, 2568, 2696, 2824, 2848
NEG = -30000.0


class Res:
    __slots__ = ("name", "w", "rs", "excl")

    def __init__(self, name="", excl=False):
        self.name = name
        self.excl = excl
        self.w = None
        self.rs = {}


class Op:
    __slots__ = ("eng", "fn", "deps", "needed", "num", "dma")


class Prog:
    ENG = ("pe", "act", "dve", "pool", "sp")

    def __init__(self, nc, es):
        self.nc = nc
        self.es = es
        self.ops = {e: [] for e in self.ENG}
        self.esem = {e: es.enter_context(nc.semaphore("es_" + e)) for e in self.ENG}
        self.dsem = {}
        self.last = {e: None for e in self.ENG}
        self.qhist = {}
        self.max_out = 10 ** 9

    def _dsem(self, key):
        if key not in self.dsem:
            self.dsem[key] = [self.es.enter_context(self.nc.semaphore("ds_" + key)), 0]
        return self.dsem[key]

    def _deps(self, eng, reads, writes):
        deps = []
        for r in reads:
            if r.w is not None:
                deps.append(r.w)
            if r.excl:
                deps.extend(v for kk, v in r.rs.items() if kk != eng)
        for w in writes:
            if w.w is not None:
                deps.append(w.w)
            deps.extend(w.rs.values())
        out = []
        for d in deps:
            if isinstance(d, Op):
                if d.eng == eng and eng == "pe":
                    continue
                d.needed = True
            out.append(d)
        return out

    def op(self, eng, fn, reads=(), writes=()):
        o = Op()
        o.eng, o.fn, o.needed, o.dma, o.num = eng, fn, False, None, 0
        o.deps = self._deps(eng, reads, writes)
        for r in reads:
            r.rs[eng] = o
        for w in writes:
            w.w = o
            w.rs = {}
        self.ops[eng].append(o)
        self.last[eng] = o
        return o

    def dma(self, q, key, fn, reads=(), writes=()):
        o = Op()
        o.eng, o.fn, o.needed, o.num = q, fn, False, 0
        o.deps = self._deps(q, reads, writes)
        h = self.qhist.setdefault(q, [])
        if len(h) >= self.max_out:
            o.deps.append(h[-self.max_out])
        s = self._dsem(key)
        s[1] += 16
        ev = (key, s[1])
        o.dma = key
        h.append(ev)
        for r in reads:
            r.rs[("d", key)] = ev
        for w in writes:
            w.w = ev
            w.rs = {}
        self.ops[q].append(o)
        return o

    def barrier(self):
        lasts = [self.last[e] for e in self.ENG if self.last[e] is not None]
        for o in lasts:
            o.needed = True
        dev = [(k, v[1]) for k, v in self.dsem.items() if v[1] > 0]
        for e in self.ENG:
            o = Op()
            o.eng, o.fn, o.needed, o.dma, o.num = e, None, False, None, 0
            o.deps = [l for l in lasts if not (l.eng == e == "pe")] + dev
            self.ops[e].append(o)

    def emit(self, blk):
        for e in self.ENG:
            c = 0
            for o in self.ops[e]:
                if o.dma is None and o.needed and o.fn is not None:
                    c += 1
                    o.num = c

        def run(e, engobj):
            seen = {}
            for o in self.ops[e]:
                for d in o.deps:
                    if isinstance(d, Op):
                        sem, val, k = self.esem[d.eng], d.num, ("e", d.eng)
                    else:
                        sem, val, k = self.dsem[d[0]][0], d[1], ("d", d[0])
                    if seen.get(k, 0) < val:
                        engobj.wait_ge(sem, val)
                        seen[k] = val
                if o.fn is not None:
                    ins = o.fn(engobj)
                    if o.dma is not None:
                        ins.then_inc(self.dsem[o.dma][0], 16)
                    elif o.needed:
                        ins.then_inc(self.esem[e], 1)

        blk.tensor(lambda t: run("pe", t))
        blk.scalar(lambda t: run("act", t))
        blk.vector(lambda t: run("dve", t))
        blk.gpsimd(lambda t: run("pool", t))
        blk.sync(lambda t: run("sp", t))


class Tile:
    __slots__ = ("ap", "res")

    def __init__(self, ap, res):
        self.ap = ap
        self.res = res


class Arena:
    def __init__(self, nc, es, nbytes):
        self.t = es.enter_context(nc.sbuf_tensor("arena", [128, nbytes], U8))
        self.off = 0
        self.cap = nbytes
        self.n = 0

    def alloc(self, shape, dt, name=None):
        n = int(np.prod(shape)) * DTSIZE[dt]
        n_al = (n + 63) // 64 * 64
        assert self.off + n_al <= self.cap, f"SBUF arena overflow {self.off}+{n_al}>{self.cap} ({name})"
        ap = self.t[:, self.off:self.off + n].bitcast(dt)
        self.off += n_al
        if len(shape) > 1:
            names = " ".join(f"d{i}" for i in range(len(shape)))
            kw = {f"d{i}": int(shape[i]) for i in range(len(shape))}
            ap = ap.rearrange(f"p ({names}) -> p {names}", **kw)
        self.n += 1
        return Tile(ap, Res(name or f"t{self.n}"))

    def mark(self):
        return self.off

    def release(self, m):
        self.off = m


class K:
    pass


_DBG = {}


def build_program(dbg=False, phases="ABCDE", lv=9, nt_e=NT):
    nc = bass.Bass("TRN2", target_bir_lowering=False)
    es = ExitStack()
    k = K()
    k.nc = nc
    k.lv = lv
    k.nt_e = nt_e
    import os
    k.skip = os.environ.get('KSKIP', '')

    def din(name, shape, dt=F32):
        return nc.dram_tensor(name, list(shape), dt, kind="ExternalInput").ap()

    def dscr(name, shape, dt):
        return nc.dram_tensor(name, list(shape), dt, kind=("ExternalOutput" if dbg else "Internal")).ap()

    I = K()
    I.x = din("x", [S, D])
    I.w_in = din("w_in", [D, 4896])
    I.gmix = din("gmix", [128, 8])
    I.fbias = din("fbias", [1, 8])
    I.w1 = din("w1", [128, 2 * 32 * 128])
    I.pos = din("pos", [128, 2 * 32])
    I.w2 = din("w2", [128, 2 * 64])
    I.wbr = din("wbr", [2, 512, 1024])
    I.wout = din("wout", [D, D])
    I.gffn = din("gffn", [1, D])
    I.gffn8 = din("gffn8", [128, 8])
    I.wq = din("wq", [D, 2048])
    I.skT = din("skT", [128, 16 * 128])
    I.pu = din("pu", [16384, D])
    I.pv = din("pv", [16384, D])
    I.gfin = din("gfin", [1, D])
    I.rope = din("c_rope", [128, 4 * S])
    I.ropec = din("c_ropec", [128, 2 * 256])
    I.trib = din("c_trib", [128, 2 * 128], BF16)
    I.trif = din("c_trif", [128, 2 * 128])
    I.cmpmask = din("c_cmpmask", [128, 2 * S], BF16)
    I.etab = din("c_etab", [128, S], BF16)
    I.selbias = din("c_selbias", [128, NT * 64])
    I.ovaug = din("c_ovaug", [128, 2 * 65], BF16)
    I.iota16 = din("c_iota16", [128, 16])
    out = nc.dram_tensor("out", [S, D], F32, kind="ExternalOutput").ap()

    Sc = K()
    Sc.XT = dscr("s_xt", [8, 128, S], BF16)
    Sc.QF = dscr("s_qf", [8, 128, S], BF16)
    Sc.KF = dscr("s_kf", [8, 128, S], BF16)
    Sc.VF = dscr("s_vf", [NT, 128, 8 * 65], BF16)
    Sc.QN = dscr("s_qn", [8, 128, S], BF16)
    Sc.KC = dscr("s_kc", [2, 128, S], BF16)
    Sc.VC = dscr("s_vc", [2, 128, S], BF16)
    Sc.KSL = dscr("s_ksl", [2, 128, S], BF16)
    Sc.KWN = dscr("s_kwn", [2, 128, S], BF16)
    Sc.VSW = dscr("s_vsw", [NT, 128, 4 * 65], BF16)
    Sc.YF = dscr("s_yf", [NT, 128, 512], BF16)
    Sc.YN = dscr("s_yn", [NT, 128, 512], BF16)
    Sc.H2 = dscr("s_h2", [S, D], F32)
    Sc.UVB = nc.dram_tensor("s_uvb", [16384, 2 * D], BF16, kind="Internal").ap()
    if dbg:
        Sc.dbg1 = dscr("s_dbg1", [128, NT * 8], F32)
        Sc.dbg2 = dscr("s_dbg2", [128, NT * 24], F32)
    k.I, k.Sc, k.out = I, Sc, out
    k.dbgC = (dscr("s_dbgc0", [128, 512], BF16), dscr("s_dbgc1", [128, 2 * 2 * 129], BF16)) if dbg else None

    P = Prog(nc, es)
    A = Arena(nc, es, 204800)
    k.P, k.A = P, A
    PS = []
    for i in range(8):
        t = es.enter_context(nc.psum_tensor(f"psb{i}", [128, 512], F32))
        PS.append(Tile(t[:, :], Res(f"ps{i}", excl=True)))
    k.PS = PS

    k.identb = A.alloc([128], BF16, "identb")
    k.identf = A.alloc([128], F32, "identf")
    k.trib = A.alloc([2, 128], BF16, "trib")
    k.trif = A.alloc([2, 128], F32, "trif")
    k.logf = A.alloc([NT, 8], F32, "logf")
    k.gate = A.alloc([NT, 24], F32, "gate")
    k.fb = A.alloc([8], F32, "fb")

    def setup_consts():
        P.op("pool", lambda g: g.memset(k.identf.ap, 0.0), writes=[k.identf.res])
        P.op("pool", lambda g: g.affine_select(out=k.identf.ap, in_=k.identf.ap, pattern=[[-1, 128]],
                                               compare_op=ALU.not_equal, fill=1.0, base=0, channel_multiplier=1),
             reads=[k.identf.res], writes=[k.identf.res])
        P.op("dve", lambda v: v.tensor_copy(out=k.identb.ap, in_=k.identf.ap), reads=[k.identf.res], writes=[k.identb.res])
        P.dma("sp", "const", lambda e: e.dma_start(out=k.trib.ap, in_=I.trib.rearrange("p (a b) -> p a b", a=2)), writes=[k.trib.res])
        P.dma("sp", "const", lambda e: e.dma_start(out=k.trif.ap, in_=I.trif.rearrange("p (a b) -> p a b", a=2)), writes=[k.trif.res])
        P.dma("sp", "const", lambda e: e.dma_start(out=k.fb.ap, in_=I.fbias[0:1, :].partition_broadcast(128)), writes=[k.fb.res])

    setup_consts()
    if "E" in phases:
        phase_0(k)
        P.barrier()
    if "A" in phases:
        phase_a(k)
    P.barrier()
    if "B" in phases:
        phase_b(k)
        P.barrier()
    if "C" in phases:
        phase_c(k)
        P.barrier()
    if "D" in phases:
        phase_d(k)
        P.barrier()
    if "E" in phases:
        phase_e(k)
        P.barrier()
    if dbg and "A" in phases:
        P.dma("sp", "dbg", lambda e: e.dma_start(out=Sc.dbg1, in_=k.logf.ap.rearrange("p a b -> p (a b)")), reads=[k.logf.res])
        P.dma("sp", "dbg", lambda e: e.dma_start(out=Sc.dbg2, in_=k.gate.ap.rearrange("p a b -> p (a b)")), reads=[k.gate.res])
    P.barrier()
    blk = es.enter_context(nc.Block())
    P.emit(blk)
    es.close()
    return nc


def phase_0(k):
    P, A, I, Sc = k.P, k.A, k.I, k.Sc
    m0 = A.mark()
    RB = 4
    st = [A.alloc([RB, D], F32, f"cv_s{i}") for i in range(3)]
    sb = [A.alloc([RB, D], BF16, f"cv_b{i}") for i in range(3)]
    n = 0
    for (src, c0) in ((I.pu, 0), (I.pv, D)):
        sv = src.rearrange("(p r) d -> p r d", p=128)
        dv = Sc.UVB.rearrange("(p r) d -> p r d", p=128)[:, :, c0:c0 + D]
        for r0 in range(0, 128, RB):
            a, b = st[n % 3], sb[n % 3]
            P.dma("sp", a.res.name, lambda e, a=a, sv=sv, r0=r0: e.dma_start(out=a.ap, in_=sv[:, r0:r0 + RB, :]), writes=[a.res])
            eng = ("act", "dve", "pool")[n % 3]
            if eng == "act":
                P.op("act", lambda e, a=a, b=b: e.copy(out=b.ap, in_=a.ap), reads=[a.res], writes=[b.res])
            else:
                P.op(eng, lambda e, a=a, b=b: e.tensor_copy(out=b.ap, in_=a.ap), reads=[a.res], writes=[b.res])
            P.dma("sp", b.res.name, lambda e, b=b, dv=dv, r0=r0: e.dma_start(out=dv[:, r0:r0 + RB, :], in_=b.ap), reads=[b.res])
            n += 1
    P.barrier()
    A.release(m0)


FM_UNITS = ([("QF", h, C_FQ + 64 * h, None, 0.125) for h in range(8)]
            + [("KF", h, C_FK + 64 * h, None, 1.0) for h in range(8)]
            + [("QN", h, C_NQ + 64 * h, h, 0.125) for h in range(8)]
            + [("KC", g, C_KC + 64 * g, None, 1.0) for g in range(2)]
            + [("VC", g, C_VC + 64 * g, None, 1.0) for g in range(2)]
            + [("KSL", g, C_KSL + 64 * g, 8 + g, 1.0) for g in range(2)]
            + [("KWN", g, C_KWN + 64 * g, 10 + g, 1.0) for g in range(2)])


def phase_a(k):
    P, A, I, Sc, PS = k.P, k.A, k.I, k.Sc, k.PS
    m0 = A.mark()
    W1 = A.alloc([8, DIN_A], BF16, "W1")
    W1r = [Res(f"W1_{c}") for c in range(8)]
    Wr = A.alloc([8, 12, 32], BF16, "Wr")
    Wr0 = Res("Wr0")
    P.op("pool", lambda g: g.memset(Wr.ap, 0.0), writes=[Wr0])
    Wrr = [Res(f"Wr_{c}") for c in range(8)]
    gmix = A.alloc([8], F32, "gmix")
    wst = [A.alloc([DIN_A], F32, f"wst{i}") for i in range(2)]
    P.dma("sp", "const", lambda e: e.dma_start(out=gmix.ap, in_=I.gmix), writes=[gmix.res])
    for c in range(8):
        ws = wst[c % 2]
        P.dma("sp", ws.res.name, lambda e, c=c, ws=ws: e.dma_start(out=ws.ap, in_=I.w_in[c * 128:(c + 1) * 128, 0:DIN_A]), writes=[ws.res])
        P.op("dve", lambda v, c=c, ws=ws: v.tensor_scalar(out=W1.ap[:, c, :], in0=ws.ap, scalar1=gmix.ap[:, c:c + 1], scalar2=None, op0=ALU.mult),
             reads=[ws.res, gmix.res], writes=[W1r[c]])
        for (base, n, u0) in ((C_NQ, 8, 0), (C_KSL, 2, 8), (C_KWN, 2, 10)):
            src = W1.ap[:, c, base:base + n * 64].rearrange("p (u d) -> p u d", d=64)
            P.op("pool", lambda g, src=src, c=c, u0=u0, n=n: g.tensor_scalar(out=Wr.ap[:, c, u0:u0 + n, 0:8], in0=src[:, :, 8:16], scalar1=-1.0, scalar2=None, op0=ALU.mult),
                 reads=[W1r[c], Wr0], writes=[Wrr[c]])
            P.op("pool", lambda g, src=src, c=c, u0=u0, n=n: g.tensor_copy(out=Wr.ap[:, c, u0:u0 + n, 8:16], in_=src[:, :, 0:8]),
                 reads=[W1r[c]], writes=[Wrr[c]])

    xs = [A.alloc([D], F32, f"xs{i}") for i in range(2)]
    junk = A.alloc([D], BF16, "junk")
    xn = [A.alloc([D], BF16, f"xn{i}") for i in range(2)]
    ss = A.alloc([NT], F32, "ss")
    ssr = [Res(f"ss{t}") for t in range(NT)]
    XTc = [A.alloc([8, 512], BF16, f"XTc{i}") for i in range(2)]
    XTr = [[Res(f"XT{i}_{j}") for j in range(4)] for i in range(2)]
    VFt = [A.alloc([8, 65], BF16, f"VFt{i}") for i in range(2)]
    VSWt = [A.alloc([4, 65], BF16, f"VSWt{i}") for i in range(2)]
    zraw = A.alloc([NT, 8], F32, "zraw")
    graw = A.alloc([NT, 24], F32, "graw")
    rope = [A.alloc([4, 512], F32, f"rope{i}") for i in range(2)]
    etc = [A.alloc([512], BF16, f"etc{i}") for i in range(2)]
    fm = [A.alloc([512], BF16, f"fm{i}") for i in range(4)]
    rt1 = [A.alloc([512], F32, f"rt1_{i}") for i in range(2)]
    rt2 = [A.alloc([512], F32, f"rt2_{i}") for i in range(2)]
    for i in range(2):
        P.op("pool", lambda g, i=i: g.memset(VFt[i].ap, 1.0), writes=[VFt[i].res])
        P.op("pool", lambda g, i=i: g.memset(VSWt[i].ap, 1.0), writes=[VSWt[i].res])
    PT = [PS[0], PS[1]]
    PSV, PSS = PS[2], PS[3]
    PSU = [PS[4], PS[5]]
    PSR = [PS[6], PS[7]]
    ucount = 0
    rcount = 0
    for t in range(NT):
        q, j = divmod(t, 4)
        x_t = xs[t % 2]
        xn_t = xn[t % 2]
        P.dma("sp", x_t.res.name, lambda e, t=t, x_t=x_t: e.dma_start(out=x_t.ap, in_=I.x[t * 128:(t + 1) * 128, :]), writes=[x_t.res])
        P.op("act", lambda a, t=t, x_t=x_t: a.activation(out=junk.ap, in_=x_t.ap, func=AF.Square, accum_out=ss.ap[:, t:t + 1]),
             reads=[x_t.res], writes=[junk.res, ssr[t]])
        P.op("act", lambda a, t=t: a.activation(out=ss.ap[:, t:t + 1], in_=ss.ap[:, t:t + 1], func=AF.Sqrt, scale=1.0 / D, bias=1e-6),
             reads=[ssr[t]], writes=[ssr[t]])
        P.op("dve", lambda v, t=t: v.reciprocal(out=ss.ap[:, t:t + 1], in_=ss.ap[:, t:t + 1]), reads=[ssr[t]], writes=[ssr[t]])
        P.op("act", lambda a, t=t, x_t=x_t, xn_t=xn_t: a.activation(out=xn_t.ap, in_=x_t.ap, func=AF.Copy, scale=ss.ap[:, t:t + 1]),
             reads=[x_t.res, ssr[t]], writes=[xn_t.res])
        pt = PT[t % 2]
        ptb = pt.ap.bitcast(BF16)
        for c in range(8):
            P.op("pe", lambda e, c=c, ptb=ptb, xn_t=xn_t: e.transpose(out=ptb[:, c * 128:(c + 1) * 128], in_=xn_t.ap[:, c * 128:(c + 1) * 128], identity=k.identb.ap),
                 reads=[xn_t.res, k.identb.res], writes=[pt.res])
        xc = XTc[q % 2]
        P.op("dve", lambda v, ptb=ptb, xc=xc, j=j: v.tensor_copy(out=xc.ap[:, :, j * 128:(j + 1) * 128], in_=ptb.rearrange("p (c t) -> p c t", c=8)),
             reads=[pt.res], writes=[XTr[q % 2][j]])
        xr = XTr[q % 2][j]
        for c in range(8):
            P.op("pe", lambda e, c=c, xc=xc, j=j: e.matmul(PSV.ap[:, 0:512], lhsT=xc.ap[:, c, j * 128:(j + 1) * 128], rhs=W1.ap[:, c, C_FV:C_FV + 512], start=(c == 0), stop=(c == 7)),
                 reads=[xr, W1r[c]], writes=[PSV.res])
        for (cb, n, o0) in ((C_FL, 8, 0), (C_NG, 24, 8), (C_VSL, 128, 32), (C_VWN, 128, 160)):
            for c in range(8):
                P.op("pe", lambda e, c=c, xc=xc, j=j, cb=cb, n=n, o0=o0: e.matmul(PSS.ap[:, o0:o0 + n], lhsT=xc.ap[:, c, j * 128:(j + 1) * 128], rhs=W1.ap[:, c, cb:cb + n], start=(c == 0), stop=(c == 7)),
                     reads=[xr, W1r[c]], writes=[PSS.res])
        vf = VFt[t % 2]
        vsw = VSWt[t % 2]
        P.op("act", lambda a, vf=vf: a.copy(out=vf.ap[:, :, 0:64], in_=PSV.ap[:, 0:512].rearrange("p (h d) -> p h d", h=8)), reads=[PSV.res], writes=[vf.res])
        P.op("dve", lambda v, vsw=vsw: v.tensor_copy(out=vsw.ap[:, :, 0:64], in_=PSS.ap[:, 32:288].rearrange("p (h d) -> p h d", h=4)), reads=[PSS.res], writes=[vsw.res])
        P.op("dve", lambda v, t=t: v.tensor_copy(out=zraw.ap[:, t, :], in_=PSS.ap[:, 0:8]), reads=[PSS.res], writes=[zraw.res])
        P.op("dve", lambda v, t=t: v.tensor_copy(out=graw.ap[:, t, :], in_=PSS.ap[:, 8:32]), reads=[PSS.res], writes=[graw.res])
        P.dma("sp", vf.res.name, lambda e, t=t, vf=vf: e.dma_start(out=Sc.VF[t], in_=vf.ap.rearrange("p h d -> p (h d)")), reads=[vf.res])
        P.dma("sp", vsw.res.name, lambda e, t=t, vsw=vsw: e.dma_start(out=Sc.VSW[t], in_=vsw.ap.rearrange("p h d -> p (h d)")), reads=[vsw.res])
        if j != 3:
            continue
        rp = rope[q % 2]
        et = etc[q % 2]
        P.dma("sp", rp.res.name, lambda e, q=q, rp=rp: e.dma_start(out=rp.ap, in_=I.rope.rearrange("p (a t) -> p a t", a=4)[:, :, q * 512:(q + 1) * 512]), writes=[rp.res])
        P.dma("sp", et.res.name, lambda e, q=q, et=et: e.dma_start(out=et.ap, in_=I.etab[:, q * 512:(q + 1) * 512]), writes=[et.res])
        if 'X' not in k.skip:
          P.dma("sp", xc.res.name, lambda e, q=q, xc=xc: e.dma_start(out=Sc.XT[:, :, q * 512:(q + 1) * 512].rearrange("c p t -> p c t"), in_=xc.ap), reads=XTr[q % 2])
        for (nm, idx, cb, ru, scale) in FM_UNITS:
            pu = PSU[ucount % 2]
            f = fm[ucount % 4]
            ucount += 1
            for c in range(8):
                P.op("pe", lambda e, c=c, pu=pu, cb=cb, xc=xc: e.matmul(pu.ap, lhsT=W1.ap[:, c, cb:cb + 128], rhs=xc.ap[:, c, :], start=(c == 0), stop=(c == 7)),
                     reads=XTr[q % 2] + [W1r[c]], writes=[pu.res])
            P.op("act", lambda a, f=f, pu=pu, scale=scale: a.activation(out=f.ap, in_=pu.ap, func=AF.Copy, scale=scale), reads=[pu.res], writes=[f.res])
            if nm == "KSL" and 'E' not in k.skip:
                P.op("pool", lambda g, f=f, et=et: g.tensor_copy(out=f.ap[64:128], in_=et.ap[64:128]), reads=[et.res], writes=[f.res])
            if ru is not None and 'R' not in k.skip:
                pr = PSR[rcount % 2]
                a1, a2 = rt1[rcount % 2], rt2[rcount % 2]
                rcount += 1
                for c in range(8):
                    P.op("pe", lambda e, c=c, pr=pr, ru=ru, xc=xc: e.matmul(pr.ap[0:32, :], lhsT=Wr.ap[:, c, ru, :], rhs=xc.ap[:, c, :], start=(c == 0), stop=(c == 7)),
                         reads=XTr[q % 2] + [Wrr[c]], writes=[pr.res])
                ti = 0 if scale != 1.0 else 2
                P.op("dve", lambda v, a1=a1, pu=pu, rp=rp, ti=ti: v.tensor_tensor(out=a1.ap[0:16], in0=pu.ap[0:16, :], in1=rp.ap[0:16, ti, :], op=ALU.mult), reads=[pu.res, rp.res], writes=[a1.res])
                P.op("dve", lambda v, a2=a2, pr=pr, rp=rp, ti=ti: v.tensor_tensor(out=a2.ap[0:16], in0=pr.ap[0:16, :], in1=rp.ap[0:16, ti + 1, :], op=ALU.mult), reads=[pr.res, rp.res], writes=[a2.res])
                P.op("dve", lambda g, f=f, a1=a1, a2=a2: g.tensor_tensor(out=f.ap[0:16], in0=a1.ap[0:16], in1=a2.ap[0:16], op=ALU.add), reads=[a1.res, a2.res], writes=[f.res])
            dst = getattr(Sc, nm)
            if 'U' not in k.skip:
              P.dma("sp", f.res.name, lambda e, f=f, dst=dst, idx=idx, q=q: e.dma_start(out=dst[idx, :, q * 512:(q + 1) * 512], in_=f.ap), reads=[f.res])
    P.op("dve", lambda v: v.tensor_tensor(out=zraw.ap, in0=zraw.ap, in1=k.fb.ap.unsqueeze(1).to_broadcast([128, NT, 8]), op=ALU.add), reads=[zraw.res, k.fb.res], writes=[zraw.res])
    P.op("act", lambda a: a.activation(out=zraw.ap, in_=zraw.ap, func=AF.Exp, scale=-1.0), reads=[zraw.res], writes=[zraw.res])
    P.op("act", lambda a: a.activation(out=zraw.ap, in_=zraw.ap, func=AF.Ln, bias=1.0, scale=1.0), reads=[zraw.res], writes=[zraw.res])
    P.op("dve", lambda v: v.tensor_scalar(out=k.logf.ap, in0=zraw.ap, scalar1=-1.0, scalar2=None, op0=ALU.mult), reads=[zraw.res], writes=[k.logf.res])
    P.op("act", lambda a: a.activation(out=k.gate.ap, in_=graw.ap, func=AF.Sigmoid), reads=[graw.res], writes=[k.gate.res])
    P.barrier()
    A.release(m0)


def phase_b(k):
    P, A, I, Sc, PS = k.P, k.A, k.I, k.Sc, k.PS
    m0 = A.mark()
    logf2 = k.logf.ap.rearrange("p a b -> p (a b)")
    tot = A.alloc([NT, 8], F32, "tot")
    pref = A.alloc([NT, 8], F32, "pref")
    negc = A.alloc([NT, 8], F32, "negc")
    bias = A.alloc([8, 8, NT], F32, "bias")
    P.op("pe", lambda e: e.matmul(PS[0].ap[:, 0:256], lhsT=k.trif.ap[:, 1, :], rhs=logf2, start=True, stop=True), reads=[k.trif.res, k.logf.res], writes=[PS[0].res])
    P.op("pe", lambda e: e.matmul(PS[1].ap[:, 0:256], lhsT=k.trif.ap[:, 0, :], rhs=logf2, start=True, stop=True), reads=[k.trif.res, k.logf.res], writes=[PS[1].res])
    P.op("dve", lambda v: v.tensor_copy(out=tot.ap.rearrange("p a b -> p (a b)"), in_=PS[0].ap[:, 0:256]), reads=[PS[0].res], writes=[tot.res])
    P.op("dve", lambda v: v.memset(pref.ap[:, 0, :], 0.0), writes=[pref.res])
    for j in range(1, NT):
        P.op("dve", lambda v, j=j: v.tensor_tensor(out=pref.ap[:, j, :], in0=pref.ap[:, j - 1, :], in1=tot.ap[:, j - 1, :], op=ALU.add), reads=[pref.res, tot.res], writes=[pref.res])
    P.op("dve", lambda v: v.scalar_tensor_tensor(out=negc.ap.rearrange("p a b -> p (a b)"), in0=PS[1].ap[:, 0:256], scalar=-1.0, in1=pref.ap.rearrange("p a b -> p (a b)"), op0=ALU.mult, op1=ALU.subtract),
         reads=[PS[1].res, pref.res], writes=[negc.res])
    for h in range(8):
        for q in range(8):
            P.op("dve", lambda v, h=h, q=q: v.tensor_scalar(out=bias.ap[:, h, q, :], in0=negc.ap[:, :, h], scalar1=pref.ap[:, 4 * q, h:h + 1], scalar2=None, op0=ALU.add),
                 reads=[negc.res, pref.res], writes=[bias.res])
    VF = A.alloc([NT, 520], BF16, "VFall")
    for i in range(4):
        P.dma("sp", "VFall", lambda e, i=i: e.dma_start(out=VF.ap[:, i * 8:(i + 1) * 8, :], in_=Sc.VF[i * 8:(i + 1) * 8].rearrange("t p f -> p t f")), writes=[VF.res])
    QK = [(A.alloc([S], BF16, f"QFh{i}"), A.alloc([S], BF16, f"KFh{i}")) for i in range(2)]
    yf = A.alloc([NT, 512], BF16, "yfox")
    yfr = [Res(f"yf{t}") for t in range(NT)]
    PT = [A.alloc([512], BF16, f"PT{i}") for i in range(4)]
    rz = [A.alloc([4], F32, f"rz{i}") for i in range(2)]
    cnt = 0
    oc = 0
    for h in range(8):
        Qh, Kh = QK[h % 2]
        P.dma("sp", Qh.res.name, lambda e, h=h, Qh=Qh: e.dma_start(out=Qh.ap, in_=Sc.QF[h]), writes=[Qh.res])
        P.dma("sp", Kh.res.name, lambda e, h=h, Kh=Kh: e.dma_start(out=Kh.ap, in_=Sc.KF[h]), writes=[Kh.res])
        for q in range(8):
            OUT = PS[6 + oc % 2]
            rzt = rz[oc % 2]
            oc += 1
            P.op("dve", lambda v, OUT=OUT: v.memset(OUT.ap[:, 0:260], 0.0), writes=[OUT.res])
            def qk_b(kt, ST, Kh=Kh, Qh=Qh, q=q):
                c0 = max(kt - 4 * q, 0) * 128
                P.op("pe", lambda e: e.matmul(ST.ap[:, c0:512], lhsT=Kh.ap[0:64, kt * 128:(kt + 1) * 128], rhs=Qh.ap[0:64, q * 512 + c0:(q + 1) * 512], start=True, stop=True),
                     reads=[Kh.res, Qh.res], writes=[ST.res])

            def rest_b(kt, ST, pt, OUT=OUT, h=h, q=q):
                j = kt - 4 * q
                c0 = max(j, 0) * 128
                P.op("act", lambda a: a.activation(out=pt.ap[:, c0:512], in_=ST.ap[:, c0:512], func=AF.Exp, bias=bias.ap[:, h, q, kt:kt + 1], scale=1.0),
                     reads=[ST.res, bias.res], writes=[pt.res])
                if j >= 0:
                    P.op("pool", lambda g: g.tensor_tensor(out=pt.ap[:, c0:c0 + 128], in0=pt.ap[:, c0:c0 + 128], in1=k.trib.ap[:, 0, :], op=ALU.mult),
                         reads=[pt.res, k.trib.res], writes=[pt.res])
                for ql in range(max(j, 0), 4):
                    P.op("pe", lambda e, ql=ql: e.matmul(OUT.ap[:, ql * 65:(ql + 1) * 65], lhsT=pt.ap[:, ql * 128:(ql + 1) * 128], rhs=VF.ap[:, kt, h * 65:(h + 1) * 65], start=False, stop=False, skip_group_check=True),
                         reads=[pt.res, VF.res], writes=[OUT.res])

            nk = 4 * q + 4
            slots = [(PS[(cnt + i) % 4], PT[(cnt + i) % 4]) for i in range(nk)]
            cnt += nk
            qk_b(0, slots[0][0])
            for kt in range(nk):
                if kt + 1 < nk:
                    qk_b(kt + 1, slots[kt + 1][0])
                rest_b(kt, slots[kt][0], slots[kt][1])
            P.op("dve", lambda v, OUT=OUT, rzt=rzt: v.reciprocal(out=rzt.ap, in_=OUT.ap[:, 0:260].rearrange("p (a b) -> p a b", b=65)[:, :, 64]), reads=[OUT.res], writes=[rzt.res])
            for ql in range(4):
                t = 4 * q + ql
                P.op("dve", lambda v, OUT=OUT, rzt=rzt, ql=ql, t=t, h=h: v.tensor_scalar(out=yf.ap[:, t, h * 64:(h + 1) * 64], in0=OUT.ap[:, ql * 65:ql * 65 + 64], scalar1=rzt.ap[:, ql:ql + 1], scalar2=None, op0=ALU.mult),
                     reads=[OUT.res, rzt.res], writes=[yfr[t]])
    for i in range(4):
        P.dma("sp", "yfox", lambda e, i=i: e.dma_start(out=Sc.YF[i * 8:(i + 1) * 8].rearrange("t p f -> p t f"), in_=yf.ap[:, i * 8:(i + 1) * 8, :]), reads=yfr[i * 8:(i + 1) * 8])
    P.barrier()
    A.release(m0)


def phase_c(k):
    P, A, I, Sc, PS = k.P, k.A, k.I, k.Sc, k.PS
    m0 = A.mark()
    kcmpT = [A.alloc([256], BF16, f"kcmpT{g}") for g in range(2)]
    VCa = A.alloc([2, 2, 129], BF16, "VCa")
    cmpmask = A.alloc([2, S], BF16, "cmpmask")
    selbias = A.alloc([NT, 64], F32, "selbias")
    ovaug = A.alloc([2, 65], BF16, "ovaug")
    VSW = A.alloc([NT, 260], BF16, "VSWall")
    P.dma("sp", "cC", lambda e: e.dma_start(out=cmpmask.ap, in_=I.cmpmask.rearrange("p (a t) -> p a t", a=2)), writes=[cmpmask.res])
    P.dma("sp", "cC", lambda e: e.dma_start(out=selbias.ap, in_=I.selbias.rearrange("p (a t) -> p a t", a=NT)), writes=[selbias.res])
    P.dma("sp", "cC", lambda e: e.dma_start(out=ovaug.ap, in_=I.ovaug.rearrange("p (a t) -> p a t", a=2)), writes=[ovaug.res])
    for i in range(4):
        P.dma("sp", "cC", lambda e, i=i: e.dma_start(out=VSW.ap[:, i * 8:(i + 1) * 8, :], in_=Sc.VSW[i * 8:(i + 1) * 8].rearrange("t p f -> p t f")), writes=[VSW.res])
    m1 = A.mark()
    w1s = A.alloc([2, 32, 128], F32, "w1s")
    w1b = A.alloc([2, 32, 128], BF16, "w1b")
    poss = A.alloc([2, 32], F32, "poss")
    posb = A.alloc([2, 32], BF16, "posb")
    w2s = A.alloc([2, 64], F32, "w2s")
    w2b = A.alloc([2, 64], BF16, "w2b")
    w2r = A.alloc([16], BF16, "w2r")
    ropec = A.alloc([2, 256], F32, "ropec")
    cst = A.alloc([2], F32, "cst")
    SRC = [[A.alloc([S], BF16, f"src{j}{g}") for g in range(2)] for j in range(2)]
    hidT = [A.alloc([256], BF16, f"hidT{i}") for i in range(2)]
    ct1 = A.alloc([256], F32, "ct1")
    ct2 = A.alloc([256], F32, "ct2")
    P.dma("sp", "cC", lambda e: e.dma_start(out=w1s.ap, in_=I.w1.rearrange("p (a b c) -> p a b c", a=2, b=32)), writes=[w1s.res])
    P.dma("sp", "cC", lambda e: e.dma_start(out=poss.ap, in_=I.pos.rearrange("p (a b) -> p a b", a=2)), writes=[poss.res])
    P.dma("sp", "cC", lambda e: e.dma_start(out=w2s.ap, in_=I.w2.rearrange("p (a b) -> p a b", a=2)), writes=[w2s.res])
    P.dma("sp", "cC", lambda e: e.dma_start(out=ropec.ap, in_=I.ropec.rearrange("p (a b) -> p a b", a=2)), writes=[ropec.res])
    for j in range(2):
        for g in range(2):
            src = (Sc.KC, Sc.VC)[j]
            P.dma("sp", "cC", lambda e, j=j, g=g, src=src: e.dma_start(out=SRC[j][g].ap, in_=src[g]), writes=[SRC[j][g].res])
    P.barrier()
    P.op("dve", lambda v: v.tensor_copy(out=w1b.ap, in_=w1s.ap), reads=[w1s.res], writes=[w1b.res])
    P.op("dve", lambda v: v.tensor_copy(out=posb.ap, in_=poss.ap), reads=[poss.res], writes=[posb.res])
    P.op("dve", lambda v: v.tensor_copy(out=w2b.ap, in_=w2s.ap), reads=[w2s.res], writes=[w2b.res])
    P.op("dve", lambda v: v.tensor_scalar(out=w2r.ap[:, 0:8], in0=w2s.ap[:, 0, 8:16], scalar1=-1.0, scalar2=None, op0=ALU.mult), reads=[w2s.res], writes=[w2r.res])
    P.op("dve", lambda v: v.tensor_copy(out=w2r.ap[:, 8:16], in_=w2s.ap[:, 0, 0:8]), reads=[w2s.res], writes=[w2r.res])
    P.op("pool", lambda g_: g_.memset(VCa.ap, 0.0), writes=[VCa.res])
    for g in range(2):
        P.op("pool", lambda g_, g=g: g_.memset(kcmpT[g].ap, 0.0), writes=[kcmpT[g].res])
    for i in range(2):
        P.op("pool", lambda g_, i=i: g_.memset(hidT[i].ap, 0.0), writes=[hidT[i].res])
    for j in range(2):
        for l in range(32):
            P.op("pe", lambda e, j=j, l=l: e.matmul(PS[7].ap[:, j:j + 1], lhsT=w1b.ap[0:64, j, l, :], rhs=posb.ap[0:64, j, l:l + 1], start=(l == 0), stop=(l == 31)),
                 reads=[w1b.res, posb.res], writes=[PS[7].res])
    P.op("dve", lambda v: v.tensor_copy(out=cst.ap, in_=PS[7].ap[:, 0:2]), reads=[PS[7].res], writes=[cst.res])
    cc = 0
    for j in range(2):
        for g in range(2):
            hp = PS[cc % 2]
            hT = hidT[cc % 2]
            cc += 1
            sv = SRC[j][g].ap[0:64, :].rearrange("p (n s) -> p n s", s=16)
            for l in range(32):
                rhs = sv[:, 0:255, l] if l < 16 else sv[:, 1:256, l - 16]
                P.op("pe", lambda e, hp=hp, j=j, l=l, rhs=rhs: e.matmul(hp.ap[:, 0:255], lhsT=w1b.ap[0:64, j, l, :], rhs=rhs, start=(l == 0), stop=(l == 31)),
                     reads=[w1b.res, SRC[j][g].res], writes=[hp.res])
            P.op("act", lambda a, hp=hp, hT=hT, j=j: a.activation(out=hT.ap[:, 0:255], in_=hp.ap[:, 0:255], func=AF.Gelu, bias=cst.ap[:, j:j + 1], scale=1.0),
                 reads=[hp.res, cst.res], writes=[hT.res])
            if j == 0:
                P.op("pe", lambda e, hT=hT: e.matmul(PS[2].ap[0:64, 0:255], lhsT=w2b.ap[:, 0, :], rhs=hT.ap[:, 0:255], start=True, stop=True), reads=[w2b.res, hT.res], writes=[PS[2].res])
                P.op("pe", lambda e, hT=hT: e.matmul(PS[3].ap[0:16, 0:255], lhsT=w2r.ap, rhs=hT.ap[:, 0:255], start=True, stop=True), reads=[w2r.res, hT.res], writes=[PS[3].res])
                P.op("act", lambda a, g=g: a.copy(out=kcmpT[g].ap[0:64, 0:255], in_=PS[2].ap[0:64, 0:255]), reads=[PS[2].res], writes=[kcmpT[g].res])
                P.op("dve", lambda v: v.tensor_tensor(out=ct1.ap[0:16, 0:255], in0=PS[2].ap[0:16, 0:255], in1=ropec.ap[0:16, 0, 0:255], op=ALU.mult), reads=[PS[2].res, ropec.res], writes=[ct1.res])
                P.op("dve", lambda v: v.tensor_tensor(out=ct2.ap[0:16, 0:255], in0=PS[3].ap[0:16, 0:255], in1=ropec.ap[0:16, 1, 0:255], op=ALU.mult), reads=[PS[3].res, ropec.res], writes=[ct2.res])
                P.op("dve", lambda v, g=g: v.tensor_tensor(out=kcmpT[g].ap[0:16, 0:255], in0=ct1.ap[0:16, 0:255], in1=ct2.ap[0:16, 0:255], op=ALU.add), reads=[ct1.res, ct2.res], writes=[kcmpT[g].res])
            else:
                for i in range(2):
                    nn = 128 if i == 0 else 127
                    P.op("pe", lambda e, hT=hT, i=i, nn=nn: e.matmul(PS[2 + i].ap[0:nn, 0:64], lhsT=hT.ap[:, i * 128:i * 128 + nn], rhs=w2b.ap[:, 1, :], start=True, stop=True),
                         reads=[w2b.res, hT.res], writes=[PS[2 + i].res])
                    P.op("act", lambda a, g=g, i=i, nn=nn: a.copy(out=VCa.ap[0:nn, i, g, 0:64], in_=PS[2 + i].ap[0:nn, 0:64]), reads=[PS[2 + i].res], writes=[VCa.res])
                    P.op("dve", lambda v, g=g, i=i: v.tensor_copy(out=VCa.ap[:, i, g, 64:129], in_=ovaug.ap[:, i, :]), reads=[ovaug.res], writes=[VCa.res])
    P.barrier()
    A.release(m1)
    if k.dbgC is not None:
        for g in range(2):
            P.dma("sp", "dbg", lambda e, g=g: e.dma_start(out=k.dbgC[0][:, g * 256:(g + 1) * 256], in_=kcmpT[g].ap), reads=[kcmpT[g].res])
        P.dma("sp", "dbg", lambda e: e.dma_start(out=k.dbgC[1], in_=VCa.ap.rearrange("p a b c -> p (a b c)")), reads=[VCa.res])
    KE = A.alloc([S], BF16, "KE")
    KW = A.alloc([S], BF16, "KW")
    QN = [[A.alloc([512], BF16, f"QN{i}_{hl}") for hl in range(4)] for i in range(2)]
    _DBG['QN'] = QN
    imp = A.alloc([4, 64], F32, "imp")
    yacc = [A.alloc([4, 256], F32, f"yacc{i}") for i in range(2)]
    ystg = [A.alloc([4, 256], BF16, f"ystg{i}") for i in range(2)]
    PT = [A.alloc([512], BF16, f"PTc{i}") for i in range(4)]
    rz = [A.alloc([4], F32, f"rzc{i}") for i in range(2)]
    cmb = [A.alloc([4], F32, f"cmb{i}") for i in range(2)]
    sc = A.alloc([64], F32, "sc")
    wk = A.alloc([64], F32, "wk")
    m8 = A.alloc([16], F32, "m8")
    pen2 = [A.alloc([2, 64], F32, f"pen2_{i}") for i in range(2)]
    STB = [PS[0], PS[1], PS[2]]
    OUTB = [PS[3], PS[4]]
    OUT2 = PS[5]
    PSM = PS[6]
    st_c = 0
    oc = 0

    def evac(OUT, t0, h, br, ya, hl, first, OUT2=None):
        nonlocal oc
        rzt, cb = rz[oc % 2], cmb[oc % 2]
        zv = OUT.ap[:, 0:260].rearrange("p (a b) -> p a b", b=65)[:, :, 64]
        if br == 0:
            P.op("dve", lambda v: v.tensor_scalar(out=rzt.ap, in0=zv, scalar1=1e-30, scalar2=None, op0=ALU.max), reads=[OUT.res], writes=[rzt.res])
            P.op("dve", lambda v: v.reciprocal(out=rzt.ap, in_=rzt.ap), reads=[rzt.res], writes=[rzt.res])
        else:
            P.op("dve", lambda v: v.reciprocal(out=rzt.ap, in_=zv), reads=[OUT.res], writes=[rzt.res])
        P.op("dve", lambda v: v.tensor_tensor(out=cb.ap, in0=rzt.ap, in1=k.gate.ap[:, t0:t0 + 4, h * 3 + br], op=ALU.mult), reads=[rzt.res, k.gate.res], writes=[cb.res])
        for ql in range(4):
            if first:
                P.op("dve", lambda v, ql=ql: v.tensor_scalar(out=ya.ap[:, ql, hl * 64:(hl + 1) * 64], in0=OUT.ap[:, ql * 65:ql * 65 + 64], scalar1=cb.ap[:, ql:ql + 1], scalar2=None, op0=ALU.mult),
                     reads=[OUT.res, cb.res], writes=[ya.res])
            else:
                P.op("dve", lambda v, ql=ql: v.scalar_tensor_tensor(out=ya.ap[:, ql, hl * 64:(hl + 1) * 64], in0=OUT.ap[:, ql * 65:ql * 65 + 64], scalar=cb.ap[:, ql:ql + 1], in1=ya.ap[:, ql, hl * 64:(hl + 1) * 64], op0=ALU.mult, op1=ALU.add),
                     reads=[OUT.res, cb.res, ya.res], writes=[ya.res])
            if OUT2 is not None:
                P.op("dve", lambda v, ql=ql: v.scalar_tensor_tensor(out=imp.ap[:, ql, :], in0=OUT2.ap[:, ql * 64:(ql + 1) * 64], scalar=rzt.ap[:, ql:ql + 1], in1=imp.ap[:, ql, :], op0=ALU.mult, op1=ALU.add),
                     reads=[OUT2.res, rzt.res, imp.res], writes=[imp.res])

    def pv(OUT, pt, kt, voff, qls, OUT2=None, vt=None):
        for ql in qls:
            if vt is None:
                rhs = VSW.ap[:, kt, voff:voff + 65]
                rr = VSW.res
            else:
                rhs = vt[0]
                rr = VCa.res
            P.op("pe", lambda e, ql=ql, rhs=rhs: e.matmul(OUT.ap[:, ql * 65:(ql + 1) * 65], lhsT=pt.ap[:, ql * 128:(ql + 1) * 128], rhs=rhs, start=False, stop=False, skip_group_check=True),
                 reads=[pt.res, rr], writes=[OUT.res])
            if OUT2 is not None:
                P.op("pe", lambda e, ql=ql: e.matmul(OUT2.ap[:, ql * 64:(ql + 1) * 64], lhsT=pt.ap[:, ql * 128:(ql + 1) * 128], rhs=vt[1], start=False, stop=False, skip_group_check=True),
                     reads=[pt.res, VCa.res], writes=[OUT2.res])

    def do_chunk(g, q):
        nonlocal st_c, oc
        if True:
            t0 = 4 * q
            qn = QN[q % 2]
            ya = yacc[q % 2]
            for hl in range(4):
                P.dma("sp", qn[hl].res.name, lambda e, hl=hl, g=g, q=q, qn=qn: e.dma_start(out=qn[hl].ap, in_=Sc.QN[4 * g + hl][:, q * 512:(q + 1) * 512]), writes=[qn[hl].res])
            P.op("pool", lambda g_: g_.memset(imp.ap, 0.0), writes=[imp.res])
            for hl in range(4):
                h = 4 * g + hl
                OUT = OUTB[oc % 2]
                P.op("dve", lambda v, OUT=OUT: v.memset(OUT.ap[:, 0:260], 0.0), writes=[OUT.res])
                P.op("dve", lambda v: v.memset(OUT2.ap[:, 0:256], 0.0), writes=[OUT2.res])
                for i in range(2 if q >= 4 else 1):
                    ST = STB[st_c % 3]
                    pt = PT[st_c % 4]
                    st_c += 1
                    P.op("pe", lambda e, ST=ST, i=i, hl=hl: e.matmul(ST.ap, lhsT=kcmpT[g].ap[0:64, i * 128:(i + 1) * 128], rhs=qn[hl].ap[0:64, :], start=True, stop=True),
                         reads=[kcmpT[g].res, qn[hl].res], writes=[ST.res])
                    P.op("act", lambda a, ST=ST, pt=pt: a.activation(out=pt.ap, in_=ST.ap, func=AF.Exp), reads=[ST.res], writes=[pt.res])
                    P.op("pool", lambda g_, pt=pt, i=i, q=q: g_.tensor_tensor(out=pt.ap, in0=pt.ap, in1=cmpmask.ap[:, i, q * 512:(q + 1) * 512], op=ALU.mult), reads=[pt.res, cmpmask.res], writes=[pt.res])
                    pv(OUT, pt, None, None, range(4), OUT2=OUT2, vt=(VCa.ap[:, i, g, 0:65], VCa.ap[:, i, g, 65:129]))
                evac(OUT, t0, h, 0, ya, hl, True, OUT2=OUT2)
                oc += 1
            for ql in range(4):
                t = t0 + ql
                p2 = pen2[ql % 2]
                P.op("dve", lambda v, ql=ql, t=t: v.tensor_tensor(out=sc.ap, in0=imp.ap[:, ql, :], in1=selbias.ap[:, t, :], op=ALU.add), reads=[imp.res, selbias.res], writes=[sc.res])
                P.op("dve", lambda v: v.max(out=m8.ap[:, 0:8], in_=sc.ap), reads=[sc.res], writes=[m8.res])
                P.op("dve", lambda v: v.match_replace(out=wk.ap, in_to_replace=m8.ap[:, 0:8], in_values=sc.ap, imm_value=-1e30), reads=[sc.res, m8.res], writes=[wk.res])
                P.op("dve", lambda v: v.max(out=m8.ap[:, 8:16], in_=wk.ap), reads=[wk.res], writes=[m8.res])
                P.op("dve", lambda v, p2=p2: v.tensor_scalar(out=p2.ap, in0=sc.ap.unsqueeze(1).to_broadcast([128, 2, 64]), scalar1=m8.ap[:, 15:16], scalar2=NEG, op0=ALU.is_lt, op1=ALU.mult),
                     reads=[sc.res, m8.res], writes=[p2.res])
                P.op("pe", lambda e, p2=p2, ql=ql: e.transpose(out=PSM.ap[:, ql * 128:(ql + 1) * 128], in_=p2.ap.rearrange("p a b -> p (a b)"), identity=k.identf.ap),
                     reads=[p2.res, k.identf.res], writes=[PSM.res])
            for hl in range(4):
                if hl % 2 == 0:
                    P.op("act", lambda a, hl=hl: a.copy(out=qn[hl].ap[64:128, :], in_=PSM.ap[64:128, :]), reads=[PSM.res], writes=[qn[hl].res])
                else:
                    P.op("dve", lambda v, hl=hl: v.tensor_copy(out=qn[hl].ap[64:128, :], in_=PSM.ap[64:128, :]), reads=[PSM.res], writes=[qn[hl].res])
            for hl in range(4):
                h = 4 * g + hl
                OUT = OUTB[oc % 2]
                P.op("dve", lambda v, OUT=OUT: v.memset(OUT.ap[:, 0:260], 0.0), writes=[OUT.res])
                def qk_s(kt, ST, hl=hl):
                    c0 = max(kt - 4 * q, 0) * 128
                    P.op("pe", lambda e: e.matmul(ST.ap[:, c0:512], lhsT=KE.ap[:, kt * 128:(kt + 1) * 128], rhs=qn[hl].ap[:, c0:512], start=True, stop=True),
                         reads=[KE.res, qn[hl].res], writes=[ST.res])

                def rest_s(kt, ST, pt, OUT=OUT):
                    j = kt - 4 * q
                    c0 = max(j, 0) * 128
                    P.op("act", lambda a: a.activation(out=pt.ap[:, c0:512], in_=ST.ap[:, c0:512], func=AF.Exp), reads=[ST.res], writes=[pt.res])
                    if j >= 0:
                        P.op("pool", lambda g_: g_.tensor_tensor(out=pt.ap[:, c0:c0 + 128], in0=pt.ap[:, c0:c0 + 128], in1=k.trib.ap[:, 0, :], op=ALU.mult), reads=[pt.res, k.trib.res], writes=[pt.res])
                    pv(OUT, pt, kt, g * 65, range(max(j, 0), 4))

                nk = 4 * q + 4
                sl = [(STB[(st_c + i) % 3], PT[(st_c + i) % 4]) for i in range(nk)]
                st_c += nk
                qk_s(0, sl[0][0])
                for kt in range(nk):
                    if kt + 1 < nk:
                        qk_s(kt + 1, sl[kt + 1][0])
                    rest_s(kt, sl[kt][0], sl[kt][1])
                evac(OUT, t0, h, 1, ya, hl, False)
                oc += 1
            for hl in range(4):
                h = 4 * g + hl
                OUT = OUTB[oc % 2]
                P.op("dve", lambda v, OUT=OUT: v.memset(OUT.ap[:, 0:260], 0.0), writes=[OUT.res])
                def rng_w(kt):
                    lo = max(kt - 4 * q, 0)
                    hi = min(kt + 4 - 4 * q, 3)
                    return lo, hi, lo * 128, (hi + 1) * 128

                def qk_w(kt, ST, hl=hl):
                    lo, hi, c0, c1 = rng_w(kt)
                    P.op("pe", lambda e: e.matmul(ST.ap[:, c0:c1], lhsT=KW.ap[0:64, kt * 128:(kt + 1) * 128], rhs=qn[hl].ap[0:64, c0:c1], start=True, stop=True),
                         reads=[KW.res, qn[hl].res], writes=[ST.res])

                def rest_w(kt, ST, pt, OUT=OUT):
                    lo, hi, c0, c1 = rng_w(kt)
                    P.op("act", lambda a: a.activation(out=pt.ap[:, c0:c1], in_=ST.ap[:, c0:c1], func=AF.Exp), reads=[ST.res], writes=[pt.res])
                    if kt >= 4 * q:
                        b0 = (kt - 4 * q) * 128
                        P.op("pool", lambda g_: g_.tensor_tensor(out=pt.ap[:, b0:b0 + 128], in0=pt.ap[:, b0:b0 + 128], in1=k.trib.ap[:, 0, :], op=ALU.mult), reads=[pt.res, k.trib.res], writes=[pt.res])
                    if 0 <= kt + 4 - 4 * q <= 3:
                        b1 = (kt + 4 - 4 * q) * 128
                        P.op("pool", lambda g_: g_.tensor_tensor(out=pt.ap[:, b1:b1 + 128], in0=pt.ap[:, b1:b1 + 128], in1=k.trib.ap[:, 1, :], op=ALU.mult), reads=[pt.res, k.trib.res], writes=[pt.res])
                    pv(OUT, pt, kt, (2 + g) * 65, range(lo, hi + 1))

                kts = list(range(max(4 * q - 4, 0), 4 * q + 4))
                sl = [(STB[(st_c + i) % 3], PT[(st_c + i) % 4]) for i in range(len(kts))]
                st_c += len(kts)
                qk_w(kts[0], sl[0][0])
                for i, kt in enumerate(kts):
                    if i + 1 < len(kts):
                        qk_w(kts[i + 1], sl[i + 1][0])
                    rest_w(kt, sl[i][0], sl[i][1])
                evac(OUT, t0, h, 2, ya, hl, False)
                oc += 1
            ys = ystg[q % 2]
            P.op("act", lambda a, ys=ys, ya=ya: a.copy(out=ys.ap, in_=ya.ap), reads=[ya.res], writes=[ys.res])
            P.dma("sp", ys.res.name, lambda e, ys=ys, g=g, t0=t0: e.dma_start(out=Sc.YN[t0:t0 + 4, :, g * 256:(g + 1) * 256].rearrange("t p f -> p t f"), in_=ys.ap), reads=[ys.res])

    for g in range(2):
        P.dma("sp", "KE", lambda e, g=g: e.dma_start(out=KE.ap, in_=Sc.KSL[g]), writes=[KE.res])
        P.dma("sp", "KW", lambda e, g=g: e.dma_start(out=KW.ap, in_=Sc.KWN[g]), writes=[KW.res])
        for q in range(8):
            do_chunk(g, q)
    P.barrier()
    A.release(m0)


def phase_d(k):
    P, A, I, Sc, PS = k.P, k.A, k.I, k.Sc, k.PS
    m0 = A.mark()
    WG = A.alloc([8, 2048], BF16, "WG")
    Wb = A.alloc([2, 4, 1024], BF16, "Wb")
    Wo = A.alloc([8, 1024], BF16, "Wo")
    gmix = A.alloc([8], F32, "gmixd")
    wst = [A.alloc([2048], F32, f"wstd{i}") for i in range(2)]
    WGr = [Res(f"WG{c}") for c in range(8)]
    Wbr = [Res(f"Wb{c}") for c in range(8)]
    Wor = [Res(f"Wo{c}") for c in range(8)]
    P.dma("sp", "cD", lambda e: e.dma_start(out=gmix.ap, in_=I.gmix), writes=[gmix.res])
    P.barrier()
    n = 0
    for c in range(8):
        ws = wst[n % 2]; n += 1
        P.dma("sp", ws.res.name, lambda e, c=c, ws=ws: e.dma_start(out=ws.ap, in_=I.w_in[c * 128:(c + 1) * 128, C_MG:C_MG + 2048]), writes=[ws.res])
        P.op("dve", lambda v, c=c, ws=ws: v.tensor_scalar(out=WG.ap[:, c, :], in0=ws.ap, scalar1=gmix.ap[:, c:c + 1], scalar2=None, op0=ALU.mult), reads=[ws.res, gmix.res], writes=[WGr[c]])
    for b in range(2):
        for c in range(4):
            ws = wst[n % 2]; n += 1
            P.dma("sp", ws.res.name, lambda e, b=b, c=c, ws=ws: e.dma_start(out=ws.ap[:, 0:1024], in_=I.wbr[b, c * 128:(c + 1) * 128, :]), writes=[ws.res])
            P.op("dve", lambda v, b=b, c=c, ws=ws: v.tensor_copy(out=Wb.ap[:, b, c, :], in_=ws.ap[:, 0:1024]), reads=[ws.res], writes=[Wbr[b * 4 + c]])
    for c in range(8):
        ws = wst[n % 2]; n += 1
        P.dma("sp", ws.res.name, lambda e, c=c, ws=ws: e.dma_start(out=ws.ap[:, 0:1024], in_=I.wout[c * 128:(c + 1) * 128, :]), writes=[ws.res])
        P.op("dve", lambda v, c=c, ws=ws: v.tensor_copy(out=Wo.ap[:, c, :], in_=ws.ap[:, 0:1024]), reads=[ws.res], writes=[Wor[c]])
    xs = [A.alloc([D], F32, f"xd{i}") for i in range(2)]
    XTt = [A.alloc([8, 128], BF16, f"XTt{i}") for i in range(2)]
    yfn = [A.alloc([2, 512], BF16, f"yfn{i}") for i in range(2)]
    yT = A.alloc([8, 128], BF16, "yT")
    sg = [A.alloc([512], F32, f"sg{i}") for i in range(2)]
    tt = [A.alloc([512], F32, f"tt{i}") for i in range(2)]
    mg = A.alloc([D], BF16, "mg")
    mT = A.alloc([8, 128], BF16, "mT")
    h2 = [A.alloc([D], F32, f"h2_{i}") for i in range(2)]
    PTR = [PS[0], PS[1]]
    PU = [PS[2], PS[3]]
    PG = [PS[4], PS[5]]
    PO = [PS[6], PS[7]]

    def do_tile(t):
        x_t, xt_t, y_t, h_t = xs[t % 2], XTt[t % 2], yfn[t % 2], h2[t % 2]
        P.dma("sp", x_t.res.name, lambda e: e.dma_start(out=x_t.ap, in_=I.x[t * 128:(t + 1) * 128, :]), writes=[x_t.res])
        P.dma("sp", xt_t.res.name, lambda e: e.dma_start(out=xt_t.ap, in_=Sc.XT[:, :, t * 128:(t + 1) * 128].rearrange("c p t -> p c t")), writes=[xt_t.res])
        P.dma("sp", y_t.res.name, lambda e: e.dma_start(out=y_t.ap[:, 0, :], in_=Sc.YF[t]), writes=[y_t.res])
        P.dma("sp", y_t.res.name, lambda e: e.dma_start(out=y_t.ap[:, 1, :], in_=Sc.YN[t]), writes=[y_t.res])
        ptb = PTR[0].ap.bitcast(BF16)
        for b in range(2):
            for c in range(4):
                P.op("pe", lambda e, b=b, c=c: e.transpose(out=ptb[:, (b * 4 + c) * 128:(b * 4 + c + 1) * 128], in_=y_t.ap[:, b, c * 128:(c + 1) * 128], identity=k.identb.ap),
                     reads=[y_t.res, k.identb.res], writes=[PTR[0].res])
        P.op("act", lambda a: a.copy(out=yT.ap, in_=ptb.rearrange("p (c t) -> p c t", c=8)), reads=[PTR[0].res], writes=[yT.res])
        for half in range(2):
            for b in range(2):
                for c in range(4):
                    P.op("pe", lambda e, b=b, c=c, half=half: e.matmul(PU[b].ap, lhsT=yT.ap[:, b * 4 + c, :], rhs=Wb.ap[:, b, c, half * 512:(half + 1) * 512], start=(c == 0), stop=(c == 3)),
                         reads=[yT.res, Wbr[b * 4 + c]], writes=[PU[b].res])
                for c in range(8):
                    P.op("pe", lambda e, b=b, c=c, half=half: e.matmul(PG[b].ap, lhsT=xt_t.ap[:, c, :], rhs=WG.ap[:, c, b * 1024 + half * 512:b * 1024 + (half + 1) * 512], start=(c == 0), stop=(c == 7)),
                         reads=[xt_t.res, WGr[c]], writes=[PG[b].res])
                P.op("act", lambda a, b=b: a.activation(out=sg[b].ap, in_=PG[b].ap, func=AF.Sigmoid), reads=[PG[b].res], writes=[sg[b].res])
                P.op("dve", lambda v, b=b: v.tensor_tensor(out=tt[b].ap, in0=PU[b].ap, in1=sg[b].ap, op=ALU.mult), reads=[PU[b].res, sg[b].res], writes=[tt[b].res])
            P.op("dve", lambda v, half=half: v.tensor_tensor(out=mg.ap[:, half * 512:(half + 1) * 512], in0=tt[0].ap, in1=tt[1].ap, op=ALU.add), reads=[tt[0].res, tt[1].res], writes=[mg.res])
        ptb1 = PTR[1].ap.bitcast(BF16)
        for c in range(8):
            P.op("pe", lambda e, c=c: e.transpose(out=ptb1[:, c * 128:(c + 1) * 128], in_=mg.ap[:, c * 128:(c + 1) * 128], identity=k.identb.ap), reads=[mg.res, k.identb.res], writes=[PTR[1].res])
        P.op("act", lambda a: a.copy(out=mT.ap, in_=ptb1.rearrange("p (c t) -> p c t", c=8)), reads=[PTR[1].res], writes=[mT.res])
        for half in range(2):
            for c in range(8):
                P.op("pe", lambda e, c=c, half=half: e.matmul(PO[half].ap, lhsT=mT.ap[:, c, :], rhs=Wo.ap[:, c, half * 512:(half + 1) * 512], start=(c == 0), stop=(c == 7)),
                     reads=[mT.res, Wor[c]], writes=[PO[half].res])
            P.op("dve", lambda v, half=half: v.tensor_tensor(out=h_t.ap[:, half * 512:(half + 1) * 512], in0=PO[half].ap, in1=x_t.ap[:, half * 512:(half + 1) * 512], op=ALU.add), reads=[PO[half].res, x_t.res], writes=[h_t.res])
        P.dma("sp", h_t.res.name, lambda e: e.dma_start(out=Sc.H2[t * 128:(t + 1) * 128, :], in_=h_t.ap), reads=[h_t.res])

    for t in range(NT):
        do_tile(t)
    P.barrier()
    A.release(m0)


NB_G = 16


def phase_e(k):
    P, A, I, Sc, PS = k.P, k.A, k.I, k.Sc, k.PS
    out = k.out
    m0 = A.mark()
    Wq = A.alloc([8, 2048], BF16, "Wq")
    Wqr = [Res(f"Wq{c}") for c in range(8)]
    skb = A.alloc([16, 128], BF16, "skb")
    gq = A.alloc([8], F32, "gq")
    gfb = A.alloc([D], F32, "gfb")
    gfin = A.alloc([D], F32, "gfin")
    iota = A.alloc([16], F32, "iota")
    wst = [A.alloc([2048], F32, f"wste{i}") for i in range(2)]
    P.dma("sp", "cE", lambda e: e.dma_start(out=gq.ap, in_=I.gffn8), writes=[gq.res])
    P.dma("sp", "cE", lambda e: e.dma_start(out=gfb.ap, in_=I.gffn[0:1, :].partition_broadcast(128)), writes=[gfb.res])
    P.dma("sp", "cE", lambda e: e.dma_start(out=gfin.ap, in_=I.gfin[0:1, :].partition_broadcast(128)), writes=[gfin.res])
    P.dma("sp", "cE", lambda e: e.dma_start(out=iota.ap, in_=I.iota16), writes=[iota.res])
    P.barrier()
    for c in range(8):
        ws = wst[c % 2]
        P.dma("sp", ws.res.name, lambda e, c=c, ws=ws: e.dma_start(out=ws.ap, in_=I.wq[c * 128:(c + 1) * 128, :]), writes=[ws.res])
        P.op("dve", lambda v, c=c, ws=ws: v.tensor_scalar(out=Wq.ap[:, c, :], in0=ws.ap, scalar1=gq.ap[:, c:c + 1], scalar2=None, op0=ALU.mult), reads=[ws.res, gq.res], writes=[Wqr[c]])
    P.dma("sp", wst[0].res.name, lambda e: e.dma_start(out=wst[0].ap, in_=I.skT), writes=[wst[0].res])
    P.op("dve", lambda v: v.tensor_copy(out=skb.ap.rearrange("p a b -> p (a b)"), in_=wst[0].ap), reads=[wst[0].res], writes=[skb.res])

    h2 = [A.alloc([D], F32, f"h2e{i}") for i in range(2)]
    junkb = A.alloc([D], BF16, "junkb")
    ssq = [A.alloc([4], F32, f"ssq{i}") for i in range(2)]
    xn2 = [A.alloc([D], F32, f"xn2_{i}") for i in range(2)]
    xnb = A.alloc([D], BF16, "xnb")
    x2T = A.alloc([8, 128], BF16, "x2T")
    qT = A.alloc([16, 128], BF16, "qT")
    sS = A.alloc([16, 128], F32, "sS")
    wk = A.alloc([128], F32, "wk_e")
    vals = A.alloc([8, 2, 16], F32, "vals")
    idxu = A.alloc([8, 2, 16], U32, "idxu")
    cand = A.alloc([16, 16], F32, "cand")
    wk2 = A.alloc([256], F32, "wk2")
    tops = A.alloc([8, 16], F32, "tops")
    posu = A.alloc([8, 16], U32, "posu")
    posf = A.alloc([8, 16], F32, "posf")
    aq = A.alloc([8, 16], F32, "aq")
    bq = A.alloc([8, 16], F32, "bq")
    i1f = A.alloc([8, 16], F32, "i1f")
    i2f = A.alloc([8, 16], F32, "i2f")
    oh = A.alloc([16, 16], F32, "oh")
    idx1 = A.alloc([8, 16], F32, "idx1")
    idx2 = A.alloc([8, 16], F32, "idx2")
    idxf = A.alloc([128], F32, "idxf")
    idxi = [A.alloc([128], U32, f"idxi{i}") for i in range(2)]
    negm = A.alloc([8], F32, "negm")
    ex = A.alloc([8, 16], F32, "ex")
    sm = A.alloc([8], F32, "sm")
    wgt = [A.alloc([8, 16], F32, f"wgt{i}") for i in range(2)]
    act = A.alloc([128], F32, "act")
    actr = [Res(f"act{i}") for i in range(128)]
    coef = A.alloc([128], F32, "coef")
    GB = [A.alloc([2 * D], BF16, f"gb{i}") for i in range(NB_G)]
    junkd = A.alloc([D], BF16, "junkd")
    cg = A.alloc([128], F32, "cg")
    cgr = [Res(f"cg{i}") for i in range(32)]
    coefr = [Res(f"coef{i}") for i in range(32)]
    DG = [A.alloc([128], BF16, f"dg{i}") for i in range(4)]
    junkf = A.alloc([D], F32, "junkf")
    hsum = A.alloc([D], F32, "hsum")
    ot = [A.alloc([D], F32, f"ot{i}") for i in range(2)]
    PTB = PS[0]
    PQ = [PS[1], PS[2]]
    PSS = [PS[3], PS[4]]
    ACC = [PS[5], PS[6]]
    gcnt = 0
    dcnt = 0

    def front(t):
        h_t, sq, x2 = h2[t % 2], ssq[t % 2], xn2[t % 2]
        ix, wg = idxi[t % 2], wgt[t % 2]
        P.dma("sp", h_t.res.name, lambda e: e.dma_start(out=h_t.ap, in_=Sc.H2[t * 128:(t + 1) * 128, :]), writes=[h_t.res])
        P.op("act", lambda a: a.activation(out=junkb.ap, in_=h_t.ap, func=AF.Square, accum_out=sq.ap[:, 0:1]), reads=[h_t.res], writes=[junkb.res, sq.res])
        P.op("act", lambda a: a.activation(out=sq.ap[:, 0:1], in_=sq.ap[:, 0:1], func=AF.Sqrt, scale=1.0 / D, bias=1e-6), reads=[sq.res], writes=[sq.res])
        P.op("dve", lambda v: v.reciprocal(out=sq.ap[:, 1:2], in_=sq.ap[:, 0:1]), reads=[sq.res], writes=[sq.res])
        P.op("dve", lambda v: v.scalar_tensor_tensor(out=x2.ap, in0=h_t.ap, scalar=sq.ap[:, 1:2], in1=gfb.ap, op0=ALU.mult, op1=ALU.mult), reads=[h_t.res, sq.res, gfb.res], writes=[x2.res])
        P.op("act", lambda a: a.activation(out=xnb.ap, in_=h_t.ap, func=AF.Copy, scale=sq.ap[:, 1:2]), reads=[h_t.res, sq.res], writes=[xnb.res])
        ptb = PTB.ap.bitcast(BF16)
        for c in range(8):
            P.op("pe", lambda e, c=c: e.transpose(out=ptb[:, c * 128:(c + 1) * 128], in_=xnb.ap[:, c * 128:(c + 1) * 128], identity=k.identb.ap), reads=[xnb.res, k.identb.res], writes=[PTB.res])
        P.op("act", lambda a: a.copy(out=x2T.ap, in_=ptb.rearrange("p (c t) -> p c t", c=8)), reads=[PTB.res], writes=[x2T.res])
        for ub in range(4):
            pq = PQ[ub % 2]
            for ul in range(4):
                u = ub * 4 + ul
                for c in range(8):
                    P.op("pe", lambda e, u=u, ul=ul, c=c, pq=pq: e.matmul(pq.ap[:, ul * 128:(ul + 1) * 128], lhsT=Wq.ap[:, c, u * 128:(u + 1) * 128], rhs=x2T.ap[:, c, :], start=(c == 0), stop=(c == 7)),
                         reads=[Wqr[c], x2T.res], writes=[pq.res])
            P.op("act", lambda a, ub=ub, pq=pq: a.copy(out=qT.ap[:, ub * 4:(ub + 1) * 4, :], in_=pq.ap.rearrange("p (a b) -> p a b", a=4)), reads=[pq.res], writes=[qT.res])
        for ub in range(4):
            pss = PSS[ub % 2]
            for ul in range(4):
                u = ub * 4 + ul
                P.op("pe", lambda e, u=u, ul=ul, pss=pss: e.matmul(pss.ap[:, ul * 128:(ul + 1) * 128], lhsT=qT.ap[:, u, :], rhs=skb.ap[:, u, :], start=True, stop=True),
                     reads=[qT.res, skb.res], writes=[pss.res])
            P.op("act", lambda a, ub=ub, pss=pss: a.copy(out=sS.ap[:, ub * 4:(ub + 1) * 4, :], in_=pss.ap.rearrange("p (a b) -> p a b", a=4)), reads=[pss.res], writes=[sS.res])

    def topk_gen(t):
        ix, wg = idxi[t % 2], wgt[t % 2]
        for h in range(8):
            for p in range(2):
                sp = sS.ap[:, 2 * h + p, :]
                v_ = vals.ap[:, h, p, :]
                i_ = idxu.ap[:, h, p, :]
                yield P.op("dve", lambda v, sp=sp, v_=v_: v.max(out=v_[:, 0:8], in_=sp), reads=[sS.res], writes=[vals.res])
                yield P.op("dve", lambda v, sp=sp, v_=v_, i_=i_: v.max_index(out=i_[:, 0:8], in_max=v_[:, 0:8], in_values=sp), reads=[sS.res, vals.res], writes=[idxu.res])
                yield P.op("dve", lambda v, sp=sp, v_=v_: v.match_replace(out=wk.ap, in_to_replace=v_[:, 0:8], in_values=sp, imm_value=-1e30), reads=[sS.res, vals.res], writes=[wk.res])
                yield P.op("dve", lambda v, v_=v_: v.max(out=v_[:, 8:16], in_=wk.ap), reads=[wk.res], writes=[vals.res])
                yield P.op("dve", lambda v, v_=v_, i_=i_: v.max_index(out=i_[:, 8:16], in_max=v_[:, 8:16], in_values=wk.ap), reads=[wk.res, vals.res], writes=[idxu.res])
            yield P.op("dve", lambda v, h=h: v.tensor_tensor(out=cand.ap, in0=vals.ap[:, h, 0, :].unsqueeze(2).to_broadcast([128, 16, 16]), in1=vals.ap[:, h, 1, :].unsqueeze(1).to_broadcast([128, 16, 16]), op=ALU.add),
                 reads=[vals.res], writes=[cand.res])
            c2 = cand.ap.rearrange("p a b -> p (a b)")
            yield P.op("dve", lambda v, h=h, c2=c2: v.max(out=tops.ap[:, h, 0:8], in_=c2), reads=[cand.res], writes=[tops.res])
            yield P.op("dve", lambda v, h=h, c2=c2: v.max_index(out=posu.ap[:, h, 0:8], in_max=tops.ap[:, h, 0:8], in_values=c2), reads=[cand.res, tops.res], writes=[posu.res])
            yield P.op("dve", lambda v, h=h, c2=c2: v.match_replace(out=wk2.ap, in_to_replace=tops.ap[:, h, 0:8], in_values=c2, imm_value=-1e30), reads=[cand.res, tops.res], writes=[wk2.res])
            yield P.op("dve", lambda v, h=h: v.max(out=tops.ap[:, h, 8:16], in_=wk2.ap), reads=[wk2.res], writes=[tops.res])
            yield P.op("dve", lambda v, h=h: v.max_index(out=posu.ap[:, h, 8:16], in_max=tops.ap[:, h, 8:16], in_values=wk2.ap), reads=[wk2.res, tops.res], writes=[posu.res])
        yield P.op("pool", lambda v: v.tensor_copy(out=posf.ap, in_=posu.ap), reads=[posu.res], writes=[posf.res])
        yield P.op("pool", lambda v: v.tensor_scalar(out=aq.ap, in0=posf.ap, scalar1=0.0625, scalar2=0.53125, op0=ALU.mult, op1=ALU.add), reads=[posf.res], writes=[aq.res])
        yield P.op("pool", lambda v: v.tensor_scalar(out=aq.ap, in0=aq.ap, scalar1=8388608.0, scalar2=None, op0=ALU.add), reads=[aq.res], writes=[aq.res])
        yield P.op("pool", lambda v: v.tensor_scalar(out=aq.ap, in0=aq.ap, scalar1=-8388609.0, scalar2=None, op0=ALU.add), reads=[aq.res], writes=[aq.res])
        yield P.op("dve", lambda v: v.scalar_tensor_tensor(out=bq.ap, in0=aq.ap, scalar=-16.0, in1=posf.ap, op0=ALU.mult, op1=ALU.add), reads=[aq.res, posf.res], writes=[bq.res])
        yield P.op("pool", lambda v: v.tensor_copy(out=i1f.ap, in_=idxu.ap[:, :, 0, :]), reads=[idxu.res], writes=[i1f.res])
        yield P.op("pool", lambda v: v.tensor_copy(out=i2f.ap, in_=idxu.ap[:, :, 1, :]), reads=[idxu.res], writes=[i2f.res])
        for h in range(8):
            for (sel, src, dst) in ((aq, i1f, idx1), (bq, i2f, idx2)):
                yield P.op("dve", lambda v, h=h, sel=sel: v.tensor_tensor(out=oh.ap, in0=sel.ap[:, h, :].unsqueeze(2).to_broadcast([128, 16, 16]), in1=iota.ap.unsqueeze(1).to_broadcast([128, 16, 16]), op=ALU.is_equal),
                     reads=[sel.res, iota.res], writes=[oh.res])
                yield P.op("dve", lambda v, h=h, src=src: v.tensor_tensor(out=oh.ap, in0=oh.ap, in1=src.ap[:, h, :].unsqueeze(1).to_broadcast([128, 16, 16]), op=ALU.mult), reads=[oh.res, src.res], writes=[oh.res])
                yield P.op("dve", lambda v, h=h, dst=dst: v.tensor_reduce(out=dst.ap[:, h, :], in_=oh.ap, axis=mybir.AxisListType.X, op=ALU.add), reads=[oh.res], writes=[dst.res])
        yield P.op("dve", lambda v: v.scalar_tensor_tensor(out=idxf.ap, in0=idx1.ap.rearrange("p a b -> p (a b)"), scalar=128.0, in1=idx2.ap.rearrange("p a b -> p (a b)"), op0=ALU.mult, op1=ALU.add),
             reads=[idx1.res, idx2.res], writes=[idxf.res])
        yield P.op("pool", lambda v: v.tensor_copy(out=ix.ap, in_=idxf.ap), reads=[idxf.res], writes=[ix.res])
        yield P.op("pool", lambda v: v.tensor_scalar(out=negm.ap, in0=tops.ap[:, :, 0], scalar1=-1.0, scalar2=None, op0=ALU.mult), reads=[tops.res], writes=[negm.res])
        yield P.op("dve", lambda v: v.tensor_tensor(out=ex.ap, in0=tops.ap, in1=negm.ap.unsqueeze(2).to_broadcast([128, 8, 16]), op=ALU.add), reads=[tops.res, negm.res], writes=[ex.res])

    def frontC(t):
        wg = wgt[t % 2]
        P.op("act", lambda a: a.activation(out=ex.ap, in_=ex.ap, func=AF.Exp), reads=[ex.res], writes=[ex.res])
        P.op("dve", lambda v: v.tensor_reduce(out=sm.ap, in_=ex.ap, axis=mybir.AxisListType.X, op=ALU.add), reads=[ex.res], writes=[sm.res])
        P.op("dve", lambda v: v.reciprocal(out=sm.ap, in_=sm.ap), reads=[sm.res], writes=[sm.res])
        P.op("dve", lambda v: v.tensor_tensor(out=wg.ap, in0=ex.ap, in1=sm.ap.unsqueeze(2).to_broadcast([128, 8, 16]), op=ALU.mult), reads=[ex.res, sm.res], writes=[wg.res])

    def back(t, gen=None):
        nonlocal gcnt, dcnt
        h_t, sq, x2 = h2[t % 2], ssq[t % 2], xn2[t % 2]
        ix, wg = idxi[t % 2], wgt[t % 2]
        o_t = ot[t % 2]
        wg2 = wg.ap.rearrange("p a b -> p (a b)")
        P.op("act", lambda v: v.copy(out=hsum.ap, in_=h_t.ap), reads=[h_t.res], writes=[hsum.res])
        gbs = []
        for slot in range(128):
            gb = GB[gcnt % NB_G]
            gcnt += 1
            gbs.append(gb)
            P.dma("pool", gb.res.name, lambda g_, gb=gb, slot=slot: g_.indirect_dma_start(out=gb.ap, out_offset=None, in_=Sc.UVB[:, :], in_offset=bass.IndirectOffsetOnAxis(ap=ix.ap[:, slot:slot + 1], axis=0)),
                  reads=[ix.res], writes=[gb.res])
            P.op("dve", lambda v, gb=gb, slot=slot: v.scalar_tensor_tensor(out=junkd.ap, in0=gb.ap[:, 0:D], scalar=1.0, in1=x2.ap, op0=ALU.mult, op1=ALU.mult, accum_out=act.ap[:, slot:slot + 1]),
                 reads=[gb.res, x2.res], writes=[junkd.res, actr[slot]])
            if gen is not None:
                for _ in range(2 if slot % 2 == 0 else 1):
                    next(gen, None)
            if slot % 4 != 3:
                continue
            s0 = slot - 3
            gi = s0 // 4
            P.op("act", lambda a, s0=s0: a.activation(out=cg.ap[:, s0:s0 + 4], in_=act.ap[:, s0:s0 + 4], func=AF.Gelu), reads=actr[s0:s0 + 4], writes=[cgr[gi]])
            P.op("dve", lambda v, s0=s0: v.tensor_tensor(out=coef.ap[:, s0:s0 + 4], in0=cg.ap[:, s0:s0 + 4], in1=wg2[:, s0:s0 + 4], op=ALU.mult), reads=[cgr[gi], wg.res], writes=[coefr[gi]])
            for sl in range(s0, s0 + 4):
                dg = DG[dcnt % 4]
                dcnt += 1
                gbv = gbs[sl]
                P.op("act", lambda a, dg=dg, sl=sl: a.activation(out=dg.ap, in_=k.identf.ap, func=AF.Copy, scale=coef.ap[:, sl:sl + 1]), reads=[k.identf.res, coefr[gi]], writes=[dg.res])
                for half in range(2):
                    P.op("pe", lambda e, dg=dg, gbv=gbv, half=half, sl=sl: e.matmul(ACC[half].ap, lhsT=dg.ap, rhs=gbv.ap[:, D + half * 512:D + (half + 1) * 512], start=(sl == 0), stop=(sl == 127)),
                         reads=[dg.res, gbv.res], writes=[ACC[half].res])
        if gen is not None:
            for _ in gen:
                pass
        for half in range(2):
            P.op("dve", lambda v, half=half: v.tensor_tensor(out=hsum.ap[:, half * 512:(half + 1) * 512], in0=ACC[half].ap, in1=hsum.ap[:, half * 512:(half + 1) * 512], op=ALU.add), reads=[ACC[half].res, hsum.res], writes=[hsum.res])
        P.op("act", lambda a: a.activation(out=junkb.ap, in_=hsum.ap, func=AF.Square, accum_out=sq.ap[:, 2:3]), reads=[hsum.res], writes=[junkb.res, sq.res])
        P.op("act", lambda a: a.activation(out=sq.ap[:, 2:3], in_=sq.ap[:, 2:3], func=AF.Sqrt, scale=1.0 / D, bias=1e-6), reads=[sq.res], writes=[sq.res])
        P.op("dve", lambda v: v.reciprocal(out=sq.ap[:, 3:4], in_=sq.ap[:, 2:3]), reads=[sq.res], writes=[sq.res])
        P.op("dve", lambda v: v.scalar_tensor_tensor(out=o_t.ap, in0=hsum.ap, scalar=sq.ap[:, 3:4], in1=gfin.ap, op0=ALU.mult, op1=ALU.mult), reads=[hsum.res, sq.res, gfin.res], writes=[o_t.res])
        P.dma("sp", o_t.res.name, lambda e: e.dma_start(out=out[t * 128:(t + 1) * 128, :], in_=o_t.ap), reads=[o_t.res])

    front(0)
    for _ in topk_gen(0):
        pass
    frontC(0)
    for t in range(k.nt_e):
        if t + 1 < k.nt_e:
            front(t + 1)
            back(t, topk_gen(t + 1))
            frontC(t + 1)
        else:
            back(t)
    P.barrier()
    A.release(m0)


def _host_consts():
    import ml_dtypes
    bf = ml_dtypes.bfloat16
    f32 = np.float32
    inv = np.power(f32(500000.0), -np.arange(0, 16, 2, dtype=f32) / f32(16)).astype(f32)
    pos = np.arange(S, dtype=f32)
    ang = (pos[:, None] * inv[None, :]).astype(f32)
    cos, sin = np.cos(ang).astype(f32), np.sin(ang).astype(f32)
    cosT = np.concatenate([cos.T, cos.T], 0)
    sinT = np.concatenate([sin.T, sin.T], 0)
    c_rope = np.zeros((128, 4 * S), f32)
    c_rope[:16] = np.concatenate([cosT * f32(0.125), sinT * f32(0.125), cosT, sinT], 1).astype(f32)
    cend = (np.arange(255) * 16 + 31).astype(f32)
    angc = (cend[:, None] * inv[None, :]).astype(f32)
    cc = np.zeros((16, 256), f32)
    sc = np.zeros((16, 256), f32)
    cc[:, :255] = np.concatenate([np.cos(angc).T, np.cos(angc).T], 0)
    sc[:, :255] = np.concatenate([np.sin(angc).T, np.sin(angc).T], 0)
    c_ropec = np.zeros((128, 512), f32)
    c_ropec[:16] = np.concatenate([cc, sc], 1)
    s_ = np.arange(128)[:, None]
    t_ = np.arange(128)[None, :]
    tri = (s_ <= t_).astype(f32)
    sup = (s_ > t_).astype(f32)
    c_trib = np.concatenate([tri, sup], 1).astype(bf)
    c_trif = np.concatenate([tri, np.ones((128, 128), f32)], 1).astype(f32)
    n_ = (np.arange(2)[None, :, None] * 128 + np.arange(128)[:, None, None])
    tt = np.arange(S)[None, None, :]
    cmpmask = ((n_ < 255) & (16 * n_ + 31 <= tt)).astype(f32).reshape(128, 2 * S).astype(bf)
    etab = np.zeros((128, S), f32)
    etab[64:] = (np.arange(S)[None, :] // 64 == np.arange(64)[:, None])
    etab = etab.astype(bf)
    q = np.arange(S)
    qb = q // 64
    j = np.arange(64)[None, :]
    sb = np.zeros((S, 64), np.float64)
    sb += (j == 0) * 1e9 + (j == qb[:, None]) * 2e9 + (j == qb[:, None] - 1) * 4e9
    sb = np.where(j > qb[:, None], -1.0 - j / 64.0, sb)
    selbias = sb.astype(f32).reshape(NT, 128, 64).transpose(1, 0, 2).reshape(128, NT * 64)
    cs = np.arange(255)[:, None] * 16
    ss_ = np.arange(64)[None, :] * 64
    ov = np.clip(np.minimum(cs + 32, ss_ + 64) - np.maximum(cs, ss_), 0, None) / 32.0
    ova = np.zeros((256, 65), f32)
    ova[:255, 0] = 1.0
    ova[:255, 1:] = ov
    ovaug = ova.reshape(2, 128, 65).transpose(1, 0, 2).reshape(128, 130).astype(bf)
    iota16 = np.tile(np.arange(16, dtype=f32)[None, :], (128, 1))
    return dict(c_rope=c_rope, c_ropec=c_ropec, c_trib=c_trib, c_trif=c_trif, c_cmpmask=cmpmask, c_etab=etab,
                c_selbias=np.ascontiguousarray(selbias), c_ovaug=np.ascontiguousarray(ovaug), c_iota16=iota16)


def prep_shared(inp):
    a = lambda v: np.ascontiguousarray(np.asarray(v, dtype=np.float32))
    sh = dict(
        w_in=a(inp["w_in"][0]),
        gmix=a(np.asarray(inp["norm_mix"])[0].reshape(8, 128).T),
        fbias=a(np.asarray(inp["fox_f_bias"])[0].reshape(1, 8)),
        w1=a(np.concatenate([np.asarray(inp["nsa_cmp_w1"])[0].reshape(2, 32, 64, 128).transpose(2, 0, 1, 3).reshape(64, -1), np.zeros((64, 8192), np.float32)], 0)),
        pos=a(np.concatenate([np.asarray(inp["nsa_cmp_pos"])[0].transpose(2, 0, 1).reshape(64, 64), np.zeros((64, 64), np.float32)], 0)),
        w2=a(np.asarray(inp["nsa_cmp_w2"])[0].transpose(1, 0, 2).reshape(128, 128)),
        wbr=a(inp["w_branch"][0]),
        wout=a(inp["w_out"][0]),
        gffn=a(np.asarray(inp["norm_ffn"])[0].reshape(1, D)),
        gffn8=a(np.asarray(inp["norm_ffn"])[0].reshape(8, 128).T),
        wq=a(inp["peer_wq"][0]),
        skT=a(np.asarray(inp["peer_subkeys"])[0].transpose(3, 0, 1, 2).reshape(128, 16 * 128)),
        pu=a(inp["peer_u"][0]),
        pv=a(inp["peer_v"][0]),
        gfin=a(np.asarray(inp["norm_final"]).reshape(1, D)),
    )
    sh.update(_host_consts())
    return sh


_NC_CACHE = {}


def kernel(**inputs):
    x = np.asarray(inputs["x"], dtype=np.float32)
    sh = prep_shared(inputs)
    if "nc" not in _NC_CACHE:
        _NC_CACHE["nc"] = build_program()
    nc = _NC_CACHE["nc"]
    in_maps = [dict(sh, x=np.ascontiguousarray(x[b])) for b in range(8)]
    res = run_bass_kernel_spmd(nc, in_maps, core_ids=list(range(8)))
    return np.stack([np.asarray(r["out"], dtype=np.float32) for r in res.results], 0)
```

```python
import numpy as np
from contextlib import ExitStack
import concourse.bass as bass
import concourse.mybir as mybir
from concourse.bass_utils import run_bass_kernel_spmd

F32 = mybir.dt.float32
BF16 = mybir.dt.bfloat16
I32 = mybir.dt.int32
U32 = mybir.dt.uint32
U8 = mybir.dt.uint8
AF = mybir.ActivationFunctionType
ALU = mybir.AluOpType
DTSIZE = {F32: 4, BF16: 2, I32: 4, U32: 4, U8: 1}

S = 4096
D = 1024
NT = 32
DIN_A = 2848
C_FQ, C_FK, C_FV, C_FL, C_NQ = 0, 512, 1024, 1536, 1544
C_KC, C_VC, C_KSL, C_VSL, C_KWN, C_VWN, C_NG, C_MG = 2056, 2184, 2312, 2440, 2568, 2696, 2824, 2848
NEG = -30000.0


class Res:
    __slots__ = ("name", "w", "rs", "excl")

    def __init__(self, name="", excl=False):
        self.name = name
        self.excl = excl
        self.w = None
        self.rs = {}


class Op:
    __slots__ = ("eng", "fn", "deps", "needed", "num", "dma")


class Prog:
    ENG = ("pe", "act", "dve", "pool", "sp")

    def __init__(self, nc, es):
        self.nc = nc
        self.es = es
        self.ops = {e: [] for e in self.ENG}
        self.esem = {e: es.enter_context(nc.semaphore("es_" + e)) for e in self.ENG}
        self.dsem = {}
        self.last = {e: None for e in self.ENG}
        self.qhist = {}
        self.max_out = 10 ** 9

    def _dsem(self, key):
        if key not in self.dsem:
            self.dsem[key] = [self.es.enter_context(self.nc.semaphore("ds_" + key)), 0]
        return self.dsem[key]

    def _deps(self, eng, reads, writes):
        deps = []
        for r in reads:
            if r.w is not None:
                deps.append(r.w)
            if r.excl:
                deps.extend(v for kk, v in r.rs.items() if kk != eng)
        for w in writes:
            if w.w is not None:
                deps.append(w.w)
            deps.extend(w.rs.values())
        out = []
        for d in deps:
            if isinstance(d, Op):
                if d.eng == eng and eng == "pe":
                    continue
                d.needed = True
            out.append(d)
        return out

    def op(self, eng, fn, reads=(), writes=()):
        o = Op()
        o.eng, o.fn, o.needed, o.dma, o.num = eng, fn, False, None, 0
        o.deps = self._deps(eng, reads, writes)
        for r in reads:
            r.rs[eng] = o
        for w in writes:
            w.w = o
            w.rs = {}
        self.ops[eng].append(o)
        self.last[eng] = o
        return o

    def dma(self, q, key, fn, reads=(), writes=()):
        o = Op()
        o.eng, o.fn, o.needed, o.num = q, fn, False, 0
        o.deps = self._deps(q, reads, writes)
        h = self.qhist.setdefault(q, [])
        if len(h) >= self.max_out:
            o.deps.append(h[-self.max_out])
        s = self._dsem(key)
        s[1] += 16
        ev = (key, s[1])
        o.dma = key
        h.append(ev)
        for r in reads:
            r.rs[("d", key)] = ev
        for w in writes:
            w.w = ev
            w.rs = {}
        self.ops[q].append(o)
        return o

    def barrier(self):
        lasts = [self.last[e] for e in self.ENG if self.last[e] is not None]
        for o in lasts:
            o.needed = True
        dev = [(k, v[1]) for k, v in self.dsem.items() if v[1] > 0]
        for e in self.ENG:
            o = Op()
            o.eng, o.fn, o.needed, o.dma, o.num = e, None, False, None, 0
            o.deps = [l for l in lasts if not (l.eng == e == "pe")] + dev
            self.ops[e].append(o)

    def emit(self, blk):
        for e in self.ENG:
            c = 0
            for o in self.ops[e]:
                if o.dma is None and o.needed and o.fn is not None:
                    c += 1
                    o.num = c

        def run(e, engobj):
            seen = {}
            for o in self.ops[e]:
                for d in o.deps:
                    if isinstance(d, Op):
                        sem, val, k = self.esem[d.eng], d.num, ("e", d.eng)
                    else:
                        sem, val, k = self.dsem[d[0]][0], d[1], ("d", d[0])
                    if seen.get(k, 0) < val:
                        engobj.wait_ge(sem, val)
                        seen[k] = val
                if o.fn is not None:
                    ins = o.fn(engobj)
                    if o.dma is not None:
                        ins.then_inc(self.dsem[o.dma][0], 16)
                    elif o.needed:
                        ins.then_inc(self.esem[e], 1)

        blk.tensor(lambda t: run("pe", t))
        blk.scalar(lambda t: run("act", t))
        blk.vector(lambda t: run("dve", t))
        blk.gpsimd(lambda t: run("pool", t))
        blk.sync(lambda t: run("sp", t))


class Tile:
    __slots__ = ("ap", "res")

    def __init__(self, ap, res):
        self.ap = ap
        self.res = res


class Arena:
    def __init__(self, nc, es, nbytes):
        self.t = es.enter_context(nc.sbuf_tensor("arena", [128, nbytes], U8))
        self.off = 0
        self.cap = nbytes
        self.n = 0

    def alloc(self, shape, dt, name=None):
        n = int(np.prod(shape)) * DTSIZE[dt]
        n_al = (n + 63) // 64 * 64
        assert self.off + n_al <= self.cap, f"SBUF arena overflow {self.off}+{n_al}>{self.cap} ({name})"
        ap = self.t[:, self.off:self.off + n].bitcast(dt)
        self.off += n_al
        if len(shape) > 1:
            names = " ".join(f"d{i}" for i in range(len(shape)))
            kw = {f"d{i}": int(shape[i]) for i in range(len(shape))}
            ap = ap.rearrange(f"p ({names}) -> p {names}", **kw)
        self.n += 1
        return Tile(ap, Res(name or f"t{self.n}"))

    def mark(self):
        return self.off

    def release(self, m):
        self.off = m


class K:
    pass


_DBG = {}


def build_program(dbg=False, phases="ABCDE", lv=9, nt_e=NT):
    nc = bass.Bass("TRN2", target_bir_lowering=False)
    es = ExitStack()
    k = K()
    k.nc = nc
    k.lv = lv
    k.nt_e = nt_e
    import os
    k.skip = os.environ.get('KSKIP', '')

    def din(name, shape, dt=F32):
        return nc.dram_tensor(name, list(shape), dt, kind="ExternalInput").ap()

    def dscr(name, shape, dt):
        return nc.dram_tensor(name, list(shape), dt, kind=("ExternalOutput" if dbg else "Internal")).ap()

    I = K()
    I.x = din("x", [S, D])
    I.w_in = din("w_in", [D, 4896])
    I.gmix = din("gmix", [128, 8])
    I.fbias = din("fbias", [1, 8])
    I.w1 = din("w1", [128, 2 * 32 * 128])
    I.pos = din("pos", [128, 2 * 32])
    I.w2 = din("w2", [128, 2 * 64])
    I.wbr = din("wbr", [2, 512, 1024])
    I.wout = din("wout", [D, D])
    I.gffn = din("gffn", [1, D])
    I.gffn8 = din("gffn8", [128, 8])
    I.wq = din("wq", [D, 2048])
    I.skT = din("skT", [128, 16 * 128])
    I.pu = din("pu", [16384, D])
    I.pv = din("pv", [16384, D])
    I.gfin = din("gfin", [1, D])
    I.rope = din("c_rope", [128, 4 * S])
    I.ropec = din("c_ropec", [128, 2 * 256])
    I.trib = din("c_trib", [128, 2 * 128], BF16)
    I.trif = din("c_trif", [128, 2 * 128])
    I.cmpmask = din("c_cmpmask", [128, 2 * S], BF16)
    I.etab = din("c_etab", [128, S], BF16)
    I.selbias = din("c_selbias", [128, NT * 64])
    I.ovaug = din("c_ovaug", [128, 2 * 65], BF16)
    I.iota16 = din("c_iota16", [128, 16])
    out = nc.dram_tensor("out", [S, D], F32, kind="ExternalOutput").ap()

    Sc = K()
    Sc.XT = dscr("s_xt", [8, 128, S], BF16)
    Sc.QF = dscr("s_qf", [8, 128, S], BF16)
    Sc.KF = dscr("s_kf", [8, 128, S], BF16)
    Sc.VF = dscr("s_vf", [NT, 128, 8 * 65], BF16)
    Sc.QN = dscr("s_qn", [8, 128, S], BF16)
    Sc.KC = dscr("s_kc", [2, 128, S], BF16)
    Sc.VC = dscr("s_vc", [2, 128, S], BF16)
    Sc.KSL = dscr("s_ksl", [2, 128, S], BF16)
    Sc.KWN = dscr("s_kwn", [2, 128, S], BF16)
    Sc.VSW = dscr("s_vsw", [NT, 128, 4 * 65], BF16)
    Sc.YF = dscr("s_yf", [NT, 128, 512], BF16)
    Sc.YN = dscr("s_yn", [NT, 128, 512], BF16)
    Sc.H2 = dscr("s_h2", [S, D], F32)
    Sc.UVB = nc.dram_tensor("s_uvb", [16384, 2 * D], BF16, kind="Internal").ap()
    if dbg:
        Sc.dbg1 = dscr("s_dbg1", [128, NT * 8], F32)
        Sc.dbg2 = dscr("s_dbg2", [128, NT * 24], F32)
    k.I, k.Sc, k.out = I, Sc, out
    k.dbgC = (dscr("s_dbgc0", [128, 512], BF16), dscr("s_dbgc1", [128, 2 * 2 * 129], BF16)) if dbg else None

    P = Prog(nc, es)
    A = Arena(nc, es, 204800)
    k.P, k.A = P, A
    PS = []
    for i in range(8):
        t = es.enter_context(nc.psum_tensor(f"psb{i}", [128, 512], F32))
        PS.append(Tile(t[:, :], Res(f"ps{i}", excl=True)))
    k.PS = PS

    k.identb = A.alloc([128], BF16, "identb")
    k.identf = A.alloc([128], F32, "identf")
    k.trib = A.alloc([2, 128], BF16, "trib")
    k.trif = A.alloc([2, 128], F32, "trif")
    k.logf = A.alloc([NT, 8], F32, "logf")
    k.gate = A.alloc([NT, 24], F32, "gate")
    k.fb = A.alloc([8], F32, "fb")

    def setup_consts():
        P.op("pool", lambda g: g.memset(k.identf.ap, 0.0), writes=[k.identf.res])
        P.op("pool", lambda g: g.affine_select(out=k.identf.ap, in_=k.identf.ap, pattern=[[-1, 128]],
                                               compare_op=ALU.not_equal, fill=1.0, base=0, channel_multiplier=1),
             reads=[k.identf.res], writes=[k.identf.res])
        P.op("dve", lambda v: v.tensor_copy(out=k.identb.ap, in_=k.identf.ap), reads=[k.identf.res], writes=[k.identb.res])
        P.dma("sp", "const", lambda e: e.dma_start(out=k.trib.ap, in_=I.trib.rearrange("p (a b) -> p a b", a=2)), writes=[k.trib.res])
        P.dma("sp", "const", lambda e: e.dma_start(out=k.trif.ap, in_=I.trif.rearrange("p (a b) -> p a b", a=2)), writes=[k.trif.res])
        P.dma("sp", "const", lambda e: e.dma_start(out=k.fb.ap, in_=I.fbias[0:1, :].partition_broadcast(128)), writes=[k.fb.res])

    setup_consts()
    k.conv = ("E" in phases)
    if "A" in phases:
        phase_a(k)
    elif k.conv:
        phase_0(k, None)
    P.barrier()
    if "B" in phases:
        phase_b(k)
        P.barrier()
    if "C" in phases:
        phase_c(k)
        P.barrier()
    if "D" in phases:
        phase_d(k)
        P.barrier()
    if "E" in phases:
        phase_e(k)
        P.barrier()
    if dbg and "A" in phases:
        P.dma("sp", "dbg", lambda e: e.dma_start(out=Sc.dbg1, in_=k.logf.ap.rearrange("p a b -> p (a b)")), reads=[k.logf.res])
        P.dma("sp", "dbg", lambda e: e.dma_start(out=Sc.dbg2, in_=k.gate.ap.rearrange("p a b -> p (a b)")), reads=[k.gate.res])
    P.barrier()
    blk = es.enter_context(nc.Block())
    P.emit(blk)
    es.close()
    return nc


def phase_0(k, after):
    P, A, I, Sc = k.P, k.A, k.I, k.Sc
    RB = 2
    st = [A.alloc([RB, D], F32, f"cv_s{i}") for i in range(3)]
    sb = [A.alloc([RB, D], BF16, f"cv_b{i}") for i in range(3)]
    n = 0
    for (src, c0) in ((I.pu, 0), (I.pv, D)):
        sv = src.rearrange("(p r) d -> p r d", p=128)
        dv = Sc.UVB.rearrange("(p r) d -> p r d", p=128)[:, :, c0:c0 + D]
        for r0 in range(0, 128, RB):
            a, b = st[n % 3], sb[n % 3]
            P.dma("pool", a.res.name, lambda e, a=a, sv=sv, r0=r0: e.dma_start(out=a.ap, in_=sv[:, r0:r0 + RB, :]), reads=([after] if (after is not None and n == 0) else []), writes=[a.res])
            P.op("pool", lambda e, a=a, b=b: e.tensor_copy(out=b.ap, in_=a.ap), reads=[a.res], writes=[b.res])
            P.dma("pool", b.res.name, lambda e, b=b, dv=dv, r0=r0: e.dma_start(out=dv[:, r0:r0 + RB, :], in_=b.ap), reads=[b.res])
            n += 1


FM_UNITS = ([("QF", h, C_FQ + 64 * h, None, 0.125) for h in range(8)]
            + [("KF", h, C_FK + 64 * h, None, 1.0) for h in range(8)]
            + [("QN", h, C_NQ + 64 * h, h, 0.125) for h in range(8)]
            + [("KC", g, C_KC + 64 * g, None, 1.0) for g in range(2)]
            + [("VC", g, C_VC + 64 * g, None, 1.0) for g in range(2)]
            + [("KSL", g, C_KSL + 64 * g, 8 + g, 1.0) for g in range(2)]
            + [("KWN", g, C_KWN + 64 * g, 10 + g, 1.0) for g in range(2)])


def phase_a(k):
    P, A, I, Sc, PS = k.P, k.A, k.I, k.Sc, k.PS
    m0 = A.mark()
    W1 = A.alloc([8, DIN_A], BF16, "W1")
    W1r = [Res(f"W1_{c}") for c in range(8)]
    Wr = A.alloc([8, 12, 32], BF16, "Wr")
    Wr0 = Res("Wr0")
    P.op("pool", lambda g: g.memset(Wr.ap, 0.0), writes=[Wr0])
    Wrr = [Res(f"Wr_{c}") for c in range(8)]
    gmix = A.alloc([8], F32, "gmix")
    wst = [A.alloc([DIN_A], F32, f"wst{i}") for i in range(2)]
    P.dma("sp", "const", lambda e: e.dma_start(out=gmix.ap, in_=I.gmix), writes=[gmix.res])
    for c in range(8):
        ws = wst[c % 2]
        P.dma("sp", ws.res.name, lambda e, c=c, ws=ws: e.dma_start(out=ws.ap, in_=I.w_in[c * 128:(c + 1) * 128, 0:DIN_A]), writes=[ws.res])
        P.op("dve", lambda v, c=c, ws=ws: v.tensor_scalar(out=W1.ap[:, c, :], in0=ws.ap, scalar1=gmix.ap[:, c:c + 1], scalar2=None, op0=ALU.mult),
             reads=[ws.res, gmix.res], writes=[W1r[c]])
        for (base, n, u0) in ((C_NQ, 8, 0), (C_KSL, 2, 8), (C_KWN, 2, 10)):
            src = W1.ap[:, c, base:base + n * 64].rearrange("p (u d) -> p u d", d=64)
            P.op("pool", lambda g, src=src, c=c, u0=u0, n=n: g.tensor_scalar(out=Wr.ap[:, c, u0:u0 + n, 0:8], in0=src[:, :, 8:16], scalar1=-1.0, scalar2=None, op0=ALU.mult),
                 reads=[W1r[c], Wr0], writes=[Wrr[c]])
            P.op("pool", lambda g, src=src, c=c, u0=u0, n=n: g.tensor_copy(out=Wr.ap[:, c, u0:u0 + n, 8:16], in_=src[:, :, 0:8]),
                 reads=[W1r[c]], writes=[Wrr[c]])

    xs = [A.alloc([D], F32, f"xs{i}") for i in range(2)]
    junk = A.alloc([D], BF16, "junk")
    xn = [A.alloc([D], BF16, f"xn{i}") for i in range(2)]
    ss = A.alloc([NT], F32, "ss")
    ssr = [Res(f"ss{t}") for t in range(NT)]
    XTc = [A.alloc([8, 512], BF16, f"XTc{i}") for i in range(2)]
    XTr = [[Res(f"XT{i}_{j}") for j in range(4)] for i in range(2)]
    VFt = [A.alloc([8, 65], BF16, f"VFt{i}") for i in range(2)]
    VSWt = [A.alloc([4, 65], BF16, f"VSWt{i}") for i in range(2)]
    zraw = A.alloc([NT, 8], F32, "zraw")
    graw = A.alloc([NT, 24], F32, "graw")
    rope = [A.alloc([4, 512], F32, f"rope{i}") for i in range(2)]
    etc = [A.alloc([512], BF16, f"etc{i}") for i in range(2)]
    fm = [A.alloc([512], BF16, f"fm{i}") for i in range(4)]
    rt1 = [A.alloc([512], F32, f"rt1_{i}") for i in range(2)]
    rt2 = [A.alloc([512], F32, f"rt2_{i}") for i in range(2)]
    for i in range(2):
        P.op("pool", lambda g, i=i: g.memset(VFt[i].ap, 1.0), writes=[VFt[i].res])
        P.op("pool", lambda g, i=i: g.memset(VSWt[i].ap, 1.0), writes=[VSWt[i].res])
    if k.conv:
        phase_0(k, W1r[7])
    PT = [PS[0], PS[1]]
    PSV, PSS = PS[2], PS[3]
    PSU = [PS[4], PS[5]]
    PSR = [PS[6], PS[7]]
    ucount = 0
    rcount = 0
    for t in range(NT):
        q, j = divmod(t, 4)
        x_t = xs[t % 2]
        xn_t = xn[t % 2]
        P.dma("sp", x_t.res.name, lambda e, t=t, x_t=x_t: e.dma_start(out=x_t.ap, in_=I.x[t * 128:(t + 1) * 128, :]), writes=[x_t.res])
        P.op("act", lambda a, t=t, x_t=x_t: a.activation(out=junk.ap, in_=x_t.ap, func=AF.Square, accum_out=ss.ap[:, t:t + 1]),
             reads=[x_t.res], writes=[junk.res, ssr[t]])
        P.op("act", lambda a, t=t: a.activation(out=ss.ap[:, t:t + 1], in_=ss.ap[:, t:t + 1], func=AF.Sqrt, scale=1.0 / D, bias=1e-6),
             reads=[ssr[t]], writes=[ssr[t]])
        P.op("dve", lambda v, t=t: v.reciprocal(out=ss.ap[:, t:t + 1], in_=ss.ap[:, t:t + 1]), reads=[ssr[t]], writes=[ssr[t]])
        P.op("act", lambda a, t=t, x_t=x_t, xn_t=xn_t: a.activation(out=xn_t.ap, in_=x_t.ap, func=AF.Copy, scale=ss.ap[:, t:t + 1]),
             reads=[x_t.res, ssr[t]], writes=[xn_t.res])
        pt = PT[t % 2]
        ptb = pt.ap.bitcast(BF16)
        for c in range(8):
            P.op("pe", lambda e, c=c, ptb=ptb, xn_t=xn_t: e.transpose(out=ptb[:, c * 128:(c + 1) * 128], in_=xn_t.ap[:, c * 128:(c + 1) * 128], identity=k.identb.ap),
                 reads=[xn_t.res, k.identb.res], writes=[pt.res])
        xc = XTc[q % 2]
        P.op("dve", lambda v, ptb=ptb, xc=xc, j=j: v.tensor_copy(out=xc.ap[:, :, j * 128:(j + 1) * 128], in_=ptb.rearrange("p (c t) -> p c t", c=8)),
             reads=[pt.res], writes=[XTr[q % 2][j]])
        xr = XTr[q % 2][j]
        for c in range(8):
            P.op("pe", lambda e, c=c, xc=xc, j=j: e.matmul(PSV.ap[:, 0:512], lhsT=xc.ap[:, c, j * 128:(j + 1) * 128], rhs=W1.ap[:, c, C_FV:C_FV + 512], start=(c == 0), stop=(c == 7)),
                 reads=[xr, W1r[c]], writes=[PSV.res])
        for (cb, n, o0) in ((C_FL, 8, 0), (C_NG, 24, 8), (C_VSL, 128, 32), (C_VWN, 128, 160)):
            for c in range(8):
                P.op("pe", lambda e, c=c, xc=xc, j=j, cb=cb, n=n, o0=o0: e.matmul(PSS.ap[:, o0:o0 + n], lhsT=xc.ap[:, c, j * 128:(j + 1) * 128], rhs=W1.ap[:, c, cb:cb + n], start=(c == 0), stop=(c == 7)),
                     reads=[xr, W1r[c]], writes=[PSS.res])
        vf = VFt[t % 2]
        vsw = VSWt[t % 2]
        P.op("act", lambda a, vf=vf: a.copy(out=vf.ap[:, :, 0:64], in_=PSV.ap[:, 0:512].rearrange("p (h d) -> p h d", h=8)), reads=[PSV.res], writes=[vf.res])
        P.op("dve", lambda v, vsw=vsw: v.tensor_copy(out=vsw.ap[:, :, 0:64], in_=PSS.ap[:, 32:288].rearrange("p (h d) -> p h d", h=4)), reads=[PSS.res], writes=[vsw.res])
        P.op("dve", lambda v, t=t: v.tensor_copy(out=zraw.ap[:, t, :], in_=PSS.ap[:, 0:8]), reads=[PSS.res], writes=[zraw.res])
        P.op("dve", lambda v, t=t: v.tensor_copy(out=graw.ap[:, t, :], in_=PSS.ap[:, 8:32]), reads=[PSS.res], writes=[graw.res])
        P.dma("sp", vf.res.name, lambda e, t=t, vf=vf: e.dma_start(out=Sc.VF[t], in_=vf.ap.rearrange("p h d -> p (h d)")), reads=[vf.res])
        P.dma("sp", vsw.res.name, lambda e, t=t, vsw=vsw: e.dma_start(out=Sc.VSW[t], in_=vsw.ap.rearrange("p h d -> p (h d)")), reads=[vsw.res])
        if j != 3:
            continue
        rp = rope[q % 2]
        et = etc[q % 2]
        P.dma("sp", rp.res.name, lambda e, q=q, rp=rp: e.dma_start(out=rp.ap, in_=I.rope.rearrange("p (a t) -> p a t", a=4)[:, :, q * 512:(q + 1) * 512]), writes=[rp.res])
        P.dma("sp", et.res.name, lambda e, q=q, et=et: e.dma_start(out=et.ap, in_=I.etab[:, q * 512:(q + 1) * 512]), writes=[et.res])
        if 'X' not in k.skip:
          P.dma("sp", xc.res.name, lambda e, q=q, xc=xc: e.dma_start(out=Sc.XT[:, :, q * 512:(q + 1) * 512].rearrange("c p t -> p c t"), in_=xc.ap), reads=XTr[q % 2])
        for (nm, idx, cb, ru, scale) in FM_UNITS:
            pu = PSU[ucount % 2]
            f = fm[ucount % 4]
            ucount += 1
            for c in range(8):
                P.op("pe", lambda e, c=c, pu=pu, cb=cb, xc=xc: e.matmul(pu.ap, lhsT=W1.ap[:, c, cb:cb + 128], rhs=xc.ap[:, c, :], start=(c == 0), stop=(c == 7)),
                     reads=XTr[q % 2] + [W1r[c]], writes=[pu.res])
            P.op("act", lambda a, f=f, pu=pu, scale=scale: a.activation(out=f.ap, in_=pu.ap, func=AF.Copy, scale=scale), reads=[pu.res], writes=[f.res])
            if nm == "KSL" and 'E' not in k.skip:
                P.op("act", lambda g, f=f, et=et: g.copy(out=f.ap[64:128], in_=et.ap[64:128]), reads=[et.res], writes=[f.res])
            if ru is not None and 'R' not in k.skip:
                pr = PSR[rcount % 2]
                a1, a2 = rt1[rcount % 2], rt2[rcount % 2]
                rcount += 1
                for c in range(8):
                    P.op("pe", lambda e, c=c, pr=pr, ru=ru, xc=xc: e.matmul(pr.ap[0:32, :], lhsT=Wr.ap[:, c, ru, :], rhs=xc.ap[:, c, :], start=(c == 0), stop=(c == 7)),
                         reads=XTr[q % 2] + [Wrr[c]], writes=[pr.res])
                ti = 0 if scale != 1.0 else 2
                P.op("dve", lambda v, a1=a1, pu=pu, rp=rp, ti=ti: v.tensor_tensor(out=a1.ap[0:16], in0=pu.ap[0:16, :], in1=rp.ap[0:16, ti, :], op=ALU.mult), reads=[pu.res, rp.res], writes=[a1.res])
                P.op("dve", lambda v, a2=a2, pr=pr, rp=rp, ti=ti: v.tensor_tensor(out=a2.ap[0:16], in0=pr.ap[0:16, :], in1=rp.ap[0:16, ti + 1, :], op=ALU.mult), reads=[pr.res, rp.res], writes=[a2.res])
                P.op("dve", lambda g, f=f, a1=a1, a2=a2: g.tensor_tensor(out=f.ap[0:16], in0=a1.ap[0:16], in1=a2.ap[0:16], op=ALU.add), reads=[a1.res, a2.res], writes=[f.res])
            dst = getattr(Sc, nm)
            if 'U' not in k.skip:
              P.dma("sp", f.res.name, lambda e, f=f, dst=dst, idx=idx, q=q: e.dma_start(out=dst[idx, :, q * 512:(q + 1) * 512], in_=f.ap), reads=[f.res])
    P.op("dve", lambda v: v.tensor_tensor(out=zraw.ap, in0=zraw.ap, in1=k.fb.ap.unsqueeze(1).to_broadcast([128, NT, 8]), op=ALU.add), reads=[zraw.res, k.fb.res], writes=[zraw.res])
    P.op("act", lambda a: a.activation(out=zraw.ap, in_=zraw.ap, func=AF.Exp, scale=-1.0), reads=[zraw.res], writes=[zraw.res])
    P.op("act", lambda a: a.activation(out=zraw.ap, in_=zraw.ap, func=AF.Ln, bias=1.0, scale=1.0), reads=[zraw.res], writes=[zraw.res])
    P.op("dve", lambda v: v.tensor_scalar(out=k.logf.ap, in0=zraw.ap, scalar1=-1.0, scalar2=None, op0=ALU.mult), reads=[zraw.res], writes=[k.logf.res])
    P.op("act", lambda a: a.activation(out=k.gate.ap, in_=graw.ap, func=AF.Sigmoid), reads=[graw.res], writes=[k.gate.res])
    P.barrier()
    A.release(m0)


def phase_b(k):
    P, A, I, Sc, PS = k.P, k.A, k.I, k.Sc, k.PS
    m0 = A.mark()
    logf2 = k.logf.ap.rearrange("p a b -> p (a b)")
    tot = A.alloc([NT, 8], F32, "tot")
    pref = A.alloc([NT, 8], F32, "pref")
    negc = A.alloc([NT, 8], F32, "negc")
    bias = A.alloc([8, 8, NT], F32, "bias")
    P.op("pe", lambda e: e.matmul(PS[0].ap[:, 0:256], lhsT=k.trif.ap[:, 1, :], rhs=logf2, start=True, stop=True), reads=[k.trif.res, k.logf.res], writes=[PS[0].res])
    P.op("pe", lambda e: e.matmul(PS[1].ap[:, 0:256], lhsT=k.trif.ap[:, 0, :], rhs=logf2, start=True, stop=True), reads=[k.trif.res, k.logf.res], writes=[PS[1].res])
    P.op("dve", lambda v: v.tensor_copy(out=tot.ap.rearrange("p a b -> p (a b)"), in_=PS[0].ap[:, 0:256]), reads=[PS[0].res], writes=[tot.res])
    P.op("dve", lambda v: v.memset(pref.ap[:, 0, :], 0.0), writes=[pref.res])
    for j in range(1, NT):
        P.op("dve", lambda v, j=j: v.tensor_tensor(out=pref.ap[:, j, :], in0=pref.ap[:, j - 1, :], in1=tot.ap[:, j - 1, :], op=ALU.add), reads=[pref.res, tot.res], writes=[pref.res])
    P.op("dve", lambda v: v.scalar_tensor_tensor(out=negc.ap.rearrange("p a b -> p (a b)"), in0=PS[1].ap[:, 0:256], scalar=-1.0, in1=pref.ap.rearrange("p a b -> p (a b)"), op0=ALU.mult, op1=ALU.subtract),
         reads=[PS[1].res, pref.res], writes=[negc.res])
    for h in range(8):
        for q in range(8):
            P.op("dve", lambda v, h=h, q=q: v.tensor_scalar(out=bias.ap[:, h, q, :], in0=negc.ap[:, :, h], scalar1=pref.ap[:, 4 * q, h:h + 1], scalar2=None, op0=ALU.add),
                 reads=[negc.res, pref.res], writes=[bias.res])
    VF = A.alloc([NT, 520], BF16, "VFall")
    for i in range(4):
        P.dma("sp", "VFall", lambda e, i=i: e.dma_start(out=VF.ap[:, i * 8:(i + 1) * 8, :], in_=Sc.VF[i * 8:(i + 1) * 8].rearrange("t p f -> p t f")), writes=[VF.res])
    QK = [(A.alloc([S], BF16, f"QFh{i}"), A.alloc([S], BF16, f"KFh{i}")) for i in range(2)]
    yf = A.alloc([NT, 512], BF16, "yfox")
    yfr = [Res(f"yf{t}") for t in range(NT)]
    PT = [A.alloc([512], BF16, f"PT{i}") for i in range(4)]
    rz = [A.alloc([4], F32, f"rz{i}") for i in range(2)]
    cnt = 0
    oc = 0
    for h in range(8):
        Qh, Kh = QK[h % 2]
        P.dma("sp", Qh.res.name, lambda e, h=h, Qh=Qh: e.dma_start(out=Qh.ap, in_=Sc.QF[h]), writes=[Qh.res])
        P.dma("sp", Kh.res.name, lambda e, h=h, Kh=Kh: e.dma_start(out=Kh.ap, in_=Sc.KF[h]), writes=[Kh.res])
        for q in range(8):
            OUT = PS[6 + oc % 2]
            rzt = rz[oc % 2]
            oc += 1
            P.op("dve", lambda v, OUT=OUT: v.memset(OUT.ap[:, 0:260], 0.0), writes=[OUT.res])
            def qk_b(kt, ST, Kh=Kh, Qh=Qh, q=q):
                c0 = max(kt - 4 * q, 0) * 128
                P.op("pe", lambda e: e.matmul(ST.ap[:, c0:512], lhsT=Kh.ap[0:64, kt * 128:(kt + 1) * 128], rhs=Qh.ap[0:64, q * 512 + c0:(q + 1) * 512], start=True, stop=True),
                     reads=[Kh.res, Qh.res], writes=[ST.res])

            def rest_b(kt, ST, pt, OUT=OUT, h=h, q=q):
                j = kt - 4 * q
                c0 = max(j, 0) * 128
                P.op("act", lambda a: a.activation(out=pt.ap[:, c0:512], in_=ST.ap[:, c0:512], func=AF.Exp, bias=bias.ap[:, h, q, kt:kt + 1], scale=1.0),
                     reads=[ST.res, bias.res], writes=[pt.res])
                if j >= 0:
                    P.op("pool", lambda g: g.tensor_tensor(out=pt.ap[:, c0:c0 + 128], in0=pt.ap[:, c0:c0 + 128], in1=k.trib.ap[:, 0, :], op=ALU.mult),
                         reads=[pt.res, k.trib.res], writes=[pt.res])
                for ql in range(max(j, 0), 4):
                    P.op("pe", lambda e, ql=ql: e.matmul(OUT.ap[:, ql * 65:(ql + 1) * 65], lhsT=pt.ap[:, ql * 128:(ql + 1) * 128], rhs=VF.ap[:, kt, h * 65:(h + 1) * 65], start=False, stop=False, skip_group_check=True),
                         reads=[pt.res, VF.res], writes=[OUT.res])

            nk = 4 * q + 4
            slots = [(PS[(cnt + i) % 4], PT[(cnt + i) % 4]) for i in range(nk)]
            cnt += nk
            qk_b(0, slots[0][0])
            for kt in range(nk):
                if kt + 1 < nk:
                    qk_b(kt + 1, slots[kt + 1][0])
                rest_b(kt, slots[kt][0], slots[kt][1])
            P.op("dve", lambda v, OUT=OUT, rzt=rzt: v.reciprocal(out=rzt.ap, in_=OUT.ap[:, 0:260].rearrange("p (a b) -> p a b", b=65)[:, :, 64]), reads=[OUT.res], writes=[rzt.res])
            for ql in range(4):
                t = 4 * q + ql
                P.op("dve", lambda v, OUT=OUT, rzt=rzt, ql=ql, t=t, h=h: v.tensor_scalar(out=yf.ap[:, t, h * 64:(h + 1) * 64], in0=OUT.ap[:, ql * 65:ql * 65 + 64], scalar1=rzt.ap[:, ql:ql + 1], scalar2=None, op0=ALU.mult),
                     reads=[OUT.res, rzt.res], writes=[yfr[t]])
    for i in range(4):
        P.dma("sp", "yfox", lambda e, i=i: e.dma_start(out=Sc.YF[i * 8:(i + 1) * 8].rearrange("t p f -> p t f"), in_=yf.ap[:, i * 8:(i + 1) * 8, :]), reads=yfr[i * 8:(i + 1) * 8])
    P.barrier()
    A.release(m0)


def phase_c(k):
    P, A, I, Sc, PS = k.P, k.A, k.I, k.Sc, k.PS
    m0 = A.mark()
    kcmpT = [A.alloc([256], BF16, f"kcmpT{g}") for g in range(2)]
    VCa = A.alloc([2, 2, 129], BF16, "VCa")
    cmpmask = A.alloc([2, S], BF16, "cmpmask")
    selbias = A.alloc([NT, 64], F32, "selbias")
    ovaug = A.alloc([2, 65], BF16, "ovaug")
    VSW = A.alloc([NT, 260], BF16, "VSWall")
    P.dma("sp", "cC", lambda e: e.dma_start(out=cmpmask.ap, in_=I.cmpmask.rearrange("p (a t) -> p a t", a=2)), writes=[cmpmask.res])
    P.dma("sp", "cC", lambda e: e.dma_start(out=selbias.ap, in_=I.selbias.rearrange("p (a t) -> p a t", a=NT)), writes=[selbias.res])
    P.dma("sp", "cC", lambda e: e.dma_start(out=ovaug.ap, in_=I.ovaug.rearrange("p (a t) -> p a t", a=2)), writes=[ovaug.res])
    for i in range(4):
        P.dma("sp", "cC", lambda e, i=i: e.dma_start(out=VSW.ap[:, i * 8:(i + 1) * 8, :], in_=Sc.VSW[i * 8:(i + 1) * 8].rearrange("t p f -> p t f")), writes=[VSW.res])
    m1 = A.mark()
    w1s = A.alloc([2, 32, 128], F32, "w1s")
    w1b = A.alloc([2, 32, 128], BF16, "w1b")
    poss = A.alloc([2, 32], F32, "poss")
    posb = A.alloc([2, 32], BF16, "posb")
    w2s = A.alloc([2, 64], F32, "w2s")
    w2b = A.alloc([2, 64], BF16, "w2b")
    w2r = A.alloc([16], BF16, "w2r")
    ropec = A.alloc([2, 256], F32, "ropec")
    cst = A.alloc([2], F32, "cst")
    SRC = [[A.alloc([S], BF16, f"src{j}{g}") for g in range(2)] for j in range(2)]
    hidT = [A.alloc([256], BF16, f"hidT{i}") for i in range(2)]
    ct1 = A.alloc([256], F32, "ct1")
    ct2 = A.alloc([256], F32, "ct2")
    P.dma("sp", "cC", lambda e: e.dma_start(out=w1s.ap, in_=I.w1.rearrange("p (a b c) -> p a b c", a=2, b=32)), writes=[w1s.res])
    P.dma("sp", "cC", lambda e: e.dma_start(out=poss.ap, in_=I.pos.rearrange("p (a b) -> p a b", a=2)), writes=[poss.res])
    P.dma("sp", "cC", lambda e: e.dma_start(out=w2s.ap, in_=I.w2.rearrange("p (a b) -> p a b", a=2)), writes=[w2s.res])
    P.dma("sp", "cC", lambda e: e.dma_start(out=ropec.ap, in_=I.ropec.rearrange("p (a b) -> p a b", a=2)), writes=[ropec.res])
    for j in range(2):
        for g in range(2):
            src = (Sc.KC, Sc.VC)[j]
            P.dma("sp", "cC", lambda e, j=j, g=g, src=src: e.dma_start(out=SRC[j][g].ap, in_=src[g]), writes=[SRC[j][g].res])
    P.barrier()
    P.op("dve", lambda v: v.tensor_copy(out=w1b.ap, in_=w1s.ap), reads=[w1s.res], writes=[w1b.res])
    P.op("dve", lambda v: v.tensor_copy(out=posb.ap, in_=poss.ap), reads=[poss.res], writes=[posb.res])
    P.op("dve", lambda v: v.tensor_copy(out=w2b.ap, in_=w2s.ap), reads=[w2s.res], writes=[w2b.res])
    P.op("dve", lambda v: v.tensor_scalar(out=w2r.ap[:, 0:8], in0=w2s.ap[:, 0, 8:16], scalar1=-1.0, scalar2=None, op0=ALU.mult), reads=[w2s.res], writes=[w2r.res])
    P.op("dve", lambda v: v.tensor_copy(out=w2r.ap[:, 8:16], in_=w2s.ap[:, 0, 0:8]), reads=[w2s.res], writes=[w2r.res])
    P.op("pool", lambda g_: g_.memset(VCa.ap, 0.0), writes=[VCa.res])
    for g in range(2):
        P.op("pool", lambda g_, g=g: g_.memset(kcmpT[g].ap, 0.0), writes=[kcmpT[g].res])
    for i in range(2):
        P.op("pool", lambda g_, i=i: g_.memset(hidT[i].ap, 0.0), writes=[hidT[i].res])
    for j in range(2):
        for l in range(32):
            P.op("pe", lambda e, j=j, l=l: e.matmul(PS[7].ap[:, j:j + 1], lhsT=w1b.ap[0:64, j, l, :], rhs=posb.ap[0:64, j, l:l + 1], start=(l == 0), stop=(l == 31)),
                 reads=[w1b.res, posb.res], writes=[PS[7].res])
    P.op("dve", lambda v: v.tensor_copy(out=cst.ap, in_=PS[7].ap[:, 0:2]), reads=[PS[7].res], writes=[cst.res])
    cc = 0
    for j in range(2):
        for g in range(2):
            hp = PS[cc % 2]
            hT = hidT[cc % 2]
            cc += 1
            sv = SRC[j][g].ap[0:64, :].rearrange("p (n s) -> p n s", s=16)
            for l in range(32):
                rhs = sv[:, 0:255, l] if l < 16 else sv[:, 1:256, l - 16]
                P.op("pe", lambda e, hp=hp, j=j, l=l, rhs=rhs: e.matmul(hp.ap[:, 0:255], lhsT=w1b.ap[0:64, j, l, :], rhs=rhs, start=(l == 0), stop=(l == 31)),
                     reads=[w1b.res, SRC[j][g].res], writes=[hp.res])
            P.op("act", lambda a, hp=hp, hT=hT, j=j: a.activation(out=hT.ap[:, 0:255], in_=hp.ap[:, 0:255], func=AF.Gelu, bias=cst.ap[:, j:j + 1], scale=1.0),
                 reads=[hp.res, cst.res], writes=[hT.res])
            if j == 0:
                P.op("pe", lambda e, hT=hT: e.matmul(PS[2].ap[0:64, 0:255], lhsT=w2b.ap[:, 0, :], rhs=hT.ap[:, 0:255], start=True, stop=True), reads=[w2b.res, hT.res], writes=[PS[2].res])
                P.op("pe", lambda e, hT=hT: e.matmul(PS[3].ap[0:16, 0:255], lhsT=w2r.ap, rhs=hT.ap[:, 0:255], start=True, stop=True), reads=[w2r.res, hT.res], writes=[PS[3].res])
                P.op("act", lambda a, g=g: a.copy(out=kcmpT[g].ap[0:64, 0:255], in_=PS[2].ap[0:64, 0:255]), reads=[PS[2].res], writes=[kcmpT[g].res])
                P.op("dve", lambda v: v.tensor_tensor(out=ct1.ap[0:16, 0:255], in0=PS[2].ap[0:16, 0:255], in1=ropec.ap[0:16, 0, 0:255], op=ALU.mult), reads=[PS[2].res, ropec.res], writes=[ct1.res])
                P.op("dve", lambda v: v.tensor_tensor(out=ct2.ap[0:16, 0:255], in0=PS[3].ap[0:16, 0:255], in1=ropec.ap[0:16, 1, 0:255], op=ALU.mult), reads=[PS[3].res, ropec.res], writes=[ct2.res])
                P.op("dve", lambda v, g=g: v.tensor_tensor(out=kcmpT[g].ap[0:16, 0:255], in0=ct1.ap[0:16, 0:255], in1=ct2.ap[0:16, 0:255], op=ALU.add), reads=[ct1.res, ct2.res], writes=[kcmpT[g].res])
            else:
                for i in range(2):
                    nn = 128 if i == 0 else 127
                    P.op("pe", lambda e, hT=hT, i=i, nn=nn: e.matmul(PS[2 + i].ap[0:nn, 0:64], lhsT=hT.ap[:, i * 128:i * 128 + nn], rhs=w2b.ap[:, 1, :], start=True, stop=True),
                         reads=[w2b.res, hT.res], writes=[PS[2 + i].res])
                    P.op("act", lambda a, g=g, i=i, nn=nn: a.copy(out=VCa.ap[0:nn, i, g, 0:64], in_=PS[2 + i].ap[0:nn, 0:64]), reads=[PS[2 + i].res], writes=[VCa.res])
                    P.op("dve", lambda v, g=g, i=i: v.tensor_copy(out=VCa.ap[:, i, g, 64:129], in_=ovaug.ap[:, i, :]), reads=[ovaug.res], writes=[VCa.res])
    P.barrier()
    A.release(m1)
    if k.dbgC is not None:
        for g in range(2):
            P.dma("sp", "dbg", lambda e, g=g: e.dma_start(out=k.dbgC[0][:, g * 256:(g + 1) * 256], in_=kcmpT[g].ap), reads=[kcmpT[g].res])
        P.dma("sp", "dbg", lambda e: e.dma_start(out=k.dbgC[1], in_=VCa.ap.rearrange("p a b c -> p (a b c)")), reads=[VCa.res])
    KE = A.alloc([S], BF16, "KE")
    KW = A.alloc([S], BF16, "KW")
    QN = [[A.alloc([512], BF16, f"QN{i}_{hl}") for hl in range(4)] for i in range(2)]
    _DBG['QN'] = QN
    imp = A.alloc([4, 64], F32, "imp")
    yacc = [A.alloc([4, 256], F32, f"yacc{i}") for i in range(2)]
    ystg = [A.alloc([4, 256], BF16, f"ystg{i}") for i in range(2)]
    PT = [A.alloc([512], BF16, f"PTc{i}") for i in range(4)]
    rz = [A.alloc([4], F32, f"rzc{i}") for i in range(2)]
    cmb = [A.alloc([4], F32, f"cmb{i}") for i in range(2)]
    sc = A.alloc([64], F32, "sc")
    wk = A.alloc([64], F32, "wk")
    m8 = A.alloc([16], F32, "m8")
    pen2 = [A.alloc([2, 64], F32, f"pen2_{i}") for i in range(2)]
    STB = [PS[0], PS[1], PS[2]]
    OUTB = [PS[3], PS[4]]
    OUT2 = PS[5]
    PSM = PS[6]
    st_c = 0
    oc = 0

    def evac(OUT, t0, h, br, ya, hl, first, OUT2=None):
        nonlocal oc
        rzt, cb = rz[oc % 2], cmb[oc % 2]
        zv = OUT.ap[:, 0:260].rearrange("p (a b) -> p a b", b=65)[:, :, 64]
        if br == 0:
            P.op("dve", lambda v: v.tensor_scalar(out=rzt.ap, in0=zv, scalar1=1e-30, scalar2=None, op0=ALU.max), reads=[OUT.res], writes=[rzt.res])
            P.op("dve", lambda v: v.reciprocal(out=rzt.ap, in_=rzt.ap), reads=[rzt.res], writes=[rzt.res])
        else:
            P.op("dve", lambda v: v.reciprocal(out=rzt.ap, in_=zv), reads=[OUT.res], writes=[rzt.res])
        P.op("dve", lambda v: v.tensor_tensor(out=cb.ap, in0=rzt.ap, in1=k.gate.ap[:, t0:t0 + 4, h * 3 + br], op=ALU.mult), reads=[rzt.res, k.gate.res], writes=[cb.res])
        for ql in range(4):
            if first:
                P.op("dve", lambda v, ql=ql: v.tensor_scalar(out=ya.ap[:, ql, hl * 64:(hl + 1) * 64], in0=OUT.ap[:, ql * 65:ql * 65 + 64], scalar1=cb.ap[:, ql:ql + 1], scalar2=None, op0=ALU.mult),
                     reads=[OUT.res, cb.res], writes=[ya.res])
            else:
                P.op("dve", lambda v, ql=ql: v.scalar_tensor_tensor(out=ya.ap[:, ql, hl * 64:(hl + 1) * 64], in0=OUT.ap[:, ql * 65:ql * 65 + 64], scalar=cb.ap[:, ql:ql + 1], in1=ya.ap[:, ql, hl * 64:(hl + 1) * 64], op0=ALU.mult, op1=ALU.add),
                     reads=[OUT.res, cb.res, ya.res], writes=[ya.res])
            if OUT2 is not None:
                P.op("dve", lambda v, ql=ql: v.scalar_tensor_tensor(out=imp.ap[:, ql, :], in0=OUT2.ap[:, ql * 64:(ql + 1) * 64], scalar=rzt.ap[:, ql:ql + 1], in1=imp.ap[:, ql, :], op0=ALU.mult, op1=ALU.add),
                     reads=[OUT2.res, rzt.res, imp.res], writes=[imp.res])

    def pv(OUT, pt, kt, voff, qls, OUT2=None, vt=None):
        for ql in qls:
            if vt is None:
                rhs = VSW.ap[:, kt, voff:voff + 65]
                rr = VSW.res
            else:
                rhs = vt[0]
                rr = VCa.res
            P.op("pe", lambda e, ql=ql, rhs=rhs: e.matmul(OUT.ap[:, ql * 65:(ql + 1) * 65], lhsT=pt.ap[:, ql * 128:(ql + 1) * 128], rhs=rhs, start=False, stop=False, skip_group_check=True),
                 reads=[pt.res, rr], writes=[OUT.res])
            if OUT2 is not None:
                P.op("pe", lambda e, ql=ql: e.matmul(OUT2.ap[:, ql * 64:(ql + 1) * 64], lhsT=pt.ap[:, ql * 128:(ql + 1) * 128], rhs=vt[1], start=False, stop=False, skip_group_check=True),
                     reads=[pt.res, VCa.res], writes=[OUT2.res])

    def do_chunk(g, q):
        nonlocal st_c, oc
        if True:
            t0 = 4 * q
            qn = QN[q % 2]
            ya = yacc[q % 2]
            for hl in range(4):
                P.dma("sp", qn[hl].res.name, lambda e, hl=hl, g=g, q=q, qn=qn: e.dma_start(out=qn[hl].ap, in_=Sc.QN[4 * g + hl][:, q * 512:(q + 1) * 512]), writes=[qn[hl].res])
            P.op("pool", lambda g_: g_.memset(imp.ap, 0.0), writes=[imp.res])
            for hl in range(4):
                h = 4 * g + hl
                OUT = OUTB[oc % 2]
                P.op("dve", lambda v, OUT=OUT: v.memset(OUT.ap[:, 0:260], 0.0), writes=[OUT.res])
                P.op("dve", lambda v: v.memset(OUT2.ap[:, 0:256], 0.0), writes=[OUT2.res])
                for i in range(2 if q >= 4 else 1):
                    ST = STB[st_c % 3]
                    pt = PT[st_c % 4]
                    st_c += 1
                    P.op("pe", lambda e, ST=ST, i=i, hl=hl: e.matmul(ST.ap, lhsT=kcmpT[g].ap[0:64, i * 128:(i + 1) * 128], rhs=qn[hl].ap[0:64, :], start=True, stop=True),
                         reads=[kcmpT[g].res, qn[hl].res], writes=[ST.res])
                    P.op("act", lambda a, ST=ST, pt=pt: a.activation(out=pt.ap, in_=ST.ap, func=AF.Exp), reads=[ST.res], writes=[pt.res])
                    P.op("pool", lambda g_, pt=pt, i=i, q=q: g_.tensor_tensor(out=pt.ap, in0=pt.ap, in1=cmpmask.ap[:, i, q * 512:(q + 1) * 512], op=ALU.mult), reads=[pt.res, cmpmask.res], writes=[pt.res])
                    pv(OUT, pt, None, None, range(4), OUT2=OUT2, vt=(VCa.ap[:, i, g, 0:65], VCa.ap[:, i, g, 65:129]))
                evac(OUT, t0, h, 0, ya, hl, True, OUT2=OUT2)
                oc += 1
            for ql in range(4):
                t = t0 + ql
                p2 = pen2[ql % 2]
                P.op("dve", lambda v, ql=ql, t=t: v.tensor_tensor(out=sc.ap, in0=imp.ap[:, ql, :], in1=selbias.ap[:, t, :], op=ALU.add), reads=[imp.res, selbias.res], writes=[sc.res])
                P.op("dve", lambda v: v.max(out=m8.ap[:, 0:8], in_=sc.ap), reads=[sc.res], writes=[m8.res])
                P.op("dve", lambda v: v.match_replace(out=wk.ap, in_to_replace=m8.ap[:, 0:8], in_values=sc.ap, imm_value=-1e30), reads=[sc.res, m8.res], writes=[wk.res])
                P.op("dve", lambda v: v.max(out=m8.ap[:, 8:16], in_=wk.ap), reads=[wk.res], writes=[m8.res])
                P.op("dve", lambda v, p2=p2: v.tensor_scalar(out=p2.ap, in0=sc.ap.unsqueeze(1).to_broadcast([128, 2, 64]), scalar1=m8.ap[:, 15:16], scalar2=NEG, op0=ALU.is_lt, op1=ALU.mult),
                     reads=[sc.res, m8.res], writes=[p2.res])
                P.op("pe", lambda e, p2=p2, ql=ql: e.transpose(out=PSM.ap[:, ql * 128:(ql + 1) * 128], in_=p2.ap.rearrange("p a b -> p (a b)"), identity=k.identf.ap),
                     reads=[p2.res, k.identf.res], writes=[PSM.res])
            for hl in range(4):
                if hl % 2 == 0:
                    P.op("act", lambda a, hl=hl: a.copy(out=qn[hl].ap[64:128, :], in_=PSM.ap[64:128, :]), reads=[PSM.res], writes=[qn[hl].res])
                else:
                    P.op("dve", lambda v, hl=hl: v.tensor_copy(out=qn[hl].ap[64:128, :], in_=PSM.ap[64:128, :]), reads=[PSM.res], writes=[qn[hl].res])
            for hl in range(4):
                h = 4 * g + hl
                OUT = OUTB[oc % 2]
                P.op("dve", lambda v, OUT=OUT: v.memset(OUT.ap[:, 0:260], 0.0), writes=[OUT.res])
                def qk_s(kt, ST, hl=hl):
                    c0 = max(kt - 4 * q, 0) * 128
                    P.op("pe", lambda e: e.matmul(ST.ap[:, c0:512], lhsT=KE.ap[:, kt * 128:(kt + 1) * 128], rhs=qn[hl].ap[:, c0:512], start=True, stop=True),
                         reads=[KE.res, qn[hl].res], writes=[ST.res])

                def rest_s(kt, ST, pt, OUT=OUT):
                    j = kt - 4 * q
                    c0 = max(j, 0) * 128
                    P.op("act", lambda a: a.activation(out=pt.ap[:, c0:512], in_=ST.ap[:, c0:512], func=AF.Exp), reads=[ST.res], writes=[pt.res])
                    if j >= 0:
                        P.op("pool", lambda g_: g_.tensor_tensor(out=pt.ap[:, c0:c0 + 128], in0=pt.ap[:, c0:c0 + 128], in1=k.trib.ap[:, 0, :], op=ALU.mult), reads=[pt.res, k.trib.res], writes=[pt.res])
                    pv(OUT, pt, kt, g * 65, range(max(j, 0), 4))

                nk = 4 * q + 4
                sl = [(STB[(st_c + i) % 3], PT[(st_c + i) % 4]) for i in range(nk)]
                st_c += nk
                qk_s(0, sl[0][0])
                for kt in range(nk):
                    if kt + 1 < nk:
                        qk_s(kt + 1, sl[kt + 1][0])
                    rest_s(kt, sl[kt][0], sl[kt][1])
                evac(OUT, t0, h, 1, ya, hl, False)
                oc += 1
            for hl in range(4):
                h = 4 * g + hl
                OUT = OUTB[oc % 2]
                P.op("dve", lambda v, OUT=OUT: v.memset(OUT.ap[:, 0:260], 0.0), writes=[OUT.res])
                def rng_w(kt):
                    lo = max(kt - 4 * q, 0)
                    hi = min(kt + 4 - 4 * q, 3)
                    return lo, hi, lo * 128, (hi + 1) * 128

                def qk_w(kt, ST, hl=hl):
                    lo, hi, c0, c1 = rng_w(kt)
                    P.op("pe", lambda e: e.matmul(ST.ap[:, c0:c1], lhsT=KW.ap[0:64, kt * 128:(kt + 1) * 128], rhs=qn[hl].ap[0:64, c0:c1], start=True, stop=True),
                         reads=[KW.res, qn[hl].res], writes=[ST.res])

                def rest_w(kt, ST, pt, OUT=OUT):
                    lo, hi, c0, c1 = rng_w(kt)
                    P.op("act", lambda a: a.activation(out=pt.ap[:, c0:c1], in_=ST.ap[:, c0:c1], func=AF.Exp), reads=[ST.res], writes=[pt.res])
                    if kt >= 4 * q:
                        b0 = (kt - 4 * q) * 128
                        P.op("pool", lambda g_: g_.tensor_tensor(out=pt.ap[:, b0:b0 + 128], in0=pt.ap[:, b0:b0 + 128], in1=k.trib.ap[:, 0, :], op=ALU.mult), reads=[pt.res, k.trib.res], writes=[pt.res])
                    if 0 <= kt + 4 - 4 * q <= 3:
                        b1 = (kt + 4 - 4 * q) * 128
                        P.op("pool", lambda g_: g_.tensor_tensor(out=pt.ap[:, b1:b1 + 128], in0=pt.ap[:, b1:b1 + 128], in1=k.trib.ap[:, 1, :], op=ALU.mult), reads=[pt.res, k.trib.res], writes=[pt.res])
                    pv(OUT, pt, kt, (2 + g) * 65, range(lo, hi + 1))

                kts = list(range(max(4 * q - 4, 0), 4 * q + 4))
                sl = [(STB[(st_c + i) % 3], PT[(st_c + i) % 4]) for i in range(len(kts))]
                st_c += len(kts)
                qk_w(kts[0], sl[0][0])
                for i, kt in enumerate(kts):
                    if i + 1 < len(kts):
                        qk_w(kts[i + 1], sl[i + 1][0])
                    rest_w(kt, sl[i][0], sl[i][1])
                evac(OUT, t0, h, 2, ya, hl, False)
                oc += 1
            ys = ystg[q % 2]
            P.op("act", lambda a, ys=ys, ya=ya: a.copy(out=ys.ap, in_=ya.ap), reads=[ya.res], writes=[ys.res])
            P.dma("sp", ys.res.name, lambda e, ys=ys, g=g, t0=t0: e.dma_start(out=Sc.YN[t0:t0 + 4, :, g * 256:(g + 1) * 256].rearrange("t p f -> p t f"), in_=ys.ap), reads=[ys.res])

    for g in range(2):
        P.dma("sp", "KE", lambda e, g=g: e.dma_start(out=KE.ap, in_=Sc.KSL[g]), writes=[KE.res])
        P.dma("sp", "KW", lambda e, g=g: e.dma_start(out=KW.ap, in_=Sc.KWN[g]), writes=[KW.res])
        for q in range(8):
            do_chunk(g, q)
    P.barrier()
    A.release(m0)


def phase_d(k):
    P, A, I, Sc, PS = k.P, k.A, k.I, k.Sc, k.PS
    m0 = A.mark()
    WG = A.alloc([8, 2048], BF16, "WG")
    Wb = A.alloc([2, 4, 1024], BF16, "Wb")
    Wo = A.alloc([8, 1024], BF16, "Wo")
    gmix = A.alloc([8], F32, "gmixd")
    wst = [A.alloc([2048], F32, f"wstd{i}") for i in range(2)]
    WGr = [Res(f"WG{c}") for c in range(8)]
    Wbr = [Res(f"Wb{c}") for c in range(8)]
    Wor = [Res(f"Wo{c}") for c in range(8)]
    P.dma("sp", "cD", lambda e: e.dma_start(out=gmix.ap, in_=I.gmix), writes=[gmix.res])
    P.barrier()
    n = 0
    for c in range(8):
        ws = wst[n % 2]; n += 1
        P.dma("sp", ws.res.name, lambda e, c=c, ws=ws: e.dma_start(out=ws.ap, in_=I.w_in[c * 128:(c + 1) * 128, C_MG:C_MG + 2048]), writes=[ws.res])
        P.op("dve", lambda v, c=c, ws=ws: v.tensor_scalar(out=WG.ap[:, c, :], in0=ws.ap, scalar1=gmix.ap[:, c:c + 1], scalar2=None, op0=ALU.mult), reads=[ws.res, gmix.res], writes=[WGr[c]])
    for b in range(2):
        for c in range(4):
            ws = wst[n % 2]; n += 1
            P.dma("sp", ws.res.name, lambda e, b=b, c=c, ws=ws: e.dma_start(out=ws.ap[:, 0:1024], in_=I.wbr[b, c * 128:(c + 1) * 128, :]), writes=[ws.res])
            P.op("dve", lambda v, b=b, c=c, ws=ws: v.tensor_copy(out=Wb.ap[:, b, c, :], in_=ws.ap[:, 0:1024]), reads=[ws.res], writes=[Wbr[b * 4 + c]])
    for c in range(8):
        ws = wst[n % 2]; n += 1
        P.dma("sp", ws.res.name, lambda e, c=c, ws=ws: e.dma_start(out=ws.ap[:, 0:1024], in_=I.wout[c * 128:(c + 1) * 128, :]), writes=[ws.res])
        P.op("dve", lambda v, c=c, ws=ws: v.tensor_copy(out=Wo.ap[:, c, :], in_=ws.ap[:, 0:1024]), reads=[ws.res], writes=[Wor[c]])
    xs = [A.alloc([D], F32, f"xd{i}") for i in range(2)]
    XTt = [A.alloc([8, 128], BF16, f"XTt{i}") for i in range(2)]
    yfn = [A.alloc([2, 512], BF16, f"yfn{i}") for i in range(2)]
    yT = A.alloc([8, 128], BF16, "yT")
    sg = [A.alloc([512], F32, f"sg{i}") for i in range(2)]
    tt = [A.alloc([512], F32, f"tt{i}") for i in range(2)]
    mg = A.alloc([D], BF16, "mg")
    mT = A.alloc([8, 128], BF16, "mT")
    h2 = [A.alloc([D], F32, f"h2_{i}") for i in range(2)]
    PTR = [PS[0], PS[1]]
    PU = [PS[2], PS[3]]
    PG = [PS[4], PS[5]]
    PO = [PS[6], PS[7]]

    def do_tile(t):
        x_t, xt_t, y_t, h_t = xs[t % 2], XTt[t % 2], yfn[t % 2], h2[t % 2]
        P.dma("sp", x_t.res.name, lambda e: e.dma_start(out=x_t.ap, in_=I.x[t * 128:(t + 1) * 128, :]), writes=[x_t.res])
        P.dma("sp", xt_t.res.name, lambda e: e.dma_start(out=xt_t.ap, in_=Sc.XT[:, :, t * 128:(t + 1) * 128].rearrange("c p t -> p c t")), writes=[xt_t.res])
        P.dma("sp", y_t.res.name, lambda e: e.dma_start(out=y_t.ap[:, 0, :], in_=Sc.YF[t]), writes=[y_t.res])
        P.dma("sp", y_t.res.name, lambda e: e.dma_start(out=y_t.ap[:, 1, :], in_=Sc.YN[t]), writes=[y_t.res])
        ptb = PTR[0].ap.bitcast(BF16)
        for b in range(2):
            for c in range(4):
                P.op("pe", lambda e, b=b, c=c: e.transpose(out=ptb[:, (b * 4 + c) * 128:(b * 4 + c + 1) * 128], in_=y_t.ap[:, b, c * 128:(c + 1) * 128], identity=k.identb.ap),
                     reads=[y_t.res, k.identb.res], writes=[PTR[0].res])
        P.op("act", lambda a: a.copy(out=yT.ap, in_=ptb.rearrange("p (c t) -> p c t", c=8)), reads=[PTR[0].res], writes=[yT.res])
        for half in range(2):
            for b in range(2):
                for c in range(4):
                    P.op("pe", lambda e, b=b, c=c, half=half: e.matmul(PU[b].ap, lhsT=yT.ap[:, b * 4 + c, :], rhs=Wb.ap[:, b, c, half * 512:(half + 1) * 512], start=(c == 0), stop=(c == 3)),
                         reads=[yT.res, Wbr[b * 4 + c]], writes=[PU[b].res])
                for c in range(8):
                    P.op("pe", lambda e, b=b, c=c, half=half: e.matmul(PG[b].ap, lhsT=xt_t.ap[:, c, :], rhs=WG.ap[:, c, b * 1024 + half * 512:b * 1024 + (half + 1) * 512], start=(c == 0), stop=(c == 7)),
                         reads=[xt_t.res, WGr[c]], writes=[PG[b].res])
                P.op("act", lambda a, b=b: a.activation(out=sg[b].ap, in_=PG[b].ap, func=AF.Sigmoid), reads=[PG[b].res], writes=[sg[b].res])
                P.op("dve", lambda v, b=b: v.tensor_tensor(out=tt[b].ap, in0=PU[b].ap, in1=sg[b].ap, op=ALU.mult), reads=[PU[b].res, sg[b].res], writes=[tt[b].res])
            P.op("dve", lambda v, half=half: v.tensor_tensor(out=mg.ap[:, half * 512:(half + 1) * 512], in0=tt[0].ap, in1=tt[1].ap, op=ALU.add), reads=[tt[0].res, tt[1].res], writes=[mg.res])
        ptb1 = PTR[1].ap.bitcast(BF16)
        for c in range(8):
            P.op("pe", lambda e, c=c: e.transpose(out=ptb1[:, c * 128:(c + 1) * 128], in_=mg.ap[:, c * 128:(c + 1) * 128], identity=k.identb.ap), reads=[mg.res, k.identb.res], writes=[PTR[1].res])
        P.op("act", lambda a: a.copy(out=mT.ap, in_=ptb1.rearrange("p (c t) -> p c t", c=8)), reads=[PTR[1].res], writes=[mT.res])
        for half in range(2):
            for c in range(8):
                P.op("pe", lambda e, c=c, half=half: e.matmul(PO[half].ap, lhsT=mT.ap[:, c, :], rhs=Wo.ap[:, c, half * 512:(half + 1) * 512], start=(c == 0), stop=(c == 7)),
                     reads=[mT.res, Wor[c]], writes=[PO[half].res])
            P.op("dve", lambda v, half=half: v.tensor_tensor(out=h_t.ap[:, half * 512:(half + 1) * 512], in0=PO[half].ap, in1=x_t.ap[:, half * 512:(half + 1) * 512], op=ALU.add), reads=[PO[half].res, x_t.res], writes=[h_t.res])
        P.dma("sp", h_t.res.name, lambda e: e.dma_start(out=Sc.H2[t * 128:(t + 1) * 128, :], in_=h_t.ap), reads=[h_t.res])

    for t in range(NT):
        do_tile(t)
    P.barrier()
    A.release(m0)


NB_G = 16


def phase_e(k):
    P, A, I, Sc, PS = k.P, k.A, k.I, k.Sc, k.PS
    out = k.out
    m0 = A.mark()
    Wq = A.alloc([8, 2048], BF16, "Wq")
    Wqr = [Res(f"Wq{c}") for c in range(8)]
    skb = A.alloc([16, 128], BF16, "skb")
    gq = A.alloc([8], F32, "gq")
    gfb = A.alloc([D], F32, "gfb")
    gfin = A.alloc([D], F32, "gfin")
    iota = A.alloc([16], F32, "iota")
    wst = [A.alloc([2048], F32, f"wste{i}") for i in range(2)]
    P.dma("sp", "cE", lambda e: e.dma_start(out=gq.ap, in_=I.gffn8), writes=[gq.res])
    P.dma("sp", "cE", lambda e: e.dma_start(out=gfb.ap, in_=I.gffn[0:1, :].partition_broadcast(128)), writes=[gfb.res])
    P.dma("sp", "cE", lambda e: e.dma_start(out=gfin.ap, in_=I.gfin[0:1, :].partition_broadcast(128)), writes=[gfin.res])
    P.dma("sp", "cE", lambda e: e.dma_start(out=iota.ap, in_=I.iota16), writes=[iota.res])
    P.barrier()
    for c in range(8):
        ws = wst[c % 2]
        P.dma("sp", ws.res.name, lambda e, c=c, ws=ws: e.dma_start(out=ws.ap, in_=I.wq[c * 128:(c + 1) * 128, :]), writes=[ws.res])
        P.op("dve", lambda v, c=c, ws=ws: v.tensor_scalar(out=Wq.ap[:, c, :], in0=ws.ap, scalar1=gq.ap[:, c:c + 1], scalar2=None, op0=ALU.mult), reads=[ws.res, gq.res], writes=[Wqr[c]])
    P.dma("sp", wst[0].res.name, lambda e: e.dma_start(out=wst[0].ap, in_=I.skT), writes=[wst[0].res])
    P.op("dve", lambda v: v.tensor_copy(out=skb.ap.rearrange("p a b -> p (a b)"), in_=wst[0].ap), reads=[wst[0].res], writes=[skb.res])

    h2 = [A.alloc([D], F32, f"h2e{i}") for i in range(2)]
    junkb = A.alloc([D], BF16, "junkb")
    ssq = [A.alloc([4], F32, f"ssq{i}") for i in range(2)]
    xn2 = [A.alloc([D], F32, f"xn2_{i}") for i in range(2)]
    xnb = A.alloc([D], BF16, "xnb")
    x2T = A.alloc([8, 128], BF16, "x2T")
    qT = A.alloc([16, 128], BF16, "qT")
    sS = A.alloc([16, 128], F32, "sS")
    wk = A.alloc([128], F32, "wk_e")
    vals = A.alloc([8, 2, 16], F32, "vals")
    idxu = A.alloc([8, 2, 16], U32, "idxu")
    cand = A.alloc([16, 16], F32, "cand")
    wk2 = A.alloc([256], F32, "wk2")
    tops = A.alloc([8, 16], F32, "tops")
    posu = A.alloc([8, 16], U32, "posu")
    posf = A.alloc([8, 16], F32, "posf")
    aq = A.alloc([8, 16], F32, "aq")
    bq = A.alloc([8, 16], F32, "bq")
    i1f = A.alloc([8, 16], F32, "i1f")
    i2f = A.alloc([8, 16], F32, "i2f")
    oh = A.alloc([16, 16], F32, "oh")
    idx1 = A.alloc([8, 16], F32, "idx1")
    idx2 = A.alloc([8, 16], F32, "idx2")
    idxf = A.alloc([128], F32, "idxf")
    idxi = [A.alloc([128], U32, f"idxi{i}") for i in range(2)]
    negm = A.alloc([8], F32, "negm")
    ex = A.alloc([8, 16], F32, "ex")
    sm = A.alloc([8], F32, "sm")
    wgt = [A.alloc([8, 16], F32, f"wgt{i}") for i in range(2)]
    act = A.alloc([128], F32, "act")
    actr = [Res(f"act{i}") for i in range(128)]
    coef = A.alloc([128], F32, "coef")
    GB = [A.alloc([2 * D], BF16, f"gb{i}") for i in range(NB_G)]
    junkd = A.alloc([D], BF16, "junkd")
    cg = A.alloc([128], F32, "cg")
    cgr = [Res(f"cg{i}") for i in range(32)]
    coefr = [Res(f"coef{i}") for i in range(32)]
    DG = [A.alloc([128], BF16, f"dg{i}") for i in range(4)]
    junkf = A.alloc([D], F32, "junkf")
    hsum = A.alloc([D], F32, "hsum")
    ot = [A.alloc([D], F32, f"ot{i}") for i in range(2)]
    PTB = PS[0]
    PQ = [PS[1], PS[2]]
    PSS = [PS[3], PS[4]]
    ACC = [PS[5], PS[6]]
    gcnt = 0
    dcnt = 0

    def front(t):
        h_t, sq, x2 = h2[t % 2], ssq[t % 2], xn2[t % 2]
        ix, wg = idxi[t % 2], wgt[t % 2]
        P.dma("sp", h_t.res.name, lambda e: e.dma_start(out=h_t.ap, in_=Sc.H2[t * 128:(t + 1) * 128, :]), writes=[h_t.res])
        P.op("act", lambda a: a.activation(out=junkb.ap, in_=h_t.ap, func=AF.Square, accum_out=sq.ap[:, 0:1]), reads=[h_t.res], writes=[junkb.res, sq.res])
        P.op("act", lambda a: a.activation(out=sq.ap[:, 0:1], in_=sq.ap[:, 0:1], func=AF.Sqrt, scale=1.0 / D, bias=1e-6), reads=[sq.res], writes=[sq.res])
        P.op("dve", lambda v: v.reciprocal(out=sq.ap[:, 1:2], in_=sq.ap[:, 0:1]), reads=[sq.res], writes=[sq.res])
        P.op("dve", lambda v: v.scalar_tensor_tensor(out=x2.ap, in0=h_t.ap, scalar=sq.ap[:, 1:2], in1=gfb.ap, op0=ALU.mult, op1=ALU.mult), reads=[h_t.res, sq.res, gfb.res], writes=[x2.res])
        P.op("act", lambda a: a.activation(out=xnb.ap, in_=h_t.ap, func=AF.Copy, scale=sq.ap[:, 1:2]), reads=[h_t.res, sq.res], writes=[xnb.res])
        ptb = PTB.ap.bitcast(BF16)
        for c in range(8):
            P.op("pe", lambda e, c=c: e.transpose(out=ptb[:, c * 128:(c + 1) * 128], in_=xnb.ap[:, c * 128:(c + 1) * 128], identity=k.identb.ap), reads=[xnb.res, k.identb.res], writes=[PTB.res])
        P.op("act", lambda a: a.copy(out=x2T.ap, in_=ptb.rearrange("p (c t) -> p c t", c=8)), reads=[PTB.res], writes=[x2T.res])
        for ub in range(4):
            pq = PQ[ub % 2]
            for ul in range(4):
                u = ub * 4 + ul
                for c in range(8):
                    P.op("pe", lambda e, u=u, ul=ul, c=c, pq=pq: e.matmul(pq.ap[:, ul * 128:(ul + 1) * 128], lhsT=Wq.ap[:, c, u * 128:(u + 1) * 128], rhs=x2T.ap[:, c, :], start=(c == 0), stop=(c == 7)),
                         reads=[Wqr[c], x2T.res], writes=[pq.res])
            P.op("act", lambda a, ub=ub, pq=pq: a.copy(out=qT.ap[:, ub * 4:(ub + 1) * 4, :], in_=pq.ap.rearrange("p (a b) -> p a b", a=4)), reads=[pq.res], writes=[qT.res])
        for ub in range(4):
            pss = PSS[ub % 2]
            for ul in range(4):
                u = ub * 4 + ul
                P.op("pe", lambda e, u=u, ul=ul, pss=pss: e.matmul(pss.ap[:, ul * 128:(ul + 1) * 128], lhsT=qT.ap[:, u, :], rhs=skb.ap[:, u, :], start=True, stop=True),
                     reads=[qT.res, skb.res], writes=[pss.res])
            P.op("act", lambda a, ub=ub, pss=pss: a.copy(out=sS.ap[:, ub * 4:(ub + 1) * 4, :], in_=pss.ap.rearrange("p (a b) -> p a b", a=4)), reads=[pss.res], writes=[sS.res])

    def topk_gen(t):
        ix, wg = idxi[t % 2], wgt[t % 2]
        for h in range(8):
            for p in range(2):
                sp = sS.ap[:, 2 * h + p, :]
                v_ = vals.ap[:, h, p, :]
                i_ = idxu.ap[:, h, p, :]
                yield P.op("dve", lambda v, sp=sp, v_=v_: v.max(out=v_[:, 0:8], in_=sp), reads=[sS.res], writes=[vals.res])
                yield P.op("dve", lambda v, sp=sp, v_=v_, i_=i_: v.max_index(out=i_[:, 0:8], in_max=v_[:, 0:8], in_values=sp), reads=[sS.res, vals.res], writes=[idxu.res])
                yield P.op("dve", lambda v, sp=sp, v_=v_: v.match_replace(out=wk.ap, in_to_replace=v_[:, 0:8], in_values=sp, imm_value=-1e30), reads=[sS.res, vals.res], writes=[wk.res])
                yield P.op("dve", lambda v, v_=v_: v.max(out=v_[:, 8:16], in_=wk.ap), reads=[wk.res], writes=[vals.res])
                yield P.op("dve", lambda v, v_=v_, i_=i_: v.max_index(out=i_[:, 8:16], in_max=v_[:, 8:16], in_values=wk.ap), reads=[wk.res, vals.res], writes=[idxu.res])
            yield P.op("dve", lambda v, h=h: v.tensor_tensor(out=cand.ap, in0=vals.ap[:, h, 0, :].unsqueeze(2).to_broadcast([128, 16, 16]), in1=vals.ap[:, h, 1, :].unsqueeze(1).to_broadcast([128, 16, 16]), op=ALU.add),
                 reads=[vals.res], writes=[cand.res])
            c2 = cand.ap.rearrange("p a b -> p (a b)")
            yield P.op("dve", lambda v, h=h, c2=c2: v.max(out=tops.ap[:, h, 0:8], in_=c2), reads=[cand.res], writes=[tops.res])
            yield P.op("dve", lambda v, h=h, c2=c2: v.max_index(out=posu.ap[:, h, 0:8], in_max=tops.ap[:, h, 0:8], in_values=c2), reads=[cand.res, tops.res], writes=[posu.res])
            yield P.op("dve", lambda v, h=h, c2=c2: v.match_replace(out=wk2.ap, in_to_replace=tops.ap[:, h, 0:8], in_values=c2, imm_value=-1e30), reads=[cand.res, tops.res], writes=[wk2.res])
            yield P.op("dve", lambda v, h=h: v.max(out=tops.ap[:, h, 8:16], in_=wk2.ap), reads=[wk2.res], writes=[tops.res])
            yield P.op("dve", lambda v, h=h: v.max_index(out=posu.ap[:, h, 8:16], in_max=tops.ap[:, h, 8:16], in_values=wk2.ap), reads=[wk2.res, tops.res], writes=[posu.res])
        yield P.op("pool", lambda v: v.tensor_copy(out=posf.ap, in_=posu.ap), reads=[posu.res], writes=[posf.res])
        yield P.op("pool", lambda v: v.tensor_scalar(out=aq.ap, in0=posf.ap, scalar1=0.0625, scalar2=0.53125, op0=ALU.mult, op1=ALU.add), reads=[posf.res], writes=[aq.res])
        yield P.op("pool", lambda v: v.tensor_scalar(out=aq.ap, in0=aq.ap, scalar1=8388608.0, scalar2=None, op0=ALU.add), reads=[aq.res], writes=[aq.res])
        yield P.op("pool", lambda v: v.tensor_scalar(out=aq.ap, in0=aq.ap, scalar1=-8388609.0, scalar2=None, op0=ALU.add), reads=[aq.res], writes=[aq.res])
        yield P.op("dve", lambda v: v.scalar_tensor_tensor(out=bq.ap, in0=aq.ap, scalar=-16.0, in1=posf.ap, op0=ALU.mult, op1=ALU.add), reads=[aq.res, posf.res], writes=[bq.res])
        yield P.op("pool", lambda v: v.tensor_copy(out=i1f.ap, in_=idxu.ap[:, :, 0, :]), reads=[idxu.res], writes=[i1f.res])
        yield P.op("pool", lambda v: v.tensor_copy(out=i2f.ap, in_=idxu.ap[:, :, 1, :]), reads=[idxu.res], writes=[i2f.res])
        for h in range(8):
            for (sel, src, dst) in ((aq, i1f, idx1), (bq, i2f, idx2)):
                yield P.op("dve", lambda v, h=h, sel=sel: v.tensor_tensor(out=oh.ap, in0=sel.ap[:, h, :].unsqueeze(2).to_broadcast([128, 16, 16]), in1=iota.ap.unsqueeze(1).to_broadcast([128, 16, 16]), op=ALU.is_equal),
                     reads=[sel.res, iota.res], writes=[oh.res])
                yield P.op("dve", lambda v, h=h, src=src: v.tensor_tensor(out=oh.ap, in0=oh.ap, in1=src.ap[:, h, :].unsqueeze(1).to_broadcast([128, 16, 16]), op=ALU.mult), reads=[oh.res, src.res], writes=[oh.res])
                yield P.op("dve", lambda v, h=h, dst=dst: v.tensor_reduce(out=dst.ap[:, h, :], in_=oh.ap, axis=mybir.AxisListType.X, op=ALU.add), reads=[oh.res], writes=[dst.res])
        yield P.op("dve", lambda v: v.scalar_tensor_tensor(out=idxf.ap, in0=idx1.ap.rearrange("p a b -> p (a b)"), scalar=128.0, in1=idx2.ap.rearrange("p a b -> p (a b)"), op0=ALU.mult, op1=ALU.add),
             reads=[idx1.res, idx2.res], writes=[idxf.res])
        yield P.op("pool", lambda v: v.tensor_copy(out=ix.ap, in_=idxf.ap), reads=[idxf.res], writes=[ix.res])
        yield P.op("pool", lambda v: v.tensor_scalar(out=negm.ap, in0=tops.ap[:, :, 0], scalar1=-1.0, scalar2=None, op0=ALU.mult), reads=[tops.res], writes=[negm.res])
        yield P.op("dve", lambda v: v.tensor_tensor(out=ex.ap, in0=tops.ap, in1=negm.ap.unsqueeze(2).to_broadcast([128, 8, 16]), op=ALU.add), reads=[tops.res, negm.res], writes=[ex.res])

    def frontC(t):
        wg = wgt[t % 2]
        P.op("act", lambda a: a.activation(out=ex.ap, in_=ex.ap, func=AF.Exp), reads=[ex.res], writes=[ex.res])
        P.op("dve", lambda v: v.tensor_reduce(out=sm.ap, in_=ex.ap, axis=mybir.AxisListType.X, op=ALU.add), reads=[ex.res], writes=[sm.res])
        P.op("dve", lambda v: v.reciprocal(out=sm.ap, in_=sm.ap), reads=[sm.res], writes=[sm.res])
        P.op("dve", lambda v: v.tensor_tensor(out=wg.ap, in0=ex.ap, in1=sm.ap.unsqueeze(2).to_broadcast([128, 8, 16]), op=ALU.mult), reads=[ex.res, sm.res], writes=[wg.res])

    def back(t, gen=None):
        nonlocal gcnt, dcnt
        h_t, sq, x2 = h2[t % 2], ssq[t % 2], xn2[t % 2]
        ix, wg = idxi[t % 2], wgt[t % 2]
        o_t = ot[t % 2]
        wg2 = wg.ap.rearrange("p a b -> p (a b)")
        P.op("act", lambda v: v.copy(out=hsum.ap, in_=h_t.ap), reads=[h_t.res], writes=[hsum.res])
        gbs = []
        for slot in range(128):
            gb = GB[gcnt % NB_G]
            gcnt += 1
            gbs.append(gb)
            P.dma("pool", gb.res.name, lambda g_, gb=gb, slot=slot: g_.indirect_dma_start(out=gb.ap, out_offset=None, in_=Sc.UVB[:, :], in_offset=bass.IndirectOffsetOnAxis(ap=ix.ap[:, slot:slot + 1], axis=0)),
                  reads=[ix.res], writes=[gb.res])
            P.op("dve", lambda v, gb=gb, slot=slot: v.scalar_tensor_tensor(out=junkd.ap, in0=gb.ap[:, 0:D], scalar=1.0, in1=x2.ap, op0=ALU.mult, op1=ALU.mult, accum_out=act.ap[:, slot:slot + 1]),
                 reads=[gb.res, x2.res], writes=[junkd.res, actr[slot]])
            if gen is not None:
                for _ in range(2 if slot % 2 == 0 else 1):
                    next(gen, None)
            if slot % 4 != 3:
                continue
            s0 = slot - 3
            gi = s0 // 4
            P.op("act", lambda a, s0=s0: a.activation(out=cg.ap[:, s0:s0 + 4], in_=act.ap[:, s0:s0 + 4], func=AF.Gelu), reads=actr[s0:s0 + 4], writes=[cgr[gi]])
            P.op("dve", lambda v, s0=s0: v.tensor_tensor(out=coef.ap[:, s0:s0 + 4], in0=cg.ap[:, s0:s0 + 4], in1=wg2[:, s0:s0 + 4], op=ALU.mult), reads=[cgr[gi], wg.res], writes=[coefr[gi]])
            for sl in range(s0, s0 + 4):
                dg = DG[dcnt % 4]
                dcnt += 1
                gbv = gbs[sl]
                P.op("act", lambda a, dg=dg, sl=sl: a.activation(out=dg.ap, in_=k.identf.ap, func=AF.Copy, scale=coef.ap[:, sl:sl + 1]), reads=[k.identf.res, coefr[gi]], writes=[dg.res])
                for half in range(2):
                    P.op("pe", lambda e, dg=dg, gbv=gbv, half=half, sl=sl: e.matmul(ACC[half].ap, lhsT=dg.ap, rhs=gbv.ap[:, D + half * 512:D + (half + 1) * 512], start=(sl == 0), stop=(sl == 127)),
                         reads=[dg.res, gbv.res], writes=[ACC[half].res])
        if gen is not None:
            for _ in gen:
                pass
        for half in range(2):
            P.op("dve", lambda v, half=half: v.tensor_tensor(out=hsum.ap[:, half * 512:(half + 1) * 512], in0=ACC[half].ap, in1=hsum.ap[:, half * 512:(half + 1) * 512], op=ALU.add), reads=[ACC[half].res, hsum.res], writes=[hsum.res])
        P.op("act", lambda a: a.activation(out=junkb.ap, in_=hsum.ap, func=AF.Square, accum_out=sq.ap[:, 2:3]), reads=[hsum.res], writes=[junkb.res, sq.res])
        P.op("act", lambda a: a.activation(out=sq.ap[:, 2:3], in_=sq.ap[:, 2:3], func=AF.Sqrt, scale=1.0 / D, bias=1e-6), reads=[sq.res], writes=[sq.res])
        P.op("dve", lambda v: v.reciprocal(out=sq.ap[:, 3:4], in_=sq.ap[:, 2:3]), reads=[sq.res], writes=[sq.res])
        P.op("dve", lambda v: v.scalar_tensor_tensor(out=o_t.ap, in0=hsum.ap, scalar=sq.ap[:, 3:4], in1=gfin.ap, op0=ALU.mult, op1=ALU.mult), reads=[hsum.res, sq.res, gfin.res], writes=[o_t.res])
        P.dma("sp", o_t.res.name, lambda e: e.dma_start(out=out[t * 128:(t + 1) * 128, :], in_=o_t.ap), reads=[o_t.res])

    front(0)
    for _ in topk_gen(0):
        pass
    frontC(0)
    for t in range(k.nt_e):
        if t + 1 < k.nt_e:
            front(t + 1)
            back(t, topk_gen(t + 1))
            frontC(t + 1)
        else:
            back(t)
    P.barrier()
    A.release(m0)


def _host_consts():
    import ml_dtypes
    bf = ml_dtypes.bfloat16
    f32 = np.float32
    inv = np.power(f32(500000.0), -np.arange(0, 16, 2, dtype=f32) / f32(16)).astype(f32)
    pos = np.arange(S, dtype=f32)
    ang = (pos[:, None] * inv[None, :]).astype(f32)
    cos, sin = np.cos(ang).astype(f32), np.sin(ang).astype(f32)
    cosT = np.concatenate([cos.T, cos.T], 0)
    sinT = np.concatenate([sin.T, sin.T], 0)
    c_rope = np.zeros((128, 4 * S), f32)
    c_rope[:16] = np.concatenate([cosT * f32(0.125), sinT * f32(0.125), cosT, sinT], 1).astype(f32)
    cend = (np.arange(255) * 16 + 31).astype(f32)
    angc = (cend[:, None] * inv[None, :]).astype(f32)
    cc = np.zeros((16, 256), f32)
    sc = np.zeros((16, 256), f32)
    cc[:, :255] = np.concatenate([np.cos(angc).T, np.cos(angc).T], 0)
    sc[:, :255] = np.concatenate([np.sin(angc).T, np.sin(angc).T], 0)
    c_ropec = np.zeros((128, 512), f32)
    c_ropec[:16] = np.concatenate([cc, sc], 1)
    s_ = np.arange(128)[:, None]
    t_ = np.arange(128)[None, :]
    tri = (s_ <= t_).astype(f32)
    sup = (s_ > t_).astype(f32)
    c_trib = np.concatenate([tri, sup], 1).astype(bf)
    c_trif = np.concatenate([tri, np.ones((128, 128), f32)], 1).astype(f32)
    n_ = (np.arange(2)[None, :, None] * 128 + np.arange(128)[:, None, None])
    tt = np.arange(S)[None, None, :]
    cmpmask = ((n_ < 255) & (16 * n_ + 31 <= tt)).astype(f32).reshape(128, 2 * S).astype(bf)
    etab = np.zeros((128, S), f32)
    etab[64:] = (np.arange(S)[None, :] // 64 == np.arange(64)[:, None])
    etab = etab.astype(bf)
    q = np.arange(S)
    qb = q // 64
    j = np.arange(64)[None, :]
    sb = np.zeros((S, 64), np.float64)
    sb += (j == 0) * 1e9 + (j == qb[:, None]) * 2e9 + (j == qb[:, None] - 1) * 4e9
    sb = np.where(j > qb[:, None], -1.0 - j / 64.0, sb)
    selbias = sb.astype(f32).reshape(NT, 128, 64).transpose(1, 0, 2).reshape(128, NT * 64)
    cs = np.arange(255)[:, None] * 16
    ss_ = np.arange(64)[None, :] * 64
    ov = np.clip(np.minimum(cs + 32, ss_ + 64) - np.maximum(cs, ss_), 0, None) / 32.0
    ova = np.zeros((256, 65), f32)
    ova[:255, 0] = 1.0
    ova[:255, 1:] = ov
    ovaug = ova.reshape(2, 128, 65).transpose(1, 0, 2).reshape(128, 130).astype(bf)
    iota16 = np.tile(np.arange(16, dtype=f32)[None, :], (128, 1))
    return dict(c_rope=c_rope, c_ropec=c_ropec, c_trib=c_trib, c_trif=c_trif, c_cmpmask=cmpmask, c_etab=etab,
                c_selbias=np.ascontiguousarray(selbias), c_ovaug=np.ascontiguousarray(ovaug), c_iota16=iota16)


def prep_shared(inp):
    a = lambda v: np.ascontiguousarray(np.asarray(v, dtype=np.float32))
    sh = dict(
        w_in=a(inp["w_in"][0]),
        gmix=a(np.asarray(inp["norm_mix"])[0].reshape(8, 128).T),
        fbias=a(np.asarray(inp["fox_f_bias"])[0].reshape(1, 8)),
        w1=a(np.concatenate([np.asarray(inp["nsa_cmp_w1"])[0].reshape(2, 32, 64, 128).transpose(2, 0, 1, 3).reshape(64, -1), np.zeros((64, 8192), np.float32)], 0)),
        pos=a(np.concatenate([np.asarray(inp["nsa_cmp_pos"])[0].transpose(2, 0, 1).reshape(64, 64), np.zeros((64, 64), np.float32)], 0)),
        w2=a(np.asarray(inp["nsa_cmp_w2"])[0].transpose(1, 0, 2).reshape(128, 128)),
        wbr=a(inp["w_branch"][0]),
        wout=a(inp["w_out"][0]),
        gffn=a(np.asarray(inp["norm_ffn"])[0].reshape(1, D)),
        gffn8=a(np.asarray(inp["norm_ffn"])[0].reshape(8, 128).T),
        wq=a(inp["peer_wq"][0]),
        skT=a(np.asarray(inp["peer_subkeys"])[0].transpose(3, 0, 1, 2).reshape(128, 16 * 128)),
        pu=a(inp["peer_u"][0]),
        pv=a(inp["peer_v"][0]),
        gfin=a(np.asarray(inp["norm_final"]).reshape(1, D)),
    )
    sh.update(_host_consts())
    return sh


_NC_CACHE = {}


def kernel(**inputs):
    x = np.asarray(inputs["x"], dtype=np.float32)
    sh = prep_shared(inputs)
    if "nc" not in _NC_CACHE:
        _NC_CACHE["nc"] = build_program()
    nc = _NC_CACHE["nc"]
    in_maps = [dict(sh, x=np.ascontiguousarray(x[b])) for b in range(8)]
    res = run_bass_kernel_spmd(nc, in_maps, core_ids=list(range(8)))
    return np.stack([np.asarray(r["out"], dtype=np.float32) for r in res.results], 0)
```

```python
import numpy as np
from contextlib import ExitStack
import concourse.bass as bass
import concourse.mybir as mybir
from concourse.bass_utils import run_bass_kernel_spmd

F32 = mybir.dt.float32
BF16 = mybir.dt.bfloat16
I32 = mybir.dt.int32
U32 = mybir.dt.uint32
U8 = mybir.dt.uint8
AF = mybir.ActivationFunctionType
ALU = mybir.AluOpType
DTSIZE = {F32: 4, BF16: 2, I32: 4, U32: 4, U8: 1}

S = 4096
D = 1024
NT = 32
DIN_A = 2848
C_FQ, C_FK, C_FV, C_FL, C_NQ = 0, 512, 1024, 1536, 1544
C_KC, C_VC, C_KSL, C_VSL, C_KWN, C_VWN, C_NG, C_MG = 2056, 2184, 2312, 2440, 2568, 2696, 2824, 2848
NEG = -30000.0


class Res:
    __slots__ = ("name", "w", "rs", "excl")

    def __init__(self, name="", excl=False):
        self.name = name
        self.excl = excl
        self.w = None
        self.rs = {}


class Op:
    __slots__ = ("eng", "fn", "deps", "needed", "num", "dma")


class Prog:
    ENG = ("pe", "act", "dve", "pool", "sp")

    def __init__(self, nc, es):
        self.nc = nc
        self.es = es
        self.ops = {e: [] for e in self.ENG}
        self.esem = {e: es.enter_context(nc.semaphore("es_" + e)) for e in self.ENG}
        self.dsem = {}
        self.last = {e: None for e in self.ENG}
        self.qhist = {}
        self.max_out = 10 ** 9

    def _dsem(self, key):
        if key not in self.dsem:
            self.dsem[key] = [self.es.enter_context(self.nc.semaphore("ds_" + key)), 0]
        return self.dsem[key]

    def _deps(self, eng, reads, writes):
        deps = []
        for r in reads:
            if r.w is not None:
                deps.append(r.w)
            if r.excl:
                deps.extend(v for kk, v in r.rs.items() if kk != eng)
        for w in writes:
            if w.w is not None:
                deps.append(w.w)
            deps.extend(w.rs.values())
        out = []
        for d in deps:
            if isinstance(d, Op):
                if d.eng == eng and eng == "pe":
                    continue
                d.needed = True
            out.append(d)
        return out

    def op(self, eng, fn, reads=(), writes=()):
        o = Op()
        o.eng, o.fn, o.needed, o.dma, o.num = eng, fn, False, None, 0
        o.deps = self._deps(eng, reads, writes)
        for r in reads:
            r.rs[eng] = o
        for w in writes:
            w.w = o
            w.rs = {}
        self.ops[eng].append(o)
        self.last[eng] = o
        return o

    def dma(self, q, key, fn, reads=(), writes=()):
        o = Op()
        o.eng, o.fn, o.needed, o.num = q, fn, False, 0
        o.deps = self._deps(q, reads, writes)
        h = self.qhist.setdefault(q, [])
        if len(h) >= self.max_out:
            o.deps.append(h[-self.max_out])
        s = self._dsem(key)
        s[1] += 16
        ev = (key, s[1])
        o.dma = key
        h.append(ev)
        for r in reads:
            r.rs[("d", key)] = ev
        for w in writes:
            w.w = ev
            w.rs = {}
        self.ops[q].append(o)
        return o

    def barrier(self):
        lasts = [self.last[e] for e in self.ENG if self.last[e] is not None]
        for o in lasts:
            o.needed = True
        dev = [(k, v[1]) for k, v in self.dsem.items() if v[1] > 0]
        for e in self.ENG:
            o = Op()
            o.eng, o.fn, o.needed, o.dma, o.num = e, None, False, None, 0
            o.deps = [l for l in lasts if not (l.eng == e == "pe")] + dev
            self.ops[e].append(o)

    def emit(self, blk):
        for e in self.ENG:
            c = 0
            for o in self.ops[e]:
                if o.dma is None and o.needed and o.fn is not None:
                    c += 1
                    o.num = c

        def run(e, engobj):
            seen = {}
            for o in self.ops[e]:
                for d in o.deps:
                    if isinstance(d, Op):
                        sem, val, k = self.esem[d.eng], d.num, ("e", d.eng)
                    else:
                        sem, val, k = self.dsem[d[0]][0], d[1], ("d", d[0])
                    if seen.get(k, 0) < val:
                        engobj.wait_ge(sem, val)
                        seen[k] = val
                if o.fn is not None:
                    ins = o.fn(engobj)
                    if o.dma is not None:
                        ins.then_inc(self.dsem[o.dma][0], 16)
                    elif o.needed:
                        ins.then_inc(self.esem[e], 1)

        blk.tensor(lambda t: run("pe", t))
        blk.scalar(lambda t: run("act", t))
        blk.vector(lambda t: run("dve", t))
        blk.gpsimd(lambda t: run("pool", t))
        blk.sync(lambda t: run("sp", t))


class Tile:
    __slots__ = ("ap", "res")

    def __init__(self, ap, res):
        self.ap = ap
        self.res = res


class Arena:
    def __init__(self, nc, es, nbytes):
        self.t = es.enter_context(nc.sbuf_tensor("arena", [128, nbytes], U8))
        self.off = 0
        self.cap = nbytes
        self.n = 0

    def alloc(self, shape, dt, name=None):
        n = int(np.prod(shape)) * DTSIZE[dt]
        n_al = (n + 63) // 64 * 64
        assert self.off + n_al <= self.cap, f"SBUF arena overflow {self.off}+{n_al}>{self.cap} ({name})"
        ap = self.t[:, self.off:self.off + n].bitcast(dt)
        self.off += n_al
        if len(shape) > 1:
            names = " ".join(f"d{i}" for i in range(len(shape)))
            kw = {f"d{i}": int(shape[i]) for i in range(len(shape))}
            ap = ap.rearrange(f"p ({names}) -> p {names}", **kw)
        self.n += 1
        return Tile(ap, Res(name or f"t{self.n}"))

    def mark(self):
        return self.off

    def release(self, m):
        self.off = m


class K:
    pass


_DBG = {}


def build_program(dbg=False, phases="ABCDE", lv=9, nt_e=NT):
    nc = bass.Bass("TRN2", target_bir_lowering=False)
    es = ExitStack()
    k = K()
    k.nc = nc
    k.lv = lv
    k.nt_e = nt_e
    import os
    k.skip = os.environ.get('KSKIP', '')

    def din(name, shape, dt=F32):
        return nc.dram_tensor(name, list(shape), dt, kind="ExternalInput").ap()

    def dscr(name, shape, dt):
        return nc.dram_tensor(name, list(shape), dt, kind=("ExternalOutput" if dbg else "Internal")).ap()

    I = K()
    I.x = din("x", [S, D])
    I.w_in = din("w_in", [D, 4896])
    I.gmix = din("gmix", [128, 8])
    I.fbias = din("fbias", [1, 8])
    I.w1 = din("w1", [128, 2 * 32 * 128])
    I.pos = din("pos", [128, 2 * 32])
    I.w2 = din("w2", [128, 2 * 64])
    I.wbr = din("wbr", [2, 512, 1024])
    I.wout = din("wout", [D, D])
    I.gffn = din("gffn", [1, D])
    I.gffn8 = din("gffn8", [128, 8])
    I.wq = din("wq", [D, 2048])
    I.skT = din("skT", [128, 16 * 128])
    I.pu = din("pu", [16384, D])
    I.pv = din("pv", [16384, D])
    I.gfin = din("gfin", [1, D])
    I.rope = din("c_rope", [128, 4 * S])
    I.ropec = din("c_ropec", [128, 2 * 256])
    I.trib = din("c_trib", [128, 2 * 128], BF16)
    I.trif = din("c_trif", [128, 2 * 128])
    I.cmpmask = din("c_cmpmask", [128, 2 * S], BF16)
    I.etab = din("c_etab", [128, S], BF16)
    I.selbias = din("c_selbias", [128, NT * 64])
    I.ovaug = din("c_ovaug", [128, 2 * 65], BF16)
    I.iota16 = din("c_iota16", [128, 16])
    out = nc.dram_tensor("out", [S, D], F32, kind="ExternalOutput").ap()

    Sc = K()
    Sc.XT = dscr("s_xt", [8, 128, S], BF16)
    Sc.QF = dscr("s_qf", [8, 128, S], BF16)
    Sc.KF = dscr("s_kf", [8, 128, S], BF16)
    Sc.VF = dscr("s_vf", [NT, 128, 8 * 65], BF16)
    Sc.QN = dscr("s_qn", [8, 128, S], BF16)
    Sc.KC = dscr("s_kc", [2, 128, S], BF16)
    Sc.VC = dscr("s_vc", [2, 128, S], BF16)
    Sc.KSL = dscr("s_ksl", [2, 128, S], BF16)
    Sc.KWN = dscr("s_kwn", [2, 128, S], BF16)
    Sc.VSW = dscr("s_vsw", [NT, 128, 4 * 65], BF16)
    Sc.YF = dscr("s_yf", [NT, 128, 512], BF16)
    Sc.YN = dscr("s_yn", [NT, 128, 512], BF16)
    Sc.H2 = dscr("s_h2", [S, D], F32)
    Sc.UVB = nc.dram_tensor("s_uvb", [16384, 2 * D], BF16, kind="Internal").ap()
    if dbg:
        Sc.dbg1 = dscr("s_dbg1", [128, NT * 8], F32)
        Sc.dbg2 = dscr("s_dbg2", [128, NT * 24], F32)
    k.I, k.Sc, k.out = I, Sc, out
    k.dbgC = (dscr("s_dbgc0", [128, 512], BF16), dscr("s_dbgc1", [128, 2 * 2 * 129], BF16)) if dbg else None

    P = Prog(nc, es)
    A = Arena(nc, es, 204800)
    k.P, k.A = P, A
    PS = []
    for i in range(8):
        t = es.enter_context(nc.psum_tensor(f"psb{i}", [128, 512], F32))
        PS.append(Tile(t[:, :], Res(f"ps{i}", excl=True)))
    k.PS = PS

    k.identb = A.alloc([128], BF16, "identb")
    k.identf = A.alloc([128], F32, "identf")
    k.trib = A.alloc([2, 128], BF16, "trib")
    k.trif = A.alloc([2, 128], F32, "trif")
    k.logf = A.alloc([NT, 8], F32, "logf")
    k.gate = A.alloc([NT, 24], F32, "gate")
    k.fb = A.alloc([8], F32, "fb")

    def setup_consts():
        P.op("pool", lambda g: g.memset(k.identf.ap, 0.0), writes=[k.identf.res])
        P.op("pool", lambda g: g.affine_select(out=k.identf.ap, in_=k.identf.ap, pattern=[[-1, 128]],
                                               compare_op=ALU.not_equal, fill=1.0, base=0, channel_multiplier=1),
             reads=[k.identf.res], writes=[k.identf.res])
        P.op("dve", lambda v: v.tensor_copy(out=k.identb.ap, in_=k.identf.ap), reads=[k.identf.res], writes=[k.identb.res])
        P.dma("sp", "const", lambda e: e.dma_start(out=k.trib.ap, in_=I.trib.rearrange("p (a b) -> p a b", a=2)), writes=[k.trib.res])
        P.dma("sp", "const", lambda e: e.dma_start(out=k.trif.ap, in_=I.trif.rearrange("p (a b) -> p a b", a=2)), writes=[k.trif.res])
        P.dma("sp", "const", lambda e: e.dma_start(out=k.fb.ap, in_=I.fbias[0:1, :].partition_broadcast(128)), writes=[k.fb.res])

    setup_consts()
    k.conv = ("E" in phases)
    if "A" in phases:
        phase_a(k)
    elif k.conv:
        _m = A.mark()
        phase_0(k, None)
        P.barrier()
        A.release(_m)
    P.barrier()
    if "B" in phases:
        phase_b(k)
        P.barrier()
    if "C" in phases:
        phase_c(k)
        P.barrier()
    if "D" in phases:
        phase_d(k)
        P.barrier()
    if "E" in phases:
        phase_e(k)
        P.barrier()
    if dbg and "A" in phases:
        P.dma("sp", "dbg", lambda e: e.dma_start(out=Sc.dbg1, in_=k.logf.ap.rearrange("p a b -> p (a b)")), reads=[k.logf.res])
        P.dma("sp", "dbg", lambda e: e.dma_start(out=Sc.dbg2, in_=k.gate.ap.rearrange("p a b -> p (a b)")), reads=[k.gate.res])
    P.barrier()
    blk = es.enter_context(nc.Block())
    P.emit(blk)
    es.close()
    return nc


def phase_0(k, after):
    P, A, I, Sc = k.P, k.A, k.I, k.Sc
    RB = 2
    st = [A.alloc([RB, D], F32, f"cv_s{i}") for i in range(3)]
    sb = [A.alloc([RB, D], BF16, f"cv_b{i}") for i in range(3)]
    n = 0
    for (src, c0) in ((I.pu, 0), (I.pv, D)):
        sv = src.rearrange("(p r) d -> p r d", p=128)
        dv = Sc.UVB.rearrange("(p r) d -> p r d", p=128)[:, :, c0:c0 + D]
        for r0 in range(0, 128, RB):
            a, b = st[n % 3], sb[n % 3]
            P.dma("pool", a.res.name, lambda e, a=a, sv=sv, r0=r0: e.dma_start(out=a.ap, in_=sv[:, r0:r0 + RB, :]), reads=([after] if (after is not None and n == 0) else []), writes=[a.res])
            P.op("pool", lambda e, a=a, b=b: e.tensor_copy(out=b.ap, in_=a.ap), reads=[a.res], writes=[b.res])
            P.dma("pool", b.res.name, lambda e, b=b, dv=dv, r0=r0: e.dma_start(out=dv[:, r0:r0 + RB, :], in_=b.ap), reads=[b.res])
            n += 1


FM_UNITS = ([("QF", h, C_FQ + 64 * h, None, 0.125) for h in range(8)]
            + [("KF", h, C_FK + 64 * h, None, 1.0) for h in range(8)]
            + [("QN", h, C_NQ + 64 * h, h, 0.125) for h in range(8)]
            + [("KC", g, C_KC + 64 * g, None, 1.0) for g in range(2)]
            + [("VC", g, C_VC + 64 * g, None, 1.0) for g in range(2)]
            + [("KSL", g, C_KSL + 64 * g, 8 + g, 1.0) for g in range(2)]
            + [("KWN", g, C_KWN + 64 * g, 10 + g, 1.0) for g in range(2)])


def phase_a(k):
    P, A, I, Sc, PS = k.P, k.A, k.I, k.Sc, k.PS
    m0 = A.mark()
    W1 = A.alloc([8, DIN_A], BF16, "W1")
    W1r = [Res(f"W1_{c}") for c in range(8)]
    Wr = A.alloc([8, 12, 32], BF16, "Wr")
    Wr0 = Res("Wr0")
    P.op("pool", lambda g: g.memset(Wr.ap, 0.0), writes=[Wr0])
    Wrr = [Res(f"Wr_{c}") for c in range(8)]
    gmix = A.alloc([8], F32, "gmix")
    wst = [A.alloc([DIN_A], F32, f"wst{i}") for i in range(2)]
    P.dma("sp", "const", lambda e: e.dma_start(out=gmix.ap, in_=I.gmix), writes=[gmix.res])
    for c in range(8):
        ws = wst[c % 2]
        P.dma("sp", ws.res.name, lambda e, c=c, ws=ws: e.dma_start(out=ws.ap, in_=I.w_in[c * 128:(c + 1) * 128, 0:DIN_A]), writes=[ws.res])
        P.op("dve", lambda v, c=c, ws=ws: v.tensor_scalar(out=W1.ap[:, c, :], in0=ws.ap, scalar1=gmix.ap[:, c:c + 1], scalar2=None, op0=ALU.mult),
             reads=[ws.res, gmix.res], writes=[W1r[c]])
        for (base, n, u0) in ((C_NQ, 8, 0), (C_KSL, 2, 8), (C_KWN, 2, 10)):
            src = W1.ap[:, c, base:base + n * 64].rearrange("p (u d) -> p u d", d=64)
            P.op("pool", lambda g, src=src, c=c, u0=u0, n=n: g.tensor_scalar(out=Wr.ap[:, c, u0:u0 + n, 0:8], in0=src[:, :, 8:16], scalar1=-1.0, scalar2=None, op0=ALU.mult),
                 reads=[W1r[c], Wr0], writes=[Wrr[c]])
            P.op("pool", lambda g, src=src, c=c, u0=u0, n=n: g.tensor_copy(out=Wr.ap[:, c, u0:u0 + n, 8:16], in_=src[:, :, 0:8]),
                 reads=[W1r[c]], writes=[Wrr[c]])

    xs = [A.alloc([D], F32, f"xs{i}") for i in range(2)]
    junk = A.alloc([D], BF16, "junk")
    xn = [A.alloc([D], BF16, f"xn{i}") for i in range(2)]
    ss = A.alloc([NT], F32, "ss")
    ssr = [Res(f"ss{t}") for t in range(NT)]
    XTc = [A.alloc([8, 512], BF16, f"XTc{i}") for i in range(2)]
    XTr = [[Res(f"XT{i}_{j}") for j in range(4)] for i in range(2)]
    VFt = [A.alloc([8, 65], BF16, f"VFt{i}") for i in range(2)]
    VSWt = [A.alloc([4, 65], BF16, f"VSWt{i}") for i in range(2)]
    zraw = A.alloc([NT, 8], F32, "zraw")
    graw = A.alloc([NT, 24], F32, "graw")
    rope = [A.alloc([4, 512], F32, f"rope{i}") for i in range(2)]
    etc = [A.alloc([512], BF16, f"etc{i}") for i in range(2)]
    fm = [A.alloc([512], BF16, f"fm{i}") for i in range(4)]
    rt1 = [A.alloc([512], F32, f"rt1_{i}") for i in range(2)]
    rt2 = [A.alloc([512], F32, f"rt2_{i}") for i in range(2)]
    for i in range(2):
        P.op("pool", lambda g, i=i: g.memset(VFt[i].ap, 1.0), writes=[VFt[i].res])
        P.op("pool", lambda g, i=i: g.memset(VSWt[i].ap, 1.0), writes=[VSWt[i].res])
    if k.conv:
        phase_0(k, W1r[7])
    PT = [PS[0], PS[1]]
    PSV, PSS = PS[2], PS[3]
    PSU = [PS[4], PS[5]]
    PSR = [PS[6], PS[7]]
    ucount = 0
    rcount = 0
    for t in range(NT):
        q, j = divmod(t, 4)
        x_t = xs[t % 2]
        xn_t = xn[t % 2]
        P.dma("sp", x_t.res.name, lambda e, t=t, x_t=x_t: e.dma_start(out=x_t.ap, in_=I.x[t * 128:(t + 1) * 128, :]), writes=[x_t.res])
        P.op("act", lambda a, t=t, x_t=x_t: a.activation(out=junk.ap, in_=x_t.ap, func=AF.Square, accum_out=ss.ap[:, t:t + 1]),
             reads=[x_t.res], writes=[junk.res, ssr[t]])
        P.op("act", lambda a, t=t: a.activation(out=ss.ap[:, t:t + 1], in_=ss.ap[:, t:t + 1], func=AF.Sqrt, scale=1.0 / D, bias=1e-6),
             reads=[ssr[t]], writes=[ssr[t]])
        P.op("dve", lambda v, t=t: v.reciprocal(out=ss.ap[:, t:t + 1], in_=ss.ap[:, t:t + 1]), reads=[ssr[t]], writes=[ssr[t]])
        P.op("act", lambda a, t=t, x_t=x_t, xn_t=xn_t: a.activation(out=xn_t.ap, in_=x_t.ap, func=AF.Copy, scale=ss.ap[:, t:t + 1]),
             reads=[x_t.res, ssr[t]], writes=[xn_t.res])
        pt = PT[t % 2]
        ptb = pt.ap.bitcast(BF16)
        for c in range(8):
            P.op("pe", lambda e, c=c, ptb=ptb, xn_t=xn_t: e.transpose(out=ptb[:, c * 128:(c + 1) * 128], in_=xn_t.ap[:, c * 128:(c + 1) * 128], identity=k.identb.ap),
                 reads=[xn_t.res, k.identb.res], writes=[pt.res])
        xc = XTc[q % 2]
        P.op("dve", lambda v, ptb=ptb, xc=xc, j=j: v.tensor_copy(out=xc.ap[:, :, j * 128:(j + 1) * 128], in_=ptb.rearrange("p (c t) -> p c t", c=8)),
             reads=[pt.res], writes=[XTr[q % 2][j]])
        xr = XTr[q % 2][j]
        for c in range(8):
            P.op("pe", lambda e, c=c, xc=xc, j=j: e.matmul(PSV.ap[:, 0:512], lhsT=xc.ap[:, c, j * 128:(j + 1) * 128], rhs=W1.ap[:, c, C_FV:C_FV + 512], start=(c == 0), stop=(c == 7)),
                 reads=[xr, W1r[c]], writes=[PSV.res])
        for (cb, n, o0) in ((C_FL, 8, 0), (C_NG, 24, 8), (C_VSL, 128, 32), (C_VWN, 128, 160)):
            for c in range(8):
                P.op("pe", lambda e, c=c, xc=xc, j=j, cb=cb, n=n, o0=o0: e.matmul(PSS.ap[:, o0:o0 + n], lhsT=xc.ap[:, c, j * 128:(j + 1) * 128], rhs=W1.ap[:, c, cb:cb + n], start=(c == 0), stop=(c == 7)),
                     reads=[xr, W1r[c]], writes=[PSS.res])
        vf = VFt[t % 2]
        vsw = VSWt[t % 2]
        P.op("act", lambda a, vf=vf: a.copy(out=vf.ap[:, :, 0:64], in_=PSV.ap[:, 0:512].rearrange("p (h d) -> p h d", h=8)), reads=[PSV.res], writes=[vf.res])
        P.op("dve", lambda v, vsw=vsw: v.tensor_copy(out=vsw.ap[:, :, 0:64], in_=PSS.ap[:, 32:288].rearrange("p (h d) -> p h d", h=4)), reads=[PSS.res], writes=[vsw.res])
        P.op("dve", lambda v, t=t: v.tensor_copy(out=zraw.ap[:, t, :], in_=PSS.ap[:, 0:8]), reads=[PSS.res], writes=[zraw.res])
        P.op("dve", lambda v, t=t: v.tensor_copy(out=graw.ap[:, t, :], in_=PSS.ap[:, 8:32]), reads=[PSS.res], writes=[graw.res])
        P.dma("sp", vf.res.name, lambda e, t=t, vf=vf: e.dma_start(out=Sc.VF[t], in_=vf.ap.rearrange("p h d -> p (h d)")), reads=[vf.res])
        P.dma("sp", vsw.res.name, lambda e, t=t, vsw=vsw: e.dma_start(out=Sc.VSW[t], in_=vsw.ap.rearrange("p h d -> p (h d)")), reads=[vsw.res])
        if j != 3:
            continue
        rp = rope[q % 2]
        et = etc[q % 2]
        P.dma("sp", rp.res.name, lambda e, q=q, rp=rp: e.dma_start(out=rp.ap, in_=I.rope.rearrange("p (a t) -> p a t", a=4)[:, :, q * 512:(q + 1) * 512]), writes=[rp.res])
        P.dma("sp", et.res.name, lambda e, q=q, et=et: e.dma_start(out=et.ap, in_=I.etab[:, q * 512:(q + 1) * 512]), writes=[et.res])
        if 'X' not in k.skip:
          P.dma("sp", xc.res.name, lambda e, q=q, xc=xc: e.dma_start(out=Sc.XT[:, :, q * 512:(q + 1) * 512].rearrange("c p t -> p c t"), in_=xc.ap), reads=XTr[q % 2])
        for (nm, idx, cb, ru, scale) in FM_UNITS:
            pu = PSU[ucount % 2]
            f = fm[ucount % 4]
            ucount += 1
            for c in range(8):
                P.op("pe", lambda e, c=c, pu=pu, cb=cb, xc=xc: e.matmul(pu.ap, lhsT=W1.ap[:, c, cb:cb + 128], rhs=xc.ap[:, c, :], start=(c == 0), stop=(c == 7)),
                     reads=XTr[q % 2] + [W1r[c]], writes=[pu.res])
            P.op("act", lambda a, f=f, pu=pu, scale=scale: a.activation(out=f.ap, in_=pu.ap, func=AF.Copy, scale=scale), reads=[pu.res], writes=[f.res])
            if nm == "KSL" and 'E' not in k.skip:
                P.op("act", lambda g, f=f, et=et: g.copy(out=f.ap[64:128], in_=et.ap[64:128]), reads=[et.res], writes=[f.res])
            if ru is not None and 'R' not in k.skip:
                pr = PSR[rcount % 2]
                a1, a2 = rt1[rcount % 2], rt2[rcount % 2]
                rcount += 1
                for c in range(8):
                    P.op("pe", lambda e, c=c, pr=pr, ru=ru, xc=xc: e.matmul(pr.ap[0:32, :], lhsT=Wr.ap[:, c, ru, :], rhs=xc.ap[:, c, :], start=(c == 0), stop=(c == 7)),
                         reads=XTr[q % 2] + [Wrr[c]], writes=[pr.res])
                ti = 0 if scale != 1.0 else 2
                P.op("dve", lambda v, a1=a1, pu=pu, rp=rp, ti=ti: v.tensor_tensor(out=a1.ap[0:16], in0=pu.ap[0:16, :], in1=rp.ap[0:16, ti, :], op=ALU.mult), reads=[pu.res, rp.res], writes=[a1.res])
                P.op("dve", lambda v, a2=a2, pr=pr, rp=rp, ti=ti: v.tensor_tensor(out=a2.ap[0:16], in0=pr.ap[0:16, :], in1=rp.ap[0:16, ti + 1, :], op=ALU.mult), reads=[pr.res, rp.res], writes=[a2.res])
                P.op("dve", lambda g, f=f, a1=a1, a2=a2: g.tensor_tensor(out=f.ap[0:16], in0=a1.ap[0:16], in1=a2.ap[0:16], op=ALU.add), reads=[a1.res, a2.res], writes=[f.res])
            dst = getattr(Sc, nm)
            if 'U' not in k.skip:
              P.dma("sp", f.res.name, lambda e, f=f, dst=dst, idx=idx, q=q: e.dma_start(out=dst[idx, :, q * 512:(q + 1) * 512], in_=f.ap), reads=[f.res])
    P.op("dve", lambda v: v.tensor_tensor(out=zraw.ap, in0=zraw.ap, in1=k.fb.ap.unsqueeze(1).to_broadcast([128, NT, 8]), op=ALU.add), reads=[zraw.res, k.fb.res], writes=[zraw.res])
    P.op("act", lambda a: a.activation(out=zraw.ap, in_=zraw.ap, func=AF.Exp, scale=-1.0), reads=[zraw.res], writes=[zraw.res])
    P.op("act", lambda a: a.activation(out=zraw.ap, in_=zraw.ap, func=AF.Ln, bias=1.0, scale=1.0), reads=[zraw.res], writes=[zraw.res])
    P.op("dve", lambda v: v.tensor_scalar(out=k.logf.ap, in0=zraw.ap, scalar1=-1.0, scalar2=None, op0=ALU.mult), reads=[zraw.res], writes=[k.logf.res])
    P.op("act", lambda a: a.activation(out=k.gate.ap, in_=graw.ap, func=AF.Sigmoid), reads=[graw.res], writes=[k.gate.res])
    P.barrier()
    A.release(m0)


def phase_b(k):
    P, A, I, Sc, PS = k.P, k.A, k.I, k.Sc, k.PS
    m0 = A.mark()
    logf2 = k.logf.ap.rearrange("p a b -> p (a b)")
    tot = A.alloc([NT, 8], F32, "tot")
    pref = A.alloc([NT, 8], F32, "pref")
    negc = A.alloc([NT, 8], F32, "negc")
    bias = A.alloc([8, 8, NT], F32, "bias")
    P.op("pe", lambda e: e.matmul(PS[0].ap[:, 0:256], lhsT=k.trif.ap[:, 1, :], rhs=logf2, start=True, stop=True), reads=[k.trif.res, k.logf.res], writes=[PS[0].res])
    P.op("pe", lambda e: e.matmul(PS[1].ap[:, 0:256], lhsT=k.trif.ap[:, 0, :], rhs=logf2, start=True, stop=True), reads=[k.trif.res, k.logf.res], writes=[PS[1].res])
    P.op("dve", lambda v: v.tensor_copy(out=tot.ap.rearrange("p a b -> p (a b)"), in_=PS[0].ap[:, 0:256]), reads=[PS[0].res], writes=[tot.res])
    P.op("dve", lambda v: v.memset(pref.ap[:, 0, :], 0.0), writes=[pref.res])
    for j in range(1, NT):
        P.op("dve", lambda v, j=j: v.tensor_tensor(out=pref.ap[:, j, :], in0=pref.ap[:, j - 1, :], in1=tot.ap[:, j - 1, :], op=ALU.add), reads=[pref.res, tot.res], writes=[pref.res])
    P.op("dve", lambda v: v.scalar_tensor_tensor(out=negc.ap.rearrange("p a b -> p (a b)"), in0=PS[1].ap[:, 0:256], scalar=-1.0, in1=pref.ap.rearrange("p a b -> p (a b)"), op0=ALU.mult, op1=ALU.subtract),
         reads=[PS[1].res, pref.res], writes=[negc.res])
    for h in range(8):
        for q in range(8):
            P.op("dve", lambda v, h=h, q=q: v.tensor_scalar(out=bias.ap[:, h, q, :], in0=negc.ap[:, :, h], scalar1=pref.ap[:, 4 * q, h:h + 1], scalar2=None, op0=ALU.add),
                 reads=[negc.res, pref.res], writes=[bias.res])
    VF = A.alloc([NT, 520], BF16, "VFall")
    for i in range(4):
        P.dma("sp", "VFall", lambda e, i=i: e.dma_start(out=VF.ap[:, i * 8:(i + 1) * 8, :], in_=Sc.VF[i * 8:(i + 1) * 8].rearrange("t p f -> p t f")), writes=[VF.res])
    QK = [(A.alloc([S], BF16, f"QFh{i}"), A.alloc([S], BF16, f"KFh{i}")) for i in range(2)]
    yf = A.alloc([NT, 512], BF16, "yfox")
    yfr = [Res(f"yf{t}") for t in range(NT)]
    PT = [A.alloc([512], BF16, f"PT{i}") for i in range(4)]
    rz = [A.alloc([4], F32, f"rz{i}") for i in range(2)]
    cnt = 0
    oc = 0
    for h in range(8):
        Qh, Kh = QK[h % 2]
        P.dma("sp", Qh.res.name, lambda e, h=h, Qh=Qh: e.dma_start(out=Qh.ap, in_=Sc.QF[h]), writes=[Qh.res])
        P.dma("sp", Kh.res.name, lambda e, h=h, Kh=Kh: e.dma_start(out=Kh.ap, in_=Sc.KF[h]), writes=[Kh.res])
        for q in range(8):
            OUT = PS[6 + oc % 2]
            rzt = rz[oc % 2]
            oc += 1
            P.op("dve", lambda v, OUT=OUT: v.memset(OUT.ap[:, 0:260], 0.0), writes=[OUT.res])
            def qk_b(kt, ST, Kh=Kh, Qh=Qh, q=q):
                c0 = max(kt - 4 * q, 0) * 128
                P.op("pe", lambda e: e.matmul(ST.ap[:, c0:512], lhsT=Kh.ap[0:64, kt * 128:(kt + 1) * 128], rhs=Qh.ap[0:64, q * 512 + c0:(q + 1) * 512], start=True, stop=True),
                     reads=[Kh.res, Qh.res], writes=[ST.res])

            def rest_b(kt, ST, pt, OUT=OUT, h=h, q=q):
                j = kt - 4 * q
                c0 = max(j, 0) * 128
                P.op("act", lambda a: a.activation(out=pt.ap[:, c0:512], in_=ST.ap[:, c0:512], func=AF.Exp, bias=bias.ap[:, h, q, kt:kt + 1], scale=1.0),
                     reads=[ST.res, bias.res], writes=[pt.res])
                if j >= 0:
                    P.op("pool", lambda g: g.tensor_tensor(out=pt.ap[:, c0:c0 + 128], in0=pt.ap[:, c0:c0 + 128], in1=k.trib.ap[:, 0, :], op=ALU.mult),
                         reads=[pt.res, k.trib.res], writes=[pt.res])
                for ql in range(max(j, 0), 4):
                    P.op("pe", lambda e, ql=ql: e.matmul(OUT.ap[:, ql * 65:(ql + 1) * 65], lhsT=pt.ap[:, ql * 128:(ql + 1) * 128], rhs=VF.ap[:, kt, h * 65:(h + 1) * 65], start=False, stop=False, skip_group_check=True),
                         reads=[pt.res, VF.res], writes=[OUT.res])

            nk = 4 * q + 4
            slots = [(PS[(cnt + i) % 4], PT[(cnt + i) % 4]) for i in range(nk)]
            cnt += nk
            qk_b(0, slots[0][0])
            for kt in range(nk):
                if kt + 1 < nk:
                    qk_b(kt + 1, slots[kt + 1][0])
                rest_b(kt, slots[kt][0], slots[kt][1])
            P.op("dve", lambda v, OUT=OUT, rzt=rzt: v.reciprocal(out=rzt.ap, in_=OUT.ap[:, 0:260].rearrange("p (a b) -> p a b", b=65)[:, :, 64]), reads=[OUT.res], writes=[rzt.res])
            for ql in range(4):
                t = 4 * q + ql
                P.op("dve", lambda v, OUT=OUT, rzt=rzt, ql=ql, t=t, h=h: v.tensor_scalar(out=yf.ap[:, t, h * 64:(h + 1) * 64], in0=OUT.ap[:, ql * 65:ql * 65 + 64], scalar1=rzt.ap[:, ql:ql + 1], scalar2=None, op0=ALU.mult),
                     reads=[OUT.res, rzt.res], writes=[yfr[t]])
    for i in range(4):
        P.dma("sp", "yfox", lambda e, i=i: e.dma_start(out=Sc.YF[i * 8:(i + 1) * 8].rearrange("t p f -> p t f"), in_=yf.ap[:, i * 8:(i + 1) * 8, :]), reads=yfr[i * 8:(i + 1) * 8])
    P.barrier()
    A.release(m0)


def phase_c(k):
    P, A, I, Sc, PS = k.P, k.A, k.I, k.Sc, k.PS
    m0 = A.mark()
    kcmpT = [A.alloc([256], BF16, f"kcmpT{g}") for g in range(2)]
    VCa = A.alloc([2, 2, 129], BF16, "VCa")
    cmpmask = A.alloc([2, S], BF16, "cmpmask")
    selbias = A.alloc([NT, 64], F32, "selbias")
    ovaug = A.alloc([2, 65], BF16, "ovaug")
    VSW = A.alloc([NT, 260], BF16, "VSWall")
    P.dma("sp", "cC", lambda e: e.dma_start(out=cmpmask.ap, in_=I.cmpmask.rearrange("p (a t) -> p a t", a=2)), writes=[cmpmask.res])
    P.dma("sp", "cC", lambda e: e.dma_start(out=selbias.ap, in_=I.selbias.rearrange("p (a t) -> p a t", a=NT)), writes=[selbias.res])
    P.dma("sp", "cC", lambda e: e.dma_start(out=ovaug.ap, in_=I.ovaug.rearrange("p (a t) -> p a t", a=2)), writes=[ovaug.res])
    for i in range(4):
        P.dma("sp", "cC", lambda e, i=i: e.dma_start(out=VSW.ap[:, i * 8:(i + 1) * 8, :], in_=Sc.VSW[i * 8:(i + 1) * 8].rearrange("t p f -> p t f")), writes=[VSW.res])
    m1 = A.mark()
    w1s = A.alloc([2, 32, 128], F32, "w1s")
    w1b = A.alloc([2, 32, 128], BF16, "w1b")
    poss = A.alloc([2, 32], F32, "poss")
    posb = A.alloc([2, 32], BF16, "posb")
    w2s = A.alloc([2, 64], F32, "w2s")
    w2b = A.alloc([2, 64], BF16, "w2b")
    w2r = A.alloc([16], BF16, "w2r")
    ropec = A.alloc([2, 256], F32, "ropec")
    cst = A.alloc([2], F32, "cst")
    SRC = [[A.alloc([S], BF16, f"src{j}{g}") for g in range(2)] for j in range(2)]
    hidT = [A.alloc([256], BF16, f"hidT{i}") for i in range(2)]
    ct1 = A.alloc([256], F32, "ct1")
    ct2 = A.alloc([256], F32, "ct2")
    P.dma("sp", "cC", lambda e: e.dma_start(out=w1s.ap, in_=I.w1.rearrange("p (a b c) -> p a b c", a=2, b=32)), writes=[w1s.res])
    P.dma("sp", "cC", lambda e: e.dma_start(out=poss.ap, in_=I.pos.rearrange("p (a b) -> p a b", a=2)), writes=[poss.res])
    P.dma("sp", "cC", lambda e: e.dma_start(out=w2s.ap, in_=I.w2.rearrange("p (a b) -> p a b", a=2)), writes=[w2s.res])
    P.dma("sp", "cC", lambda e: e.dma_start(out=ropec.ap, in_=I.ropec.rearrange("p (a b) -> p a b", a=2)), writes=[ropec.res])
    for j in range(2):
        for g in range(2):
            src = (Sc.KC, Sc.VC)[j]
            P.dma("sp", "cC", lambda e, j=j, g=g, src=src: e.dma_start(out=SRC[j][g].ap, in_=src[g]), writes=[SRC[j][g].res])
    P.barrier()
    P.op("dve", lambda v: v.tensor_copy(out=w1b.ap, in_=w1s.ap), reads=[w1s.res], writes=[w1b.res])
    P.op("dve", lambda v: v.tensor_copy(out=posb.ap, in_=poss.ap), reads=[poss.res], writes=[posb.res])
    P.op("dve", lambda v: v.tensor_copy(out=w2b.ap, in_=w2s.ap), reads=[w2s.res], writes=[w2b.res])
    P.op("dve", lambda v: v.tensor_scalar(out=w2r.ap[:, 0:8], in0=w2s.ap[:, 0, 8:16], scalar1=-1.0, scalar2=None, op0=ALU.mult), reads=[w2s.res], writes=[w2r.res])
    P.op("dve", lambda v: v.tensor_copy(out=w2r.ap[:, 8:16], in_=w2s.ap[:, 0, 0:8]), reads=[w2s.res], writes=[w2r.res])
    P.op("pool", lambda g_: g_.memset(VCa.ap, 0.0), writes=[VCa.res])
    for g in range(2):
        P.op("pool", lambda g_, g=g: g_.memset(kcmpT[g].ap, 0.0), writes=[kcmpT[g].res])
    for i in range(2):
        P.op("pool", lambda g_, i=i: g_.memset(hidT[i].ap, 0.0), writes=[hidT[i].res])
    for j in range(2):
        for l in range(32):
            P.op("pe", lambda e, j=j, l=l: e.matmul(PS[7].ap[:, j:j + 1], lhsT=w1b.ap[0:64, j, l, :], rhs=posb.ap[0:64, j, l:l + 1], start=(l == 0), stop=(l == 31)),
                 reads=[w1b.res, posb.res], writes=[PS[7].res])
    P.op("dve", lambda v: v.tensor_copy(out=cst.ap, in_=PS[7].ap[:, 0:2]), reads=[PS[7].res], writes=[cst.res])
    cc = 0
    for j in range(2):
        for g in range(2):
            hp = PS[cc % 2]
            hT = hidT[cc % 2]
            cc += 1
            sv = SRC[j][g].ap[0:64, :].rearrange("p (n s) -> p n s", s=16)
            for l in range(32):
                rhs = sv[:, 0:255, l] if l < 16 else sv[:, 1:256, l - 16]
                P.op("pe", lambda e, hp=hp, j=j, l=l, rhs=rhs: e.matmul(hp.ap[:, 0:255], lhsT=w1b.ap[0:64, j, l, :], rhs=rhs, start=(l == 0), stop=(l == 31)),
                     reads=[w1b.res, SRC[j][g].res], writes=[hp.res])
            P.op("act", lambda a, hp=hp, hT=hT, j=j: a.activation(out=hT.ap[:, 0:255], in_=hp.ap[:, 0:255], func=AF.Gelu, bias=cst.ap[:, j:j + 1], scale=1.0),
                 reads=[hp.res, cst.res], writes=[hT.res])
            if j == 0:
                P.op("pe", lambda e, hT=hT: e.matmul(PS[2].ap[0:64, 0:255], lhsT=w2b.ap[:, 0, :], rhs=hT.ap[:, 0:255], start=True, stop=True), reads=[w2b.res, hT.res], writes=[PS[2].res])
                P.op("pe", lambda e, hT=hT: e.matmul(PS[3].ap[0:16, 0:255], lhsT=w2r.ap, rhs=hT.ap[:, 0:255], start=True, stop=True), reads=[w2r.res, hT.res], writes=[PS[3].res])
                P.op("act", lambda a, g=g: a.copy(out=kcmpT[g].ap[0:64, 0:255], in_=PS[2].ap[0:64, 0:255]), reads=[PS[2].res], writes=[kcmpT[g].res])
                P.op("dve", lambda v: v.tensor_tensor(out=ct1.ap[0:16, 0:255], in0=PS[2].ap[0:16, 0:255], in1=ropec.ap[0:16, 0, 0:255], op=ALU.mult), reads=[PS[2].res, ropec.res], writes=[ct1.res])
                P.op("dve", lambda v: v.tensor_tensor(out=ct2.ap[0:16, 0:255], in0=PS[3].ap[0:16, 0:255], in1=ropec.ap[0:16, 1, 0:255], op=ALU.mult), reads=[PS[3].res, ropec.res], writes=[ct2.res])
                P.op("dve", lambda v, g=g: v.tensor_tensor(out=kcmpT[g].ap[0:16, 0:255], in0=ct1.ap[0:16, 0:255], in1=ct2.ap[0:16, 0:255], op=ALU.add), reads=[ct1.res, ct2.res], writes=[kcmpT[g].res])
            else:
                for i in range(2):
                    nn = 128 if i == 0 else 127
                    P.op("pe", lambda e, hT=hT, i=i, nn=nn: e.matmul(PS[2 + i].ap[0:nn, 0:64], lhsT=hT.ap[:, i * 128:i * 128 + nn], rhs=w2b.ap[:, 1, :], start=True, stop=True),
                         reads=[w2b.res, hT.res], writes=[PS[2 + i].res])
                    P.op("act", lambda a, g=g, i=i, nn=nn: a.copy(out=VCa.ap[0:nn, i, g, 0:64], in_=PS[2 + i].ap[0:nn, 0:64]), reads=[PS[2 + i].res], writes=[VCa.res])
                    P.op("dve", lambda v, g=g, i=i: v.tensor_copy(out=VCa.ap[:, i, g, 64:129], in_=ovaug.ap[:, i, :]), reads=[ovaug.res], writes=[VCa.res])
    P.barrier()
    A.release(m1)
    if k.dbgC is not None:
        for g in range(2):
            P.dma("sp", "dbg", lambda e, g=g: e.dma_start(out=k.dbgC[0][:, g * 256:(g + 1) * 256], in_=kcmpT[g].ap), reads=[kcmpT[g].res])
        P.dma("sp", "dbg", lambda e: e.dma_start(out=k.dbgC[1], in_=VCa.ap.rearrange("p a b c -> p (a b c)")), reads=[VCa.res])
    KE = A.alloc([S], BF16, "KE")
    KW = A.alloc([S], BF16, "KW")
    QN = [[A.alloc([512], BF16, f"QN{i}_{hl}") for hl in range(4)] for i in range(2)]
    _DBG['QN'] = QN
    imp = A.alloc([4, 64], F32, "imp")
    yacc = [A.alloc([4, 256], F32, f"yacc{i}") for i in range(2)]
    ystg = [A.alloc([4, 256], BF16, f"ystg{i}") for i in range(2)]
    PT = [A.alloc([512], BF16, f"PTc{i}") for i in range(4)]
    rz = [A.alloc([4], F32, f"rzc{i}") for i in range(2)]
    cmb = [A.alloc([4], F32, f"cmb{i}") for i in range(2)]
    sc = A.alloc([64], F32, "sc")
    wk = A.alloc([64], F32, "wk")
    m8 = A.alloc([16], F32, "m8")
    pen2 = [A.alloc([2, 64], F32, f"pen2_{i}") for i in range(2)]
    STB = [PS[0], PS[1], PS[2]]
    OUTB = [PS[3], PS[4]]
    OUT2 = PS[5]
    PSM = PS[6]
    st_c = 0
    oc = 0

    def evac(OUT, t0, h, br, ya, hl, first, OUT2=None):
        nonlocal oc
        rzt, cb = rz[oc % 2], cmb[oc % 2]
        zv = OUT.ap[:, 0:260].rearrange("p (a b) -> p a b", b=65)[:, :, 64]
        if br == 0:
            P.op("dve", lambda v: v.tensor_scalar(out=rzt.ap, in0=zv, scalar1=1e-30, scalar2=None, op0=ALU.max), reads=[OUT.res], writes=[rzt.res])
            P.op("dve", lambda v: v.reciprocal(out=rzt.ap, in_=rzt.ap), reads=[rzt.res], writes=[rzt.res])
        else:
            P.op("dve", lambda v: v.reciprocal(out=rzt.ap, in_=zv), reads=[OUT.res], writes=[rzt.res])
        P.op("dve", lambda v: v.tensor_tensor(out=cb.ap, in0=rzt.ap, in1=k.gate.ap[:, t0:t0 + 4, h * 3 + br], op=ALU.mult), reads=[rzt.res, k.gate.res], writes=[cb.res])
        for ql in range(4):
            if first:
                P.op("dve", lambda v, ql=ql: v.tensor_scalar(out=ya.ap[:, ql, hl * 64:(hl + 1) * 64], in0=OUT.ap[:, ql * 65:ql * 65 + 64], scalar1=cb.ap[:, ql:ql + 1], scalar2=None, op0=ALU.mult),
                     reads=[OUT.res, cb.res], writes=[ya.res])
            else:
                P.op("dve", lambda v, ql=ql: v.scalar_tensor_tensor(out=ya.ap[:, ql, hl * 64:(hl + 1) * 64], in0=OUT.ap[:, ql * 65:ql * 65 + 64], scalar=cb.ap[:, ql:ql + 1], in1=ya.ap[:, ql, hl * 64:(hl + 1) * 64], op0=ALU.mult, op1=ALU.add),
                     reads=[OUT.res, cb.res, ya.res], writes=[ya.res])
            if OUT2 is not None:
                P.op("dve", lambda v, ql=ql: v.scalar_tensor_tensor(out=imp.ap[:, ql, :], in0=OUT2.ap[:, ql * 64:(ql + 1) * 64], scalar=rzt.ap[:, ql:ql + 1], in1=imp.ap[:, ql, :], op0=ALU.mult, op1=ALU.add),
                     reads=[OUT2.res, rzt.res, imp.res], writes=[imp.res])

    def pv(OUT, pt, kt, voff, qls, OUT2=None, vt=None):
        for ql in qls:
            if vt is None:
                rhs = VSW.ap[:, kt, voff:voff + 65]
                rr = VSW.res
            else:
                rhs = vt[0]
                rr = VCa.res
            P.op("pe", lambda e, ql=ql, rhs=rhs: e.matmul(OUT.ap[:, ql * 65:(ql + 1) * 65], lhsT=pt.ap[:, ql * 128:(ql + 1) * 128], rhs=rhs, start=False, stop=False, skip_group_check=True),
                 reads=[pt.res, rr], writes=[OUT.res])
            if OUT2 is not None:
                P.op("pe", lambda e, ql=ql: e.matmul(OUT2.ap[:, ql * 64:(ql + 1) * 64], lhsT=pt.ap[:, ql * 128:(ql + 1) * 128], rhs=vt[1], start=False, stop=False, skip_group_check=True),
                     reads=[pt.res, VCa.res], writes=[OUT2.res])

    def do_chunk(g, q):
        nonlocal st_c, oc
        if True:
            t0 = 4 * q
            qn = QN[q % 2]
            ya = yacc[q % 2]
            for hl in range(4):
                P.dma("sp", qn[hl].res.name, lambda e, hl=hl, g=g, q=q, qn=qn: e.dma_start(out=qn[hl].ap, in_=Sc.QN[4 * g + hl][:, q * 512:(q + 1) * 512]), writes=[qn[hl].res])
            P.op("pool", lambda g_: g_.memset(imp.ap, 0.0), writes=[imp.res])
            for hl in range(4):
                h = 4 * g + hl
                OUT = OUTB[oc % 2]
                P.op("dve", lambda v, OUT=OUT: v.memset(OUT.ap[:, 0:260], 0.0), writes=[OUT.res])
                P.op("dve", lambda v: v.memset(OUT2.ap[:, 0:256], 0.0), writes=[OUT2.res])
                for i in range(2 if q >= 4 else 1):
                    ST = STB[st_c % 3]
                    pt = PT[st_c % 4]
                    st_c += 1
                    P.op("pe", lambda e, ST=ST, i=i, hl=hl: e.matmul(ST.ap, lhsT=kcmpT[g].ap[0:64, i * 128:(i + 1) * 128], rhs=qn[hl].ap[0:64, :], start=True, stop=True),
                         reads=[kcmpT[g].res, qn[hl].res], writes=[ST.res])
                    P.op("act", lambda a, ST=ST, pt=pt: a.activation(out=pt.ap, in_=ST.ap, func=AF.Exp), reads=[ST.res], writes=[pt.res])
                    P.op("pool", lambda g_, pt=pt, i=i, q=q: g_.tensor_tensor(out=pt.ap, in0=pt.ap, in1=cmpmask.ap[:, i, q * 512:(q + 1) * 512], op=ALU.mult), reads=[pt.res, cmpmask.res], writes=[pt.res])
                    pv(OUT, pt, None, None, range(4), OUT2=OUT2, vt=(VCa.ap[:, i, g, 0:65], VCa.ap[:, i, g, 65:129]))
                evac(OUT, t0, h, 0, ya, hl, True, OUT2=OUT2)
                oc += 1
            for ql in range(4):
                t = t0 + ql
                p2 = pen2[ql % 2]
                P.op("dve", lambda v, ql=ql, t=t: v.tensor_tensor(out=sc.ap, in0=imp.ap[:, ql, :], in1=selbias.ap[:, t, :], op=ALU.add), reads=[imp.res, selbias.res], writes=[sc.res])
                P.op("dve", lambda v: v.max(out=m8.ap[:, 0:8], in_=sc.ap), reads=[sc.res], writes=[m8.res])
                P.op("dve", lambda v: v.match_replace(out=wk.ap, in_to_replace=m8.ap[:, 0:8], in_values=sc.ap, imm_value=-1e30), reads=[sc.res, m8.res], writes=[wk.res])
                P.op("dve", lambda v: v.max(out=m8.ap[:, 8:16], in_=wk.ap), reads=[wk.res], writes=[m8.res])
                P.op("dve", lambda v, p2=p2: v.tensor_scalar(out=p2.ap, in0=sc.ap.unsqueeze(1).to_broadcast([128, 2, 64]), scalar1=m8.ap[:, 15:16], scalar2=NEG, op0=ALU.is_lt, op1=ALU.mult),
                     reads=[sc.res, m8.res], writes=[p2.res])
                P.op("pe", lambda e, p2=p2, ql=ql: e.transpose(out=PSM.ap[:, ql * 128:(ql + 1) * 128], in_=p2.ap.rearrange("p a b -> p (a b)"), identity=k.identf.ap),
                     reads=[p2.res, k.identf.res], writes=[PSM.res])
            for hl in range(4):
                if hl % 2 == 0:
                    P.op("act", lambda a, hl=hl: a.copy(out=qn[hl].ap[64:128, :], in_=PSM.ap[64:128, :]), reads=[PSM.res], writes=[qn[hl].res])
                else:
                    P.op("dve", lambda v, hl=hl: v.tensor_copy(out=qn[hl].ap[64:128, :], in_=PSM.ap[64:128, :]), reads=[PSM.res], writes=[qn[hl].res])
            for hl in range(4):
                h = 4 * g + hl
                OUT = OUTB[oc % 2]
                P.op("dve", lambda v, OUT=OUT: v.memset(OUT.ap[:, 0:260], 0.0), writes=[OUT.res])
                def qk_s(kt, ST, hl=hl):
                    c0 = max(kt - 4 * q, 0) * 128
                    P.op("pe", lambda e: e.matmul(ST.ap[:, c0:512], lhsT=KE.ap[:, kt * 128:(kt + 1) * 128], rhs=qn[hl].ap[:, c0:512], start=True, stop=True),
                         reads=[KE.res, qn[hl].res], writes=[ST.res])

                def rest_s(kt, ST, pt, OUT=OUT):
                    j = kt - 4 * q
                    c0 = max(j, 0) * 128
                    P.op("act", lambda a: a.activation(out=pt.ap[:, c0:512], in_=ST.ap[:, c0:512], func=AF.Exp), reads=[ST.res], writes=[pt.res])
                    if j >= 0:
                        P.op("pool", lambda g_: g_.tensor_tensor(out=pt.ap[:, c0:c0 + 128], in0=pt.ap[:, c0:c0 + 128], in1=k.trib.ap[:, 0, :], op=ALU.mult), reads=[pt.res, k.trib.res], writes=[pt.res])
                    pv(OUT, pt, kt, g * 65, range(max(j, 0), 4))

                nk = 4 * q + 4
                sl = [(STB[(st_c + i) % 3], PT[(st_c + i) % 4]) for i in range(nk)]
                st_c += nk
                qk_s(0, sl[0][0])
                for kt in range(nk):
                    if kt + 1 < nk:
                        qk_s(kt + 1, sl[kt + 1][0])
                    rest_s(kt, sl[kt][0], sl[kt][1])
                evac(OUT, t0, h, 1, ya, hl, False)
                oc += 1
            for hl in range(4):
                h = 4 * g + hl
                OUT = OUTB[oc % 2]
                P.op("dve", lambda v, OUT=OUT: v.memset(OUT.ap[:, 0:260], 0.0), writes=[OUT.res])
                def rng_w(kt):
                    lo = max(kt - 4 * q, 0)
                    hi = min(kt + 4 - 4 * q, 3)
                    return lo, hi, lo * 128, (hi + 1) * 128

                def qk_w(kt, ST, hl=hl):
                    lo, hi, c0, c1 = rng_w(kt)
                    P.op("pe", lambda e: e.matmul(ST.ap[:, c0:c1], lhsT=KW.ap[0:64, kt * 128:(kt + 1) * 128], rhs=qn[hl].ap[0:64, c0:c1], start=True, stop=True),
                         reads=[KW.res, qn[hl].res], writes=[ST.res])

                def rest_w(kt, ST, pt, OUT=OUT):
                    lo, hi, c0, c1 = rng_w(kt)
                    P.op("act", lambda a: a.activation(out=pt.ap[:, c0:c1], in_=ST.ap[:, c0:c1], func=AF.Exp), reads=[ST.res], writes=[pt.res])
                    if kt >= 4 * q:
                        b0 = (kt - 4 * q) * 128
                        P.op("pool", lambda g_: g_.tensor_tensor(out=pt.ap[:, b0:b0 + 128], in0=pt.ap[:, b0:b0 + 128], in1=k.trib.ap[:, 0, :], op=ALU.mult), reads=[pt.res, k.trib.res], writes=[pt.res])
                    if 0 <= kt + 4 - 4 * q <= 3:
                        b1 = (kt + 4 - 4 * q) * 128
                        P.op("pool", lambda g_: g_.tensor_tensor(out=pt.ap[:, b1:b1 + 128], in0=pt.ap[:, b1:b1 + 128], in1=k.trib.ap[:, 1, :], op=ALU.mult), reads=[pt.res, k.trib.res], writes=[pt.res])
                    pv(OUT, pt, kt, (2 + g) * 65, range(lo, hi + 1))

                kts = list(range(max(4 * q - 4, 0), 4 * q + 4))
                sl = [(STB[(st_c + i) % 3], PT[(st_c + i) % 4]) for i in range(len(kts))]
                st_c += len(kts)
                qk_w(kts[0], sl[0][0])
                for i, kt in enumerate(kts):
                    if i + 1 < len(kts):
                        qk_w(kts[i + 1], sl[i + 1][0])
                    rest_w(kt, sl[i][0], sl[i][1])
                evac(OUT, t0, h, 2, ya, hl, False)
                oc += 1
            ys = ystg[q % 2]
            P.op("act", lambda a, ys=ys, ya=ya: a.copy(out=ys.ap, in_=ya.ap), reads=[ya.res], writes=[ys.res])
            P.dma("sp", ys.res.name, lambda e, ys=ys, g=g, t0=t0: e.dma_start(out=Sc.YN[t0:t0 + 4, :, g * 256:(g + 1) * 256].rearrange("t p f -> p t f"), in_=ys.ap), reads=[ys.res])

    for g in range(2):
        P.dma("sp", "KE", lambda e, g=g: e.dma_start(out=KE.ap, in_=Sc.KSL[g]), writes=[KE.res])
        P.dma("sp", "KW", lambda e, g=g: e.dma_start(out=KW.ap, in_=Sc.KWN[g]), writes=[KW.res])
        for q in range(8):
            do_chunk(g, q)
    P.barrier()
    A.release(m0)


def phase_d(k):
    P, A, I, Sc, PS = k.P, k.A, k.I, k.Sc, k.PS
    m0 = A.mark()
    WG = A.alloc([8, 2048], BF16, "WG")
    Wb = A.alloc([2, 4, 1024], BF16, "Wb")
    Wo = A.alloc([8, 1024], BF16, "Wo")
    gmix = A.alloc([8], F32, "gmixd")
    wst = [A.alloc([2048], F32, f"wstd{i}") for i in range(2)]
    WGr = [Res(f"WG{c}") for c in range(8)]
    Wbr = [Res(f"Wb{c}") for c in range(8)]
    Wor = [Res(f"Wo{c}") for c in range(8)]
    P.dma("sp", "cD", lambda e: e.dma_start(out=gmix.ap, in_=I.gmix), writes=[gmix.res])
    P.barrier()
    n = 0
    for c in range(8):
        ws = wst[n % 2]; n += 1
        P.dma("sp", ws.res.name, lambda e, c=c, ws=ws: e.dma_start(out=ws.ap, in_=I.w_in[c * 128:(c + 1) * 128, C_MG:C_MG + 2048]), writes=[ws.res])
        P.op("dve", lambda v, c=c, ws=ws: v.tensor_scalar(out=WG.ap[:, c, :], in0=ws.ap, scalar1=gmix.ap[:, c:c + 1], scalar2=None, op0=ALU.mult), reads=[ws.res, gmix.res], writes=[WGr[c]])
    for b in range(2):
        for c in range(4):
            ws = wst[n % 2]; n += 1
            P.dma("sp", ws.res.name, lambda e, b=b, c=c, ws=ws: e.dma_start(out=ws.ap[:, 0:1024], in_=I.wbr[b, c * 128:(c + 1) * 128, :]), writes=[ws.res])
            P.op("dve", lambda v, b=b, c=c, ws=ws: v.tensor_copy(out=Wb.ap[:, b, c, :], in_=ws.ap[:, 0:1024]), reads=[ws.res], writes=[Wbr[b * 4 + c]])
    for c in range(8):
        ws = wst[n % 2]; n += 1
        P.dma("sp", ws.res.name, lambda e, c=c, ws=ws: e.dma_start(out=ws.ap[:, 0:1024], in_=I.wout[c * 128:(c + 1) * 128, :]), writes=[ws.res])
        P.op("dve", lambda v, c=c, ws=ws: v.tensor_copy(out=Wo.ap[:, c, :], in_=ws.ap[:, 0:1024]), reads=[ws.res], writes=[Wor[c]])
    xs = [A.alloc([D], F32, f"xd{i}") for i in range(2)]
    XTt = [A.alloc([8, 128], BF16, f"XTt{i}") for i in range(2)]
    yfn = [A.alloc([2, 512], BF16, f"yfn{i}") for i in range(2)]
    yT = A.alloc([8, 128], BF16, "yT")
    sg = [A.alloc([512], F32, f"sg{i}") for i in range(2)]
    tt = [A.alloc([512], F32, f"tt{i}") for i in range(2)]
    mg = A.alloc([D], BF16, "mg")
    mT = A.alloc([8, 128], BF16, "mT")
    h2 = [A.alloc([D], F32, f"h2_{i}") for i in range(2)]
    PTR = [PS[0], PS[1]]
    PU = [PS[2], PS[3]]
    PG = [PS[4], PS[5]]
    PO = [PS[6], PS[7]]

    def do_tile(t):
        x_t, xt_t, y_t, h_t = xs[t % 2], XTt[t % 2], yfn[t % 2], h2[t % 2]
        P.dma("sp", x_t.res.name, lambda e: e.dma_start(out=x_t.ap, in_=I.x[t * 128:(t + 1) * 128, :]), writes=[x_t.res])
        P.dma("sp", xt_t.res.name, lambda e: e.dma_start(out=xt_t.ap, in_=Sc.XT[:, :, t * 128:(t + 1) * 128].rearrange("c p t -> p c t")), writes=[xt_t.res])
        P.dma("sp", y_t.res.name, lambda e: e.dma_start(out=y_t.ap[:, 0, :], in_=Sc.YF[t]), writes=[y_t.res])
        P.dma("sp", y_t.res.name, lambda e: e.dma_start(out=y_t.ap[:, 1, :], in_=Sc.YN[t]), writes=[y_t.res])
        ptb = PTR[0].ap.bitcast(BF16)
        for b in range(2):
            for c in range(4):
                P.op("pe", lambda e, b=b, c=c: e.transpose(out=ptb[:, (b * 4 + c) * 128:(b * 4 + c + 1) * 128], in_=y_t.ap[:, b, c * 128:(c + 1) * 128], identity=k.identb.ap),
                     reads=[y_t.res, k.identb.res], writes=[PTR[0].res])
        P.op("act", lambda a: a.copy(out=yT.ap, in_=ptb.rearrange("p (c t) -> p c t", c=8)), reads=[PTR[0].res], writes=[yT.res])
        for half in range(2):
            for b in range(2):
                for c in range(4):
                    P.op("pe", lambda e, b=b, c=c, half=half: e.matmul(PU[b].ap, lhsT=yT.ap[:, b * 4 + c, :], rhs=Wb.ap[:, b, c, half * 512:(half + 1) * 512], start=(c == 0), stop=(c == 3)),
                         reads=[yT.res, Wbr[b * 4 + c]], writes=[PU[b].res])
                for c in range(8):
                    P.op("pe", lambda e, b=b, c=c, half=half: e.matmul(PG[b].ap, lhsT=xt_t.ap[:, c, :], rhs=WG.ap[:, c, b * 1024 + half * 512:b * 1024 + (half + 1) * 512], start=(c == 0), stop=(c == 7)),
                         reads=[xt_t.res, WGr[c]], writes=[PG[b].res])
                P.op("act", lambda a, b=b: a.activation(out=sg[b].ap, in_=PG[b].ap, func=AF.Sigmoid), reads=[PG[b].res], writes=[sg[b].res])
                P.op("dve", lambda v, b=b: v.tensor_tensor(out=tt[b].ap, in0=PU[b].ap, in1=sg[b].ap, op=ALU.mult), reads=[PU[b].res, sg[b].res], writes=[tt[b].res])
            P.op("dve", lambda v, half=half: v.tensor_tensor(out=mg.ap[:, half * 512:(half + 1) * 512], in0=tt[0].ap, in1=tt[1].ap, op=ALU.add), reads=[tt[0].res, tt[1].res], writes=[mg.res])
        ptb1 = PTR[1].ap.bitcast(BF16)
        for c in range(8):
            P.op("pe", lambda e, c=c: e.transpose(out=ptb1[:, c * 128:(c + 1) * 128], in_=mg.ap[:, c * 128:(c + 1) * 128], identity=k.identb.ap), reads=[mg.res, k.identb.res], writes=[PTR[1].res])
        P.op("act", lambda a: a.copy(out=mT.ap, in_=ptb1.rearrange("p (c t) -> p c t", c=8)), reads=[PTR[1].res], writes=[mT.res])
        for half in range(2):
            for c in range(8):
                P.op("pe", lambda e, c=c, half=half: e.matmul(PO[half].ap, lhsT=mT.ap[:, c, :], rhs=Wo.ap[:, c, half * 512:(half + 1) * 512], start=(c == 0), stop=(c == 7)),
                     reads=[mT.res, Wor[c]], writes=[PO[half].res])
            P.op("dve", lambda v, half=half: v.tensor_tensor(out=h_t.ap[:, half * 512:(half + 1) * 512], in0=PO[half].ap, in1=x_t.ap[:, half * 512:(half + 1) * 512], op=ALU.add), reads=[PO[half].res, x_t.res], writes=[h_t.res])
        P.dma("sp", h_t.res.name, lambda e: e.dma_start(out=Sc.H2[t * 128:(t + 1) * 128, :], in_=h_t.ap), reads=[h_t.res])

    for t in range(NT):
        do_tile(t)
    P.barrier()
    A.release(m0)


NB_G = 16


def phase_e(k):
    P, A, I, Sc, PS = k.P, k.A, k.I, k.Sc, k.PS
    out = k.out
    m0 = A.mark()
    Wq = A.alloc([8, 2048], BF16, "Wq")
    Wqr = [Res(f"Wq{c}") for c in range(8)]
    skb = A.alloc([16, 128], BF16, "skb")
    gq = A.alloc([8], F32, "gq")
    gfb = A.alloc([D], F32, "gfb")
    gfin = A.alloc([D], F32, "gfin")
    iota = A.alloc([16], F32, "iota")
    wst = [A.alloc([2048], F32, f"wste{i}") for i in range(2)]
    P.dma("sp", "cE", lambda e: e.dma_start(out=gq.ap, in_=I.gffn8), writes=[gq.res])
    P.dma("sp", "cE", lambda e: e.dma_start(out=gfb.ap, in_=I.gffn[0:1, :].partition_broadcast(128)), writes=[gfb.res])
    P.dma("sp", "cE", lambda e: e.dma_start(out=gfin.ap, in_=I.gfin[0:1, :].partition_broadcast(128)), writes=[gfin.res])
    P.dma("sp", "cE", lambda e: e.dma_start(out=iota.ap, in_=I.iota16), writes=[iota.res])
    P.barrier()
    for c in range(8):
        ws = wst[c % 2]
        P.dma("sp", ws.res.name, lambda e, c=c, ws=ws: e.dma_start(out=ws.ap, in_=I.wq[c * 128:(c + 1) * 128, :]), writes=[ws.res])
        P.op("dve", lambda v, c=c, ws=ws: v.tensor_scalar(out=Wq.ap[:, c, :], in0=ws.ap, scalar1=gq.ap[:, c:c + 1], scalar2=None, op0=ALU.mult), reads=[ws.res, gq.res], writes=[Wqr[c]])
    P.dma("sp", wst[0].res.name, lambda e: e.dma_start(out=wst[0].ap, in_=I.skT), writes=[wst[0].res])
    P.op("dve", lambda v: v.tensor_copy(out=skb.ap.rearrange("p a b -> p (a b)"), in_=wst[0].ap), reads=[wst[0].res], writes=[skb.res])

    h2 = [A.alloc([D], F32, f"h2e{i}") for i in range(2)]
    junkb = A.alloc([D], BF16, "junkb")
    ssq = [A.alloc([4], F32, f"ssq{i}") for i in range(2)]
    xn2 = [A.alloc([D], F32, f"xn2_{i}") for i in range(2)]
    xnb = A.alloc([D], BF16, "xnb")
    x2T = A.alloc([8, 128], BF16, "x2T")
    qT = A.alloc([16, 128], BF16, "qT")
    sS = A.alloc([16, 128], F32, "sS")
    wk = A.alloc([128], F32, "wk_e")
    vals = A.alloc([8, 2, 16], F32, "vals")
    idxu = A.alloc([8, 2, 16], U32, "idxu")
    cand = A.alloc([16, 16], F32, "cand")
    wk2 = A.alloc([256], F32, "wk2")
    tops = A.alloc([8, 16], F32, "tops")
    posu = A.alloc([8, 16], U32, "posu")
    posf = A.alloc([8, 16], F32, "posf")
    aq = A.alloc([8, 16], F32, "aq")
    bq = A.alloc([8, 16], F32, "bq")
    i1f = A.alloc([8, 16], F32, "i1f")
    i2f = A.alloc([8, 16], F32, "i2f")
    oh = A.alloc([16, 16], F32, "oh")
    idx1 = A.alloc([8, 16], F32, "idx1")
    idx2 = A.alloc([8, 16], F32, "idx2")
    idxf = A.alloc([128], F32, "idxf")
    idxi = [A.alloc([128], U32, f"idxi{i}") for i in range(2)]
    negm = A.alloc([8], F32, "negm")
    ex = A.alloc([8, 16], F32, "ex")
    sm = A.alloc([8], F32, "sm")
    wgt = [A.alloc([8, 16], F32, f"wgt{i}") for i in range(2)]
    act = A.alloc([128], F32, "act")
    actr = [Res(f"act{i}") for i in range(128)]
    coef = A.alloc([128], F32, "coef")
    GB = [A.alloc([2 * D], BF16, f"gb{i}") for i in range(NB_G)]
    junkd = A.alloc([D], BF16, "junkd")
    cg = A.alloc([128], F32, "cg")
    cgr = [Res(f"cg{i}") for i in range(32)]
    coefr = [Res(f"coef{i}") for i in range(32)]
    DG = [A.alloc([128], BF16, f"dg{i}") for i in range(4)]
    junkf = A.alloc([D], F32, "junkf")
    hsum = A.alloc([D], F32, "hsum")
    ot = [A.alloc([D], F32, f"ot{i}") for i in range(2)]
    PTB = PS[0]
    PQ = [PS[1], PS[2]]
    PSS = [PS[3], PS[4]]
    ACC = [PS[5], PS[6]]
    gcnt = 0
    dcnt = 0

    def front(t):
        h_t, sq, x2 = h2[t % 2], ssq[t % 2], xn2[t % 2]
        ix, wg = idxi[t % 2], wgt[t % 2]
        P.dma("sp", h_t.res.name, lambda e: e.dma_start(out=h_t.ap, in_=Sc.H2[t * 128:(t + 1) * 128, :]), writes=[h_t.res])
        P.op("act", lambda a: a.activation(out=junkb.ap, in_=h_t.ap, func=AF.Square, accum_out=sq.ap[:, 0:1]), reads=[h_t.res], writes=[junkb.res, sq.res])
        P.op("act", lambda a: a.activation(out=sq.ap[:, 0:1], in_=sq.ap[:, 0:1], func=AF.Sqrt, scale=1.0 / D, bias=1e-6), reads=[sq.res], writes=[sq.res])
        P.op("dve", lambda v: v.reciprocal(out=sq.ap[:, 1:2], in_=sq.ap[:, 0:1]), reads=[sq.res], writes=[sq.res])
        P.op("dve", lambda v: v.scalar_tensor_tensor(out=x2.ap, in0=h_t.ap, scalar=sq.ap[:, 1:2], in1=gfb.ap, op0=ALU.mult, op1=ALU.mult), reads=[h_t.res, sq.res, gfb.res], writes=[x2.res])
        P.op("act", lambda a: a.activation(out=xnb.ap, in_=h_t.ap, func=AF.Copy, scale=sq.ap[:, 1:2]), reads=[h_t.res, sq.res], writes=[xnb.res])
        ptb = PTB.ap.bitcast(BF16)
        for c in range(8):
            P.op("pe", lambda e, c=c: e.transpose(out=ptb[:, c * 128:(c + 1) * 128], in_=xnb.ap[:, c * 128:(c + 1) * 128], identity=k.identb.ap), reads=[xnb.res, k.identb.res], writes=[PTB.res])
        P.op("act", lambda a: a.copy(out=x2T.ap, in_=ptb.rearrange("p (c t) -> p c t", c=8)), reads=[PTB.res], writes=[x2T.res])
        for ub in range(4):
            pq = PQ[ub % 2]
            for ul in range(4):
                u = ub * 4 + ul
                for c in range(8):
                    P.op("pe", lambda e, u=u, ul=ul, c=c, pq=pq: e.matmul(pq.ap[:, ul * 128:(ul + 1) * 128], lhsT=Wq.ap[:, c, u * 128:(u + 1) * 128], rhs=x2T.ap[:, c, :], start=(c == 0), stop=(c == 7)),
                         reads=[Wqr[c], x2T.res], writes=[pq.res])
            P.op("act", lambda a, ub=ub, pq=pq: a.copy(out=qT.ap[:, ub * 4:(ub + 1) * 4, :], in_=pq.ap.rearrange("p (a b) -> p a b", a=4)), reads=[pq.res], writes=[qT.res])
        for ub in range(4):
            pss = PSS[ub % 2]
            for ul in range(4):
                u = ub * 4 + ul
                P.op("pe", lambda e, u=u, ul=ul, pss=pss: e.matmul(pss.ap[:, ul * 128:(ul + 1) * 128], lhsT=qT.ap[:, u, :], rhs=skb.ap[:, u, :], start=True, stop=True),
                     reads=[qT.res, skb.res], writes=[pss.res])
            P.op("act", lambda a, ub=ub, pss=pss: a.copy(out=sS.ap[:, ub * 4:(ub + 1) * 4, :], in_=pss.ap.rearrange("p (a b) -> p a b", a=4)), reads=[pss.res], writes=[sS.res])

    def topk_gen(t):
        ix, wg = idxi[t % 2], wgt[t % 2]
        for h in range(8):
            for p in range(2):
                sp = sS.ap[:, 2 * h + p, :]
                v_ = vals.ap[:, h, p, :]
                i_ = idxu.ap[:, h, p, :]
                yield P.op("dve", lambda v, sp=sp, v_=v_: v.max(out=v_[:, 0:8], in_=sp), reads=[sS.res], writes=[vals.res])
                yield P.op("dve", lambda v, sp=sp, v_=v_, i_=i_: v.max_index(out=i_[:, 0:8], in_max=v_[:, 0:8], in_values=sp), reads=[sS.res, vals.res], writes=[idxu.res])
                yield P.op("dve", lambda v, sp=sp, v_=v_: v.match_replace(out=wk.ap, in_to_replace=v_[:, 0:8], in_values=sp, imm_value=-1e30), reads=[sS.res, vals.res], writes=[wk.res])
                yield P.op("dve", lambda v, v_=v_: v.max(out=v_[:, 8:16], in_=wk.ap), reads=[wk.res], writes=[vals.res])
                yield P.op("dve", lambda v, v_=v_, i_=i_: v.max_index(out=i_[:, 8:16], in_max=v_[:, 8:16], in_values=wk.ap), reads=[wk.res, vals.res], writes=[idxu.res])
            yield P.op("dve", lambda v, h=h: v.tensor_tensor(out=cand.ap, in0=vals.ap[:, h, 0, :].unsqueeze(2).to_broadcast([128, 16, 16]), in1=vals.ap[:, h, 1, :].unsqueeze(1).to_broadcast([128, 16, 16]), op=ALU.add),
                 reads=[vals.res], writes=[cand.res])
            c2 = cand.ap.rearrange("p a b -> p (a b)")
            yield P.op("dve", lambda v, h=h, c2=c2: v.max(out=tops.ap[:, h, 0:8], in_=c2), reads=[cand.res], writes=[tops.res])
            yield P.op("dve", lambda v, h=h, c2=c2: v.max_index(out=posu.ap[:, h, 0:8], in_max=tops.ap[:, h, 0:8], in_values=c2), reads=[cand.res, tops.res], writes=[posu.res])
            yield P.op("dve", lambda v, h=h, c2=c2: v.match_replace(out=wk2.ap, in_to_replace=tops.ap[:, h, 0:8], in_values=c2, imm_value=-1e30), reads=[cand.res, tops.res], writes=[wk2.res])
            yield P.op("dve", lambda v, h=h: v.max(out=tops.ap[:, h, 8:16], in_=wk2.ap), reads=[wk2.res], writes=[tops.res])
            yield P.op("dve", lambda v, h=h: v.max_index(out=posu.ap[:, h, 8:16], in_max=tops.ap[:, h, 8:16], in_values=wk2.ap), reads=[wk2.res, tops.res], writes=[posu.res])
        yield P.op("act", lambda v: v.copy(out=posf.ap, in_=posu.ap), reads=[posu.res], writes=[posf.res])
        yield P.op("pool", lambda v: v.tensor_scalar(out=aq.ap, in0=posf.ap, scalar1=0.0625, scalar2=0.53125, op0=ALU.mult, op1=ALU.add), reads=[posf.res], writes=[aq.res])
        yield P.op("pool", lambda v: v.tensor_scalar(out=aq.ap, in0=aq.ap, scalar1=8388608.0, scalar2=None, op0=ALU.add), reads=[aq.res], writes=[aq.res])
        yield P.op("pool", lambda v: v.tensor_scalar(out=aq.ap, in0=aq.ap, scalar1=-8388609.0, scalar2=None, op0=ALU.add), reads=[aq.res], writes=[aq.res])
        yield P.op("dve", lambda v: v.scalar_tensor_tensor(out=bq.ap, in0=aq.ap, scalar=-16.0, in1=posf.ap, op0=ALU.mult, op1=ALU.add), reads=[aq.res, posf.res], writes=[bq.res])
        yield P.op("act", lambda v: v.copy(out=i1f.ap, in_=idxu.ap[:, :, 0, :]), reads=[idxu.res], writes=[i1f.res])
        yield P.op("act", lambda v: v.copy(out=i2f.ap, in_=idxu.ap[:, :, 1, :]), reads=[idxu.res], writes=[i2f.res])
        for h in range(8):
            for (sel, src, dst) in ((aq, i1f, idx1), (bq, i2f, idx2)):
                yield P.op("dve", lambda v, h=h, sel=sel: v.tensor_tensor(out=oh.ap, in0=sel.ap[:, h, :].unsqueeze(2).to_broadcast([128, 16, 16]), in1=iota.ap.unsqueeze(1).to_broadcast([128, 16, 16]), op=ALU.is_equal),
                     reads=[sel.res, iota.res], writes=[oh.res])
                yield P.op("dve", lambda v, h=h, src=src: v.tensor_tensor(out=oh.ap, in0=oh.ap, in1=src.ap[:, h, :].unsqueeze(1).to_broadcast([128, 16, 16]), op=ALU.mult), reads=[oh.res, src.res], writes=[oh.res])
                yield P.op("dve", lambda v, h=h, dst=dst: v.tensor_reduce(out=dst.ap[:, h, :], in_=oh.ap, axis=mybir.AxisListType.X, op=ALU.add), reads=[oh.res], writes=[dst.res])
        yield P.op("dve", lambda v: v.scalar_tensor_tensor(out=idxf.ap, in0=idx1.ap.rearrange("p a b -> p (a b)"), scalar=128.0, in1=idx2.ap.rearrange("p a b -> p (a b)"), op0=ALU.mult, op1=ALU.add),
             reads=[idx1.res, idx2.res], writes=[idxf.res])
        yield P.op("act", lambda v: v.copy(out=ix.ap, in_=idxf.ap), reads=[idxf.res], writes=[ix.res])
        yield P.op("act", lambda v: v.mul(out=negm.ap, in_=tops.ap[:, :, 0], mul=-1.0), reads=[tops.res], writes=[negm.res])
        yield P.op("dve", lambda v: v.tensor_tensor(out=ex.ap, in0=tops.ap, in1=negm.ap.unsqueeze(2).to_broadcast([128, 8, 16]), op=ALU.add), reads=[tops.res, negm.res], writes=[ex.res])

    def frontC(t):
        wg = wgt[t % 2]
        P.op("act", lambda a: a.activation(out=ex.ap, in_=ex.ap, func=AF.Exp), reads=[ex.res], writes=[ex.res])
        P.op("dve", lambda v: v.tensor_reduce(out=sm.ap, in_=ex.ap, axis=mybir.AxisListType.X, op=ALU.add), reads=[ex.res], writes=[sm.res])
        P.op("dve", lambda v: v.reciprocal(out=sm.ap, in_=sm.ap), reads=[sm.res], writes=[sm.res])
        P.op("dve", lambda v: v.tensor_tensor(out=wg.ap, in0=ex.ap, in1=sm.ap.unsqueeze(2).to_broadcast([128, 8, 16]), op=ALU.mult), reads=[ex.res, sm.res], writes=[wg.res])

    def back(t, gen=None):
        nonlocal gcnt, dcnt
        h_t, sq, x2 = h2[t % 2], ssq[t % 2], xn2[t % 2]
        ix, wg = idxi[t % 2], wgt[t % 2]
        o_t = ot[t % 2]
        wg2 = wg.ap.rearrange("p a b -> p (a b)")
        P.op("act", lambda v: v.copy(out=hsum.ap, in_=h_t.ap), reads=[h_t.res], writes=[hsum.res])
        gbs = []
        for slot in range(128):
            gb = GB[gcnt % NB_G]
            gcnt += 1
            gbs.append(gb)
            P.dma("pool", gb.res.name, lambda g_, gb=gb, slot=slot: g_.indirect_dma_start(out=gb.ap, out_offset=None, in_=Sc.UVB[:, :], in_offset=bass.IndirectOffsetOnAxis(ap=ix.ap[:, slot:slot + 1], axis=0)),
                  reads=[ix.res], writes=[gb.res])
            P.op("dve", lambda v, gb=gb, slot=slot: v.scalar_tensor_tensor(out=junkd.ap, in0=gb.ap[:, 0:D], scalar=1.0, in1=x2.ap, op0=ALU.mult, op1=ALU.mult, accum_out=act.ap[:, slot:slot + 1]),
                 reads=[gb.res, x2.res], writes=[junkd.res, actr[slot]])
            if gen is not None:
                for _ in range(2 if slot % 2 == 0 else 1):
                    next(gen, None)
            if slot % 4 != 3:
                continue
            s0 = slot - 3
            gi = s0 // 4
            P.op("act", lambda a, s0=s0: a.activation(out=cg.ap[:, s0:s0 + 4], in_=act.ap[:, s0:s0 + 4], func=AF.Gelu), reads=actr[s0:s0 + 4], writes=[cgr[gi]])
            P.op("dve", lambda v, s0=s0: v.tensor_tensor(out=coef.ap[:, s0:s0 + 4], in0=cg.ap[:, s0:s0 + 4], in1=wg2[:, s0:s0 + 4], op=ALU.mult), reads=[cgr[gi], wg.res], writes=[coefr[gi]])
            for sl in range(s0, s0 + 4):
                dg = DG[dcnt % 4]
                dcnt += 1
                gbv = gbs[sl]
                P.op("act", lambda a, dg=dg, sl=sl: a.activation(out=dg.ap, in_=k.identf.ap, func=AF.Copy, scale=coef.ap[:, sl:sl + 1]), reads=[k.identf.res, coefr[gi]], writes=[dg.res])
                for half in range(2):
                    P.op("pe", lambda e, dg=dg, gbv=gbv, half=half, sl=sl: e.matmul(ACC[half].ap, lhsT=dg.ap, rhs=gbv.ap[:, D + half * 512:D + (half + 1) * 512], start=(sl == 0), stop=(sl == 127)),
                         reads=[dg.res, gbv.res], writes=[ACC[half].res])
        if gen is not None:
            for _ in gen:
                pass
        for half in range(2):
            P.op("dve", lambda v, half=half: v.tensor_tensor(out=hsum.ap[:, half * 512:(half + 1) * 512], in0=ACC[half].ap, in1=hsum.ap[:, half * 512:(half + 1) * 512], op=ALU.add), reads=[ACC[half].res, hsum.res], writes=[hsum.res])
        P.op("act", lambda a: a.activation(out=junkb.ap, in_=hsum.ap, func=AF.Square, accum_out=sq.ap[:, 2:3]), reads=[hsum.res], writes=[junkb.res, sq.res])
        P.op("act", lambda a: a.activation(out=sq.ap[:, 2:3], in_=sq.ap[:, 2:3], func=AF.Sqrt, scale=1.0 / D, bias=1e-6), reads=[sq.res], writes=[sq.res])
        P.op("dve", lambda v: v.reciprocal(out=sq.ap[:, 3:4], in_=sq.ap[:, 2:3]), reads=[sq.res], writes=[sq.res])
        P.op("dve", lambda v: v.scalar_tensor_tensor(out=o_t.ap, in0=hsum.ap, scalar=sq.ap[:, 3:4], in1=gfin.ap, op0=ALU.mult, op1=ALU.mult), reads=[hsum.res, sq.res, gfin.res], writes=[o_t.res])
        P.dma("sp", o_t.res.name, lambda e: e.dma_start(out=out[t * 128:(t + 1) * 128, :], in_=o_t.ap), reads=[o_t.res])

    front(0)
    for _ in topk_gen(0):
        pass
    frontC(0)
    for t in range(k.nt_e):
        if t + 1 < k.nt_e:
            front(t + 1)
            back(t, topk_gen(t + 1))
            frontC(t + 1)
        else:
            back(t)
    P.barrier()
    A.release(m0)


def _host_consts():
    import ml_dtypes
    bf = ml_dtypes.bfloat16
    f32 = np.float32
    inv = np.power(f32(500000.0), -np.arange(0, 16, 2, dtype=f32) / f32(16)).astype(f32)
    pos = np.arange(S, dtype=f32)
    ang = (pos[:, None] * inv[None, :]).astype(f32)
    cos, sin = np.cos(ang).astype(f32), np.sin(ang).astype(f32)
    cosT = np.concatenate([cos.T, cos.T], 0)
    sinT = np.concatenate([sin.T, sin.T], 0)
    c_rope = np.zeros((128, 4 * S), f32)
    c_rope[:16] = np.concatenate([cosT * f32(0.125), sinT * f32(0.125), cosT, sinT], 1).astype(f32)
    cend = (np.arange(255) * 16 + 31).astype(f32)
    angc = (cend[:, None] * inv[None, :]).astype(f32)
    cc = np.zeros((16, 256), f32)
    sc = np.zeros((16, 256), f32)
    cc[:, :255] = np.concatenate([np.cos(angc).T, np.cos(angc).T], 0)
    sc[:, :255] = np.concatenate([np.sin(angc).T, np.sin(angc).T], 0)
    c_ropec = np.zeros((128, 512), f32)
    c_ropec[:16] = np.concatenate([cc, sc], 1)
    s_ = np.arange(128)[:, None]
    t_ = np.arange(128)[None, :]
    tri = (s_ <= t_).astype(f32)
    sup = (s_ > t_).astype(f32)
    c_trib = np.concatenate([tri, sup], 1).astype(bf)
    c_trif = np.concatenate([tri, np.ones((128, 128), f32)], 1).astype(f32)
    n_ = (np.arange(2)[None, :, None] * 128 + np.arange(128)[:, None, None])
    tt = np.arange(S)[None, None, :]
    cmpmask = ((n_ < 255) & (16 * n_ + 31 <= tt)).astype(f32).reshape(128, 2 * S).astype(bf)
    etab = np.zeros((128, S), f32)
    etab[64:] = (np.arange(S)[None, :] // 64 == np.arange(64)[:, None])
    etab = etab.astype(bf)
    q = np.arange(S)
    qb = q // 64
    j = np.arange(64)[None, :]
    sb = np.zeros((S, 64), np.float64)
    sb += (j == 0) * 1e9 + (j == qb[:, None]) * 2e9 + (j == qb[:, None] - 1) * 4e9
    sb = np.where(j > qb[:, None], -1.0 - j / 64.0, sb)
    selbias = sb.astype(f32).reshape(NT, 128, 64).transpose(1, 0, 2).reshape(128, NT * 64)
    cs = np.arange(255)[:, None] * 16
    ss_ = np.arange(64)[None, :] * 64
    ov = np.clip(np.minimum(cs + 32, ss_ + 64) - np.maximum(cs, ss_), 0, None) / 32.0
    ova = np.zeros((256, 65), f32)
    ova[:255, 0] = 1.0
    ova[:255, 1:] = ov
    ovaug = ova.reshape(2, 128, 65).transpose(1, 0, 2).reshape(128, 130).astype(bf)
    iota16 = np.tile(np.arange(16, dtype=f32)[None, :], (128, 1))
    return dict(c_rope=c_rope, c_ropec=c_ropec, c_trib=c_trib, c_trif=c_trif, c_cmpmask=cmpmask, c_etab=etab,
                c_selbias=np.ascontiguousarray(selbias), c_ovaug=np.ascontiguousarray(ovaug), c_iota16=iota16)


def prep_shared(inp):
    a = lambda v: np.ascontiguousarray(np.asarray(v, dtype=np.float32))
    sh = dict(
        w_in=a(inp["w_in"][0]),
        gmix=a(np.asarray(inp["norm_mix"])[0].reshape(8, 128).T),
        fbias=a(np.asarray(inp["fox_f_bias"])[0].reshape(1, 8)),
        w1=a(np.concatenate([np.asarray(inp["nsa_cmp_w1"])[0].reshape(2, 32, 64, 128).transpose(2, 0, 1, 3).reshape(64, -1), np.zeros((64, 8192), np.float32)], 0)),
        pos=a(np.concatenate([np.asarray(inp["nsa_cmp_pos"])[0].transpose(2, 0, 1).reshape(64, 64), np.zeros((64, 64), np.float32)], 0)),
        w2=a(np.asarray(inp["nsa_cmp_w2"])[0].transpose(1, 0, 2).reshape(128, 128)),
        wbr=a(inp["w_branch"][0]),
        wout=a(inp["w_out"][0]),
        gffn=a(np.asarray(inp["norm_ffn"])[0].reshape(1, D)),
        gffn8=a(np.asarray(inp["norm_ffn"])[0].reshape(8, 128).T),
        wq=a(inp["peer_wq"][0]),
        skT=a(np.asarray(inp["peer_subkeys"])[0].transpose(3, 0, 1, 2).reshape(128, 16 * 128)),
        pu=a(inp["peer_u"][0]),
        pv=a(inp["peer_v"][0]),
        gfin=a(np.asarray(inp["norm_final"]).reshape(1, D)),
    )
    sh.update(_host_consts())
    return sh


_NC_CACHE = {}


def kernel(**inputs):
    x = np.asarray(inputs["x"], dtype=np.float32)
    sh = prep_shared(inputs)
    if "nc" not in _NC_CACHE:
        _NC_CACHE["nc"] = build_program()
    nc = _NC_CACHE["nc"]
    in_maps = [dict(sh, x=np.ascontiguousarray(x[b])) for b in range(8)]
    res = run_bass_kernel_spmd(nc, in_maps, core_ids=list(range(8)))
    return np.stack([np.asarray(r["out"], dtype=np.float32) for r in res.results], 0)
```
